# Optimizing a Trainium2 kernel written in Bass

```python
import math
import jax
import jax.numpy as jnp
from jax import lax
import numpy as np

D_MODEL = 1024
BATCH = 8
SEQ = 8192
DEPTH = 2

GRID_W = 64
HEAD_DIM = 64
N_BRANCH = 4
BRANCH_W = 512
A_HEADS = 4
A_QK_W = A_HEADS * 2 * HEAD_DIM
A_V_W = A_HEADS * 2 * HEAD_DIM
B_PATTERNS = ((128, 1), (512, 4), (2048, 16))
B_HEADS = 8
B_QBLOCK = 64
B_W = B_HEADS * HEAD_DIM
C_HEADS = 8
C_W = C_HEADS * HEAD_DIM
NA_KH = 8
NA_KW = 16
NA_COLBLOCK = 16
NA_COLSPAN = 32
D_HEADS = 8
D_KV_HEADS = 2
D_Q_W = D_HEADS * HEAD_DIM
D_KV_W = D_KV_HEADS * HEAD_DIM
ROPE_THETA = 10000.0
Q_BLOCK = 128
D_FF = 4 * D_MODEL
GATE_W = N_BRANCH * D_MODEL
IN_W = 2 * A_QK_W + A_V_W + 3 * B_W * 3 + 3 * C_W + D_Q_W + 2 * D_KV_W + GATE_W
EPS = 1e-6
NEG_INF = -1e30
F32 = jnp.float32

kernel_name = 'hybrid_gated_multimixer_encoder'


def rms_norm(x, g):
    xf = x.astype(F32)
    y = xf * lax.rsqrt(jnp.mean(xf * xf, axis=-1, keepdims=True) + EPS)
    return (y * g.astype(F32)).astype(x.dtype)


def alibi_slopes(n):
    return jnp.asarray(np.array([2.0 ** (-8.0 * (i + 1) / n) for i in range(n)], dtype=np.float32))


def split_cols(x, sizes):
    points, acc = [], 0
    for s in sizes[:-1]:
        acc += s
        points.append(acc)
    return jnp.split(x, points, axis=-1)


def diff_attention(q, k, v, lam, subln_g, out_scale):
    B_, S, H = q.shape[:3]
    nb = S // Q_BLOCK
    slopes = alibi_slopes(H)
    pos = jnp.arange(S)
    scale = HEAD_DIM ** -0.5
    qb = q.reshape(B_, nb, Q_BLOCK, H, 2, HEAD_DIM).transpose(1, 0, 2, 3, 4, 5)

    def block(args):
        i, qi = args
        tq = i * Q_BLOCK + jnp.arange(Q_BLOCK)
        dist = jnp.abs(tq[:, None] - pos[None, :]).astype(F32)
        bias = -slopes[:, None, None] * dist[None]
        s = jnp.einsum('bqhcd,bshcd->bhcqs', qi, k, preferred_element_type=F32) * scale
        p = jax.nn.softmax(s + bias[None, :, None], axis=-1)
        a = p[:, :, 0] - lam * p[:, :, 1]
        return jnp.einsum('bhqs,bshe->bqhe', a.astype(v.dtype), v)

    o = lax.map(block, (jnp.arange(nb), qb))
    o = o.transpose(1, 0, 2, 3, 4).reshape(B_, S, H, 2 * HEAD_DIM)
    o = rms_norm(o, subln_g) * out_scale
    return o.reshape(B_, S, H * 2 * HEAD_DIM)


def dilated_pattern(q, k, v, dil, radius, slopes):
    B_, S, H, Dh = q.shape
    L = S // dil
    nb = -(-L // B_QBLOCK)
    Lp = nb * B_QBLOCK
    pad = Lp - L

    def to_sub(x):
        return x.reshape(B_, L, dil, H, Dh).transpose(0, 2, 1, 3, 4)

    qs = jnp.pad(to_sub(q), ((0, 0), (0, 0), (0, pad), (0, 0), (0, 0)))
    ks = jnp.pad(to_sub(k), ((0, 0), (0, 0), (radius, pad + radius), (0, 0), (0, 0)))
    vs = jnp.pad(to_sub(v), ((0, 0), (0, 0), (radius, pad + radius), (0, 0), (0, 0)))
    span = B_QBLOCK + 2 * radius
    kidx = jnp.arange(nb)[:, None] * B_QBLOCK + jnp.arange(span)[None, :]
    kb = ks[:, :, kidx]
    vb = vs[:, :, kidx]
    qb = qs.reshape(B_, dil, nb, B_QBLOCK, H, Dh)
    qpos = jnp.arange(Lp).reshape(nb, B_QBLOCK)
    kpos = kidx - radius
    rel = kpos[:, None, :] - qpos[:, :, None]
    valid = (jnp.abs(rel) <= radius) & (kpos[:, None, :] >= 0) & (kpos[:, None, :] < L)
    dist = (jnp.abs(rel) * dil).astype(F32)
    bias = -slopes[None, :, None, None] * dist[:, None]
    s = jnp.einsum('bjnqhd,bjnkhd->bjnhqk', qb, kb, preferred_element_type=F32) * (Dh ** -0.5) + bias
    s = jnp.where(valid[:, None], s, NEG_INF)
    m = jnp.max(s, axis=-1)
    p = jnp.exp(s - m[..., None])
    den = jnp.sum(p, axis=-1)
    o = jnp.einsum('bjnhqk,bjnkhd->bjnqhd', p.astype(v.dtype), vb, preferred_element_type=F32)
    o = o / den.transpose(0, 1, 2, 4, 3)[..., None]
    o = o.reshape(B_, dil, Lp, H, Dh)[:, :, :L].transpose(0, 2, 1, 3, 4).reshape(B_, S, H, Dh)

    def back(t):
        t = t.transpose(0, 1, 2, 4, 3).reshape(B_, dil, Lp, H)[:, :, :L]
        return t.transpose(0, 2, 1, 3).reshape(B_, S, H)

    return o, back(m), back(den)


def dilated_mixture(b_qkv):
    B_, S = b_qkv.shape[:2]
    slopes = alibi_slopes(B_HEADS)
    outs, ms, dens = [], [], []
    for g, (window, dil) in enumerate(B_PATTERNS):
        o, m, den = dilated_pattern(b_qkv[:, :, g, 0], b_qkv[:, :, g, 1], b_qkv[:, :, g, 2],
                                    dil, window // (2 * dil), slopes)
        outs.append(o)
        ms.append(m)
        dens.append(den)
    m_star = jnp.max(jnp.stack(ms), axis=0)
    ws = [d * jnp.exp(m - m_star) for d, m in zip(dens, ms)]
    num = ws[0][..., None] * outs[0]
    tot = ws[0]
    for w, o in zip(ws[1:], outs[1:]):
        num = num + w[..., None] * o
        tot = tot + w
    out = num / tot[..., None]
    return out.astype(b_qkv.dtype).reshape(B_, S, B_W)


def neighbourhood_attention(q, k, v, rpb):
    B_, S, H, Dh = q.shape
    rows = S // GRID_W
    kh = min(NA_KH, rows)
    ncb = GRID_W // NA_COLBLOCK
    K = kh * NA_COLSPAN
    qg = q.reshape(B_, rows, GRID_W, H, Dh)
    kg = k.reshape(B_, rows, GRID_W, H, Dh)
    vg = v.reshape(B_, rows, GRID_W, H, Dh)
    qcol = jnp.arange(GRID_W).reshape(ncb, NA_COLBLOCK)
    qstart = jnp.clip(qcol - NA_KW // 2, 0, GRID_W - NA_KW)
    kstart = jnp.clip(jnp.arange(ncb) * NA_COLBLOCK - NA_KW // 2, 0, GRID_W - NA_COLSPAN)
    kcol = kstart[:, None] + jnp.arange(NA_COLSPAN)[None, :]
    col_ok = (kcol[:, None, :] >= qstart[:, :, None]) & (kcol[:, None, :] < qstart[:, :, None] + NA_KW)
    mask = jnp.broadcast_to(col_ok[:, :, None, :], (ncb, NA_COLBLOCK, kh, NA_COLSPAN)).reshape(ncb, NA_COLBLOCK, K)
    col_idx = jnp.clip(kcol[:, None, :] - qcol[:, :, None] + NA_KW - 1, 0, 2 * NA_KW - 2)

    def row_fn(r):
        rs = jnp.clip(r - kh // 2, 0, rows - kh)
        kr = lax.dynamic_slice_in_dim(kg, rs, kh, axis=1)[:, :, kcol]
        vr = lax.dynamic_slice_in_dim(vg, rs, kh, axis=1)[:, :, kcol]
        kr = kr.transpose(0, 2, 1, 3, 4, 5).reshape(B_, ncb, K, H, Dh)
        vr = vr.transpose(0, 2, 1, 3, 4, 5).reshape(B_, ncb, K, H, Dh)
        qr = lax.dynamic_index_in_dim(qg, r, axis=1, keepdims=False).reshape(B_, ncb, NA_COLBLOCK, H, Dh)
        row_idx = rs + jnp.arange(kh) - r + NA_KH - 1
        bias = rpb[:, row_idx[None, None, :, None], col_idx[:, :, None, :]]
        bias = bias.reshape(H, ncb, NA_COLBLOCK, K).astype(F32)
        s = jnp.einsum('bcqhd,bckhd->bhcqk', qr, kr, preferred_element_type=F32) * (Dh ** -0.5) + bias[None]
        s = jnp.where(mask[None, None], s, NEG_INF)
        p = jax.nn.softmax(s, axis=-1)
        o = jnp.einsum('bhcqk,bckhd->bcqhd', p.astype(vr.dtype), vr)
        return o.reshape(B_, GRID_W, H, Dh)

    o = lax.map(row_fn, jnp.arange(rows))
    return o.transpose(1, 0, 2, 3, 4).reshape(B_, S, H * Dh)


def _rotate(x, pos):
    n = x.shape[-1] // 2
    inv = ROPE_THETA ** (-jnp.arange(n, dtype=F32) / n)
    ang = pos.astype(F32)[:, None] * inv[None, :]
    cos = jnp.cos(ang)[None, :, None, :]
    sin = jnp.sin(ang)[None, :, None, :]
    x1, x2 = x[..., :n], x[..., n:]
    return jnp.concatenate([x1 * cos - x2 * sin, x2 * cos + x1 * sin], axis=-1)


def axial_rope(x, row_pos, col_pos):
    xf = x.astype(F32)
    half = x.shape[-1] // 2
    out = jnp.concatenate([_rotate(xf[..., :half], row_pos), _rotate(xf[..., half:], col_pos)], axis=-1)
    return out.astype(x.dtype)


def gqa_attention(q, k, v):
    B_, S, Hq, Dh = q.shape
    Hkv = k.shape[2]
    rep = Hq // Hkv
    nb = S // Q_BLOCK
    qb = q.reshape(B_, nb, Q_BLOCK, Hkv, rep, Dh).transpose(1, 0, 2, 3, 4, 5)

    def block(qi):
        s = jnp.einsum('bqgrd,bsgd->bgrqs', qi, k, preferred_element_type=F32) * (Dh ** -0.5)
        p = jax.nn.softmax(s, axis=-1)
        return jnp.einsum('bgrqs,bsgd->bqgrd', p.astype(v.dtype), v)

    o = lax.map(block, qb)
    return o.transpose(1, 0, 2, 3, 4, 5).reshape(B_, S, Hq * Dh)


def setup_inputs(seed: int = 0) -> dict:
    key = jax.random.key(seed)
    ks = jax.random.split(key, 14)

    def nrm(k, shape, scale):
        return scale * jax.random.normal(k, shape, F32)

    return {
        'x': nrm(ks[0], (BATCH, SEQ, D_MODEL), 1.0),
        'norm_mix': 1.0 + nrm(ks[1], (DEPTH, D_MODEL), 0.05),
        'w_in': nrm(ks[2], (DEPTH, D_MODEL, IN_W), D_MODEL ** -0.5),
        'b_gate': nrm(ks[3], (DEPTH, GATE_W), 0.1),
        'diff_lambda': nrm(ks[4], (DEPTH, 4, HEAD_DIM), 0.1),
        'diff_subln': 1.0 + nrm(ks[5], (DEPTH, 2 * HEAD_DIM), 0.05),
        'na_rpb': nrm(ks[6], (DEPTH, C_HEADS, 2 * NA_KH - 1, 2 * NA_KW - 1), 0.1),
        'qk_norm': 1.0 + nrm(ks[7], (DEPTH, 2, HEAD_DIM), 0.05),
        'w_branch': nrm(ks[8], (DEPTH, N_BRANCH, BRANCH_W, D_MODEL), BRANCH_W ** -0.5),
        'w_out': nrm(ks[9], (DEPTH, D_MODEL, D_MODEL), D_MODEL ** -0.5),
        'norm_ffn': 1.0 + nrm(ks[10], (DEPTH, D_MODEL), 0.05),
        'w_ff1': nrm(ks[11], (DEPTH, D_MODEL, D_FF), D_MODEL ** -0.5),
        'w_ff2': nrm(ks[12], (DEPTH, D_FF, D_MODEL), D_FF ** -0.5),
        'norm_final': 1.0 + nrm(ks[13], (D_MODEL,), 0.05),
    }


def reference(x, norm_mix, w_in, b_gate, diff_lambda, diff_subln, na_rpb, qk_norm, w_branch, w_out,
              norm_ffn, w_ff1, w_ff2, norm_final):
    B_, S, _ = x.shape
    pos = jnp.arange(S)
    row_pos = pos // GRID_W
    col_pos = pos % GRID_W
    sizes = [A_QK_W, A_QK_W, A_V_W, len(B_PATTERNS) * 3 * B_W, C_W, C_W, C_W, D_Q_W, D_KV_W, D_KV_W, GATE_W]
    for l in range(DEPTH):
        h = rms_norm(x, norm_mix[l])
        proj = jnp.einsum('bsd,de->bse', h, w_in[l])
        a_q, a_k, a_v, b_qkv, c_q, c_k, c_v, d_q, d_k, d_v, gate = split_cols(proj, sizes)

        lamp = diff_lambda[l].astype(F32)
        lam_init = 0.8 - 0.6 * math.exp(-0.3 * l)
        lam = jnp.exp(jnp.sum(lamp[0] * lamp[1])) - jnp.exp(jnp.sum(lamp[2] * lamp[3])) + lam_init
        y_a = diff_attention(a_q.reshape(B_, S, A_HEADS, 2, HEAD_DIM), a_k.reshape(B_, S, A_HEADS, 2, HEAD_DIM),
                             a_v.reshape(B_, S, A_HEADS, 2 * HEAD_DIM), lam, diff_subln[l], 1.0 - lam_init)

        y_b = dilated_mixture(b_qkv.reshape(B_, S, len(B_PATTERNS), 3, B_HEADS, HEAD_DIM))

        y_c = neighbourhood_attention(c_q.reshape(B_, S, C_HEADS, HEAD_DIM), c_k.reshape(B_, S, C_HEADS, HEAD_DIM),
                                      c_v.reshape(B_, S, C_HEADS, HEAD_DIM), na_rpb[l])

        qd = axial_rope(rms_norm(d_q.reshape(B_, S, D_HEADS, HEAD_DIM), qk_norm[l, 0]), row_pos, col_pos)
        kd = axial_rope(rms_norm(d_k.reshape(B_, S, D_KV_HEADS, HEAD_DIM), qk_norm[l, 1]), row_pos, col_pos)
        y_d = gqa_attention(qd, kd, d_v.reshape(B_, S, D_KV_HEADS, HEAD_DIM))

        g = jax.nn.sigmoid((gate + b_gate[l]).astype(F32)).astype(x.dtype).reshape(B_, S, N_BRANCH, D_MODEL)
        branches = (y_a, y_b, y_c, y_d)
        merged = g[:, :, 0] * jnp.einsum('bsk,kd->bsd', branches[0], w_branch[l, 0])
        for n in range(1, N_BRANCH):
            merged = merged + g[:, :, n] * jnp.einsum('bsk,kd->bsd', branches[n], w_branch[l, n])
        x = x + jnp.einsum('bsd,de->bse', merged, w_out[l])

        u = jax.nn.relu(jnp.einsum('bsd,df->bsf', rms_norm(x, norm_ffn[l]), w_ff1[l]))
        x = x + jnp.einsum('bsf,fd->bsd', u * u, w_ff2[l])
    return rms_norm(x, norm_final)
```

```python
import math
import numpy as np
import ml_dtypes
import concourse.bass as bass
import concourse.mybir as mybir
from concourse.bass_utils import run_bass_kernel_spmd

F32 = mybir.dt.float32
BF16 = mybir.dt.bfloat16
AF = mybir.ActivationFunctionType
ALU = mybir.AluOpType
AX = mybir.AxisListType

D_MODEL = 1024
DEPTH = 2
GRID_W = 64
IN_W = 12544
NQK = 8448
EPS = 1e-6
NEG = -1e30
B_PATTERNS = ((128, 1), (512, 4), (2048, 16))
OFF_AQ, OFF_AK, OFF_AV, OFF_B, OFF_CQ, OFF_CK, OFF_CV, OFF_DQ, OFF_DK, OFF_DV, OFF_G = (
    0, 512, 1024, 1536, 6144, 6656, 7168, 7680, 8192, 8320, 8448)
VC_A, VC_B, VC_C, VC_D, VC_N = 0, 512, 2048, 2560, 2688

ENGS = ("pe", "act", "dve", "pool", "sp")
SEM_ROLL = 8000
DMA_SLOTS = 8
SB_BASE = 16512
SB_LIMIT = 228864


class Op:
    __slots__ = ("eng", "fn", "deps", "is_dma", "sig", "need_sig", "slot")

    def __init__(self, eng, fn, is_dma):
        self.eng = eng
        self.fn = fn
        self.is_dma = is_dma
        self.deps = []
        self.sig = None
        self.need_sig = False
        self.slot = None


class Sched:
    def __init__(self, nc):
        self.nc = nc
        self.ops = {e: [] for e in ENGS}
        self.last_w = {}
        self.readers = {}
        self.pending = {e: [] for e in ENGS}

    def _dep(self, op, d):
        if d is op:
            return
        if (not d.is_dma) and (not op.is_dma) and d.eng == op.eng and op.eng == "pe":
            return
        d.need_sig = True
        op.deps.append(d)

    def add(self, eng, fn, reads=(), writes=(), is_dma=False):
        op = Op(eng, fn, is_dma)
        deps = {}
        for r in reads:
            w = self.last_w.get(r)
            if w is not None:
                deps[id(w)] = w
        for r in writes:
            w = self.last_w.get(r)
            if w is not None:
                deps[id(w)] = w
            for rd in self.readers.get(r, ()):
                deps[id(rd)] = rd
        for d in self.pending[eng]:
            deps[id(d)] = d
        self.pending[eng] = []
        for d in deps.values():
            self._dep(op, d)
        for r in writes:
            self.last_w[r] = op
            self.readers[r] = []
        for r in reads:
            if r not in writes:
                self.readers.setdefault(r, []).append(op)
        self.ops[eng].append(op)
        return op

    def barrier(self):
        deps = []
        for e in ENGS:
            ops = self.ops[e]
            for op in reversed(ops):
                if not op.is_dma:
                    deps.append(op)
                    break
            k = 0
            for op in reversed(ops):
                if op.is_dma:
                    deps.append(op)
                    k += 1
                    if k >= DMA_SLOTS:
                        break
        for e in ENGS:
            self.pending[e] = list(deps)
        self.last_w.clear()
        self.readers.clear()

    def emit(self, final_waits=()):
        nc = self.nc
        sem_ctx = []

        def new_sem(name):
            cm = nc.semaphore(name)
            s = cm.__enter__()
            sem_ctx.append(cm)
            return s

        cnt = 0
        for e in ENGS:
            cur = None
            val = 0
            slots = [None] * DMA_SLOTS
            slotv = [0] * DMA_SLOTS
            nd = 0
            for op in self.ops[e]:
                if op.is_dma:
                    s = nd % DMA_SLOTS
                    nd += 1
                    if slots[s] is None or slotv[s] + 16 > SEM_ROLL:
                        slots[s] = new_sem(f"d{e}{s}_{cnt}")
                        cnt += 1
                        slotv[s] = 0
                        prev = None
                    else:
                        prev = (slots[s], slotv[s])
                    slotv[s] += 16
                    op.sig = (slots[s], slotv[s])
                    op.slot = prev
                elif op.need_sig:
                    if cur is None or val + 1 > SEM_ROLL:
                        cur = new_sem(f"c{e}_{cnt}")
                        cnt += 1
                        val = 0
                    val += 1
                    op.sig = (cur, val)
        self.n_sems = cnt
        engmap = {"pe": "tensor", "act": "scalar", "dve": "vector", "pool": "gpsimd", "sp": "sync"}
        with nc.Block() as block:
            for e in ENGS:
                ops = self.ops[e]

                def body(eng, ops=ops, e=e):
                    waited = {}

                    def wait(sem, v):
                        k = id(sem)
                        if waited.get(k, 0) >= v:
                            return
                        waited[k] = v
                        eng.wait_ge(sem, v)

                    for op in ops:
                        if op.is_dma and op.slot is not None:
                            wait(*op.slot)
                        for d in op.deps:
                            wait(*d.sig)
                        ins = op.fn(eng)
                        if op.sig is not None:
                            ins.then_inc(op.sig[0], 16 if op.is_dma else 1)
                    if e == "sp":
                        for fw in final_waits:
                            wait(*fw.sig)

                getattr(block, engmap[e])(body)
        for cm in reversed(sem_ctx):
            cm.__exit__(None, None, None)


def alibi_slopes(n):
    return [2.0 ** (-8.0 * (i + 1) / n) for i in range(n)]


def c_mask_meta(S):
    rows = S // GRID_W
    M = rows // 2
    reps = {"int": 2, "m0": 0, "m1": 1, "mL2": M - 2, "mL1": M - 1}
    meta = []
    tiles = []
    kc = np.arange(64)
    c = np.arange(64)
    qstart = np.clip(c - 8, 0, GRID_W - 16)
    colok = (kc[:, None] >= qstart[None, :]) & (kc[:, None] < qstart[None, :] + 16)
    for cls, m in reps.items():
        for dr in range(-4, 5):
            mk = m + dr
            if mk < 0 or mk >= M:
                continue
            t = np.full((128, 128), NEG, np.float32)
            anyv = False
            for a in range(2):
                for b in range(2):
                    r = 2 * m + b
                    kr = 2 * mk + a
                    rs = min(max(r - 4, 0), rows - 8)
                    if rs <= kr < rs + 8:
                        t[a * 64:(a + 1) * 64, b * 64:(b + 1) * 64] = np.where(colok, 0.0, NEG)
                        anyv = True
            if anyv:
                meta.append((cls, dr))
                tiles.append(t)
    return meta, np.stack(tiles)


def make_consts(S):
    bf = ml_dtypes.bfloat16
    c = {}
    c["c_ident"] = np.eye(128, dtype=np.float32).astype(bf)
    c["c_identf"] = np.eye(128, dtype=np.float32)
    p = np.arange(128)
    i32 = (p % 64) % 32
    partner = np.where(i32 < 16, p + 16, p - 16)
    pm = np.zeros((128, 128), np.float32)
    pm[partner, p] = 1.0
    c["c_pm"] = pm
    c["c_bones"] = (p[:, None] // 64 == p[None, :] // 64).astype(np.float32)
    c["c_ones_f"] = np.ones((128, 128), np.float32)
    c["c_j64"] = np.ascontiguousarray(np.eye(64, dtype=np.float32)[::-1])
    c["c_ones_b"] = np.ones((128, 128), np.float32).astype(bf)
    t = np.arange(S)
    d = p % 64
    half = d // 32
    idx = (d % 32) % 16
    first = (d % 32) < 16
    inv = (10000.0 ** (-np.arange(16, dtype=np.float32) / 16)).astype(np.float32)
    pos = np.where(half[:, None] == 0, (t // GRID_W)[None, :], (t % GRID_W)[None, :]).astype(np.float32)
    ang = pos * inv[idx][:, None]
    rope = np.zeros((2, 128, S), np.float32)
    rope[0] = np.cos(ang)
    rope[1] = np.where(first[:, None], -np.sin(ang), np.sin(ang))
    c["c_rope"] = rope
    sl = alibi_slopes(4)
    kaug = np.zeros((4, 5, S), np.float32)
    qaug = np.zeros((4, 2, 5, S), np.float32)
    for h in range(4):
        s8 = 8.0 * sl[h]
        kaug[h, 0:3] = 1.0
        kaug[h, 3] = s8 * (128 * (t // 128))
        kaug[h, 4] = s8 * (t % 128)
        qaug[h, 0, 0] = -s8 * (512 * (t // 512))
        qaug[h, 0, 1] = -s8 * (256 * ((t % 512) // 256))
        qaug[h, 0, 2] = -s8 * (t % 256)
        qaug[h, 0, 3:5] = 1.0
        qaug[h, 1] = -qaug[h, 0]
    c["c_kaug"] = kaug.astype(bf)
    c["c_qaug"] = qaug.astype(bf)
    assert np.array_equal(c["c_kaug"].astype(np.float32), kaug) and np.array_equal(c["c_qaug"].astype(np.float32), qaug)
    k = np.arange(128)[:, None]
    q = np.arange(128)[None, :]
    q5 = np.arange(512)[None, :]
    cd = np.zeros((128, 4, 4, 512), np.float32)
    for h in range(4):
        for rel in range(4):
            dd = q5 - (k + 128 * rel)
            cd[:, h, rel, :] = 2.0 * sl[h] * np.minimum(dd, 0)
    c["c_cd"] = cd
    slb = alibi_slopes(8)
    mb = np.zeros((24, 128, 3, 128), np.float32)
    for g, (win, dil) in enumerate(B_PATTERNS):
        for h in range(8):
            for r in range(3):
                rel = (128 * (r - 1) + k) - q
                mb[g * 8 + h, :, r, :] = np.where(np.abs(rel) <= 64, -slb[h] * np.abs(rel) * dil, NEG)
    c["c_mb"] = mb.reshape(24, 128, 384)
    meta, mcv = c_mask_meta(S)
    c["c_mcv"] = np.ascontiguousarray(mcv.transpose(1, 0, 2))
    return c, meta


class Builder:
    def __init__(self, S, nl=DEPTH, dbg=None):
        self.Sq = S
        self.nl = nl
        self.dbg = dbg or {}
        self.nc = bass.Bass("TRN2", target_bir_lowering=False)
        self.S = Sched(self.nc)
        self.off = SB_BASE
        self.ncnt = 0
        self.consts, self.cmeta = make_consts(S)
        self.outs = []

    def T(self, name, shape, dt):
        sz = 4 if dt == F32 else 2
        n = 1
        for s in shape[1:]:
            n *= s
        nbytes = (n * sz + 63) // 64 * 64
        self.ncnt += 1
        t = self.nc.alloc_sbuf_tensor_at(f"{name}_{self.ncnt}", list(shape), dt, offset=self.off)
        self.off += nbytes
        assert self.off <= SB_LIMIT, (name, self.off)
        return t

    def dram(self, name, shape, dt, kind="Internal"):
        return self.nc.dram_tensor(name, list(shape), dt, kind=kind).ap()

    def mm(self, out, lhsT, rhs, start, stop, rd, wr):
        return self.S.add("pe", lambda e: e.matmul(out, lhsT=lhsT, rhs=rhs, start=start, stop=stop, skip_group_check=True), rd, wr)

    def tr(self, out, in_, ident, rd, wr):
        return self.S.add("pe", lambda e: e.transpose(out, in_, ident), rd, wr)

    def act(self, out, in_, func, rd, wr, bias=None, scale=1.0, accum=None):
        def f(e):
            kw = {}
            if bias is not None:
                kw["bias"] = bias
            if accum is not None:
                kw["accum_out"] = accum
            return e.activation(out=out, in_=in_, func=func, scale=scale, **kw)
        return self.S.add("act", f, rd, wr)

    def tt(self, eng, out, in0, in1, op, rd, wr):
        return self.S.add(eng, lambda e: e.tensor_tensor(out=out, in0=in0, in1=in1, op=op), rd, wr)

    def ts(self, eng, out, in0, s1, op0, rd, wr, s2=None, op1=None):
        if op1 is None:
            return self.S.add(eng, lambda e: e.tensor_scalar(out=out, in0=in0, scalar1=s1, scalar2=None, op0=op0), rd, wr)
        return self.S.add(eng, lambda e: e.tensor_scalar(out=out, in0=in0, scalar1=s1, scalar2=s2, op0=op0, op1=op1), rd, wr)

    def stt(self, eng, out, in0, scalar, in1, op0, op1, rd, wr):
        return self.S.add(eng, lambda e: e.scalar_tensor_tensor(out=out, in0=in0, scalar=scalar, in1=in1, op0=op0, op1=op1), rd, wr)

    def cp(self, eng, out, in_, rd, wr):
        if eng == "act":
            return self.S.add("act", lambda e: e.copy(out=out, in_=in_), rd, wr)
        return self.S.add(eng, lambda e: e.tensor_copy(out=out, in_=in_), rd, wr)

    def recip(self, out, in_, rd, wr):
        return self.S.add("dve", lambda e: e.reciprocal(out=out, in_=in_), rd, wr)

    def memset(self, eng, ap, v, wr):
        return self.S.add(eng, lambda e: e.memset(ap, v), [], wr)

    def dma(self, q, out, in_, rd, wr, slow=False):
        if slow:
            return self.S.add(q, lambda e: e.dma_start(out=out, in_=in_, allow_slow_non_contiguous=True), rd, wr, is_dma=True)
        return self.S.add(q, lambda e: e.dma_start(out=out, in_=in_), rd, wr, is_dma=True)

    def build(self, stages="WPABCDM"):
        nc = self.nc
        S = self.Sq
        nl = self.nl
        dbg = self.dbg
        di = lambda n, sh, dt=F32: nc.dram_tensor(n, list(sh), dt, kind="ExternalInput").ap()
        self.x_in = di("x", [S, D_MODEL])
        self.norm_mix = di("norm_mix", [DEPTH, D_MODEL])
        self.w_in = di("w_in", [DEPTH, D_MODEL, IN_W])
        self.b_gate = di("b_gate", [DEPTH, 4096])
        self.diff_lambda = di("diff_lambda", [DEPTH, 256])
        self.diff_subln = di("diff_subln", [DEPTH, 128])
        self.na_rpb = di("na_rpb", [DEPTH, 120, 31])
        self.qk_norm = di("qk_norm", [DEPTH, 2, 64])
        self.w_branch = di("w_branch", [DEPTH, 2048, D_MODEL])
        self.w_out = di("w_out", [DEPTH, D_MODEL, D_MODEL])
        self.norm_ffn = di("norm_ffn", [DEPTH, D_MODEL])
        self.w_ff1 = di("w_ff1", [DEPTH, D_MODEL, 4096])
        self.w_ff2 = di("w_ff2", [DEPTH, 4096, D_MODEL])
        self.norm_final = di("norm_final", [1, D_MODEL])
        self.cd = {}
        for k, v in self.consts.items():
            self.cd[k] = di(k, v.shape, BF16 if v.dtype == ml_dtypes.bfloat16 else F32)
        okind = "ExternalOutput"
        self.out = nc.dram_tensor("out", [S, D_MODEL], F32, kind=okind).ap()
        dk = lambda n: okind if dbg.get(n) else "Internal"
        self.wb_in = [self.dram(f"wb_in{l}", [D_MODEL, IN_W], BF16) for l in range(nl)]
        self.wb_br = [self.dram(f"wb_br{l}", [2048, D_MODEL], BF16) for l in range(nl)]
        self.wb_out = [self.dram(f"wb_out{l}", [D_MODEL, D_MODEL], BF16) for l in range(nl)]
        self.wb_f1 = [self.dram(f"wb_f1{l}", [D_MODEL, 4096], BF16) for l in range(nl)]
        self.wb_f2 = [self.dram(f"wb_f2{l}", [4096, D_MODEL], BF16) for l in range(nl)]
        self.qkT = self.dram("qkT", [NQK, S], BF16, dk("qkT"))
        self.vtm = self.dram("vtm", [S, VC_N], BF16, dk("vtm"))
        self.gT = self.dram("gT", [4096, S], BF16, dk("gT"))
        if dbg.get("yT_in"):
            self.yT = di("yT", [4, 512, S], BF16)
        else:
            self.yT = self.dram("yT", [4, 512, S], BF16, dk("yT"))
        self.xres = self.dram("xres", [S, D_MODEL], F32, dk("xres"))
        self.rpbr = self.dram("rpbr", [120, 128], F32)
        for n in ("qkT", "vtm", "gT", "yT", "xres"):
            if dbg.get(n):
                self.outs.append(n)

        self.ident = self.T("ident", [128, 128], BF16)
        self.pm = self.T("pm", [128, 128], F32)
        self.identf = self.T("identf", [128, 128], F32)
        self.bones = self.T("bones", [128, 128], F32)
        self.ones_f = self.T("ones_f", [128, 128], F32)
        self.ones_b = self.T("ones_b", [128, 128], BF16)
        self.small = self.T("small", [128, 64], F32)
        for t, n in ((self.ident, "c_ident"), (self.pm, "c_pm"), (self.identf, "c_identf"), (self.bones, "c_bones"),
                     (self.ones_f, "c_ones_f"), (self.ones_b, "c_ones_b")):
            self.dma("sp", t[:], self.cd[n][:, :], [], ["const"])
        self.pb = [nc.alloc_psum_tensor(f"pb{i}", [128, 512], F32) for i in range(7)]
        self.pbt = nc.alloc_psum_tensor("pbt", [128, 1024], BF16)
        self.base_off = self.off
        self.S.barrier()

        if "W" in stages:
            self.phase_w()
        final = []
        for l in range(nl):
            self.layer_smalls(l)
            xsrc = self.x_in if l == 0 else self.xres
            if "P" in stages:
                self.phase_p(l, xsrc)
            if "D" in stages:
                self.mixer_d(l)
            if "A" in stages:
                self.mixer_a(l)
            if "C" in stages:
                self.mixer_c(l)
            if "B" in stages:
                self.mixer_b(l)
            if "M" in stages:
                final = self.phase_m(l, xsrc, last=(l == nl - 1))
        self.S.barrier()
        self.S.emit(final_waits=final)
        return nc

    def phase_reset(self):
        self.S.barrier()
        self.off = self.base_off

    def phase_w(self):
        self.phase_reset()
        nl = self.nl
        gcol = self.T("gcol", [128, 32], F32)
        for l in range(nl):
            self.dma("sp", gcol[:, l * 16:l * 16 + 8], self.norm_mix[l, :].rearrange("(c p) -> p c", p=128), [], ["gcol"], slow=True)
            self.dma("sp", gcol[:, l * 16 + 8:l * 16 + 16], self.norm_ffn[l, :].rearrange("(c p) -> p c", p=128), [], ["gcol"], slow=True)
        bi = [self.T(f"wci{i}", [128, 4096], F32) for i in range(2)]
        bo = [self.T(f"wco{i}", [128, 4096], BF16) for i in range(2)]
        n = 0
        for l in range(nl):
            jobs = [(self.w_in[l], self.wb_in[l], D_MODEL, IN_W, l * 16),
                    (self.w_branch[l], self.wb_br[l], 2048, D_MODEL, None),
                    (self.w_out[l], self.wb_out[l], D_MODEL, D_MODEL, None),
                    (self.w_ff1[l], self.wb_f1[l], D_MODEL, 4096, l * 16 + 8),
                    (self.w_ff2[l], self.wb_f2[l], 4096, D_MODEL, None)]
            for src, dst, R, C, gc in jobs:
                for rc in range(R // 128):
                    for c0 in range(0, C, 4096):
                        cw = min(4096, C - c0)
                        i = n % 2
                        n += 1
                        self.dma("sp", bi[i][:, 0:cw], src[rc * 128:(rc + 1) * 128, c0:c0 + cw], [], [f"wci{i}"])
                        eng = "dve" if n % 2 else "pool"
                        if gc is not None:
                            self.ts(eng, bo[i][:, 0:cw], bi[i][:, 0:cw], gcol[:, gc + rc:gc + rc + 1], ALU.mult,
                                    [f"wci{i}", "gcol"], [f"wco{i}"])
                        else:
                            self.cp(eng, bo[i][:, 0:cw], bi[i][:, 0:cw], [f"wci{i}"], [f"wco{i}"])
                        self.dma("pool", dst[rc * 128:(rc + 1) * 128, c0:c0 + cw], bo[i][:, 0:cw], [f"wco{i}"], [])

    def layer_smalls(self, l):
        self.phase_reset()
        sm = self.small
        lam_init = 0.8 - 0.6 * math.exp(-0.3 * l)
        lt = self.T("lamt", [128, 256], F32)
        pr = self.T("lampr", [128, 128], F32)
        sv = self.T("lamsv", [128, 4], F32)
        self.dma("sp", lt[:], self.diff_lambda[l:l + 1, :].partition_broadcast(128), [], ["lamt"])
        self.tt("dve", pr[:, 0:64], lt[:, 0:64], lt[:, 64:128], ALU.mult, ["lamt"], ["lampr"])
        self.tt("dve", pr[:, 64:128], lt[:, 128:192], lt[:, 192:256], ALU.mult, ["lamt"], ["lampr"])
        self.S.add("dve", lambda e: e.reduce_sum(out=sv[:, 0:2], in_=pr[:].rearrange("p (a b) -> p a b", a=2), axis=AX.X),
                   ["lampr"], ["lamsv"])
        self.act(sv[:, 2:4], sv[:, 0:2], AF.Exp, ["lamsv"], ["lamsv"])
        self.tt("dve", sm[:, 0:1], sv[:, 3:4], sv[:, 2:3], ALU.subtract, ["lamsv"], ["small"])
        self.ts("dve", sm[:, 0:1], sm[:, 0:1], -lam_init, ALU.add, ["small"], ["small"])
        self.dma("sp", sm[:, 1:2], self.diff_subln[l, :].rearrange("(p o) -> p o", o=1), [], ["small"])
        self.ts("dve", sm[:, 1:2], sm[:, 1:2], 1.0 - lam_init, ALU.mult, ["small"], ["small"])
        for i in range(2):
            for hf in range(2):
                self.dma("sp", sm[hf * 64:(hf + 1) * 64, 2 + i:3 + i], self.qk_norm[l, i, :].rearrange("(p o) -> p o", o=1), [], ["small"])
        self.dma("sp", sm[:, 8:40], self.b_gate[l, :].rearrange("(c p) -> p c", p=128), [], ["small"], slow=True)

    def phase_p(self, l, xsrc):
        self.phase_reset()
        S = self.Sq
        sm = self.small
        NT = S // 512
        xts = [self.T(f"xtP{i}", [128, 4, 1024], F32) for i in range(2)]
        hTs = [self.T(f"hTP{i}", [128, 8, 512], BF16) for i in range(2)]
        wts = [self.T(f"wtP{i}", [128, 8, 512], BF16) for i in range(3)]
        stg = [self.T(f"stgP{i}", [128, 512], BF16) for i in range(4)]
        vst = [self.T(f"vstP{i}", [128, 4, 512], BF16) for i in range(2)]
        rope = [self.T(f"ropeP{i}", [128, 2, 512], F32) for i in range(2)]
        dtmp = [self.T(f"dtmpP{i}", [128, 512], F32) for i in range(5)]
        plan = []
        for c0 in range(0, OFF_G, 512):
            if c0 in (OFF_AV, OFF_B + 1024, OFF_B + 1536 + 1024, OFF_B + 3072 + 1024, OFF_CV):
                vc = {OFF_AV: VC_A, OFF_B + 1024: VC_B, OFF_B + 2560: VC_B + 512, OFF_B + 4096: VC_B + 1024, OFF_CV: VC_C}[c0]
                plan.append((c0, 512, "tm", vc))
            elif c0 == OFF_DQ:
                plan.append((c0, 512, "dq", None))
            elif c0 == OFF_DK:
                plan.append((c0, 256, "dkv", None))
            else:
                plan.append((c0, 512, "fm", None))
        for c0 in range(OFF_G, IN_W, 512):
            plan.append((c0, 512, "gate", None))
        wn = 0
        sn = 0
        pbn = 0
        for tt_ in range(NT):
            t0 = tt_ * 512
            xt = xts[tt_ % 2]
            hT = hTs[tt_ % 2]
            xk, hk = f"xtP{tt_ % 2}", f"hTP{tt_ % 2}"
            rp = rope[tt_ % 2]
            rk = f"ropeP{tt_ % 2}"
            self.dma("sp", xt[:], xsrc[t0:t0 + 512, :].rearrange("(s p) d -> p s d", p=128), [], [xk])
            self.dma("sp", rp[:], self.cd["c_rope"][:, :, t0:t0 + 512].rearrange("a p t -> p a t"), [], [rk])
            mark = self.off
            self.rmsnorm_T_keys(xt, hT, xk, hk)
            self.off = mark
            for (c0, ncol, kind, vc) in plan:
                w = wts[wn % 3]
                wk = f"wtP{wn % 3}"
                wn += 1
                self.dma("sp", w[:, :, 0:ncol], self.wb_in[l][:, c0:c0 + ncol].rearrange("(k p) n -> p k n", p=128), [], [wk])
                if kind in ("fm", "gate", "dq", "dkv"):
                    nfm = {"fm": 4, "gate": 4, "dq": 4, "dkv": 1}[kind]
                    for j in range(nfm):
                        bank = self.pb[pbn % 4]
                        bk = f"pb{pbn % 4}"
                        pbn += 1
                        for kc in range(8):
                            self.mm(bank[:], w[:, kc, j * 128:(j + 1) * 128], hT[:, kc, :], kc == 0, kc == 7, [wk, hk], [bk])
                        st = stg[sn % 4]
                        sk = f"stgP{sn % 4}"
                        sn += 1
                        col = c0 + j * 128
                        if kind == "fm":
                            self.cp("act" if sn % 2 else "dve", st[:], bank[:], [bk], [sk])
                            self.dma("pool", self.qkT[col:col + 128, t0:t0 + 512], st[:], [sk], [])
                        elif kind == "gate":
                            gc = (col - OFF_G) // 128
                            self.act(st[:], bank[:], AF.Sigmoid, [bk, "small"], [sk], bias=sm[:, 8 + gc:9 + gc])
                            self.dma("pool", self.gT[col - OFF_G:col - OFF_G + 128, t0:t0 + 512], st[:], [sk], [])
                        else:
                            gi = 2 if kind == "dq" else 3
                            sq, rst, xg, t1, t2 = dtmp
                            self.act(sq[:], bank[:], AF.Square, [bk], ["dsq"])
                            self.mm(self.pb[4][:], self.bones[:], sq[:], True, True, ["dsq", "const"], ["pb4"])
                            self.ts("dve", rst[:], self.pb[4][:], 1.0 / 64, ALU.mult, ["pb4"], ["drst"], s2=EPS, op1=ALU.add)
                            self.act(rst[:], rst[:], AF.Sqrt, ["drst"], ["drst"])
                            self.recip(rst[:], rst[:], ["drst"], ["drst"])
                            self.ts("dve", xg[:], bank[:], sm[:, gi:gi + 1], ALU.mult, [bk, "small"], ["dxg"])
                            self.mm(self.pb[5][:], self.pm[:], xg[:], True, True, ["dxg", "const"], ["pb5"])
                            self.tt("pool", t1[:], xg[:], rp[:, 0, :], ALU.mult, ["dxg", rk], ["dt1"])
                            self.tt("dve", t2[:], self.pb[5][:], rp[:, 1, :], ALU.mult, ["pb5", rk], ["dt2"])
                            self.tt("pool", t1[:], t1[:], t2[:], ALU.add, ["dt1", "dt2"], ["dt1"])
                            self.tt("dve", st[:], t1[:], rst[:], ALU.mult, ["dt1", "drst"], [sk])
                            self.dma("pool", self.qkT[col:col + 128, t0:t0 + 512], st[:], [sk], [])
                if kind in ("tm", "dkv"):
                    if kind == "tm":
                        wc0, nvc, vcol = 0, 512, vc
                    else:
                        wc0, nvc, vcol = 128, 128, VC_D
                    vs = vst[wn % 2]
                    vk = f"vstP{wn % 2}"
                    for s in range(4):
                        bank = self.pb[pbn % 4]
                        bk = f"pb{pbn % 4}"
                        pbn += 1
                        for kc in range(8):
                            self.mm(bank[:, 0:nvc], hT[:, kc, s * 128:(s + 1) * 128], w[:, kc, wc0:wc0 + nvc], kc == 0, kc == 7, [wk, hk], [bk])
                        self.cp("act" if s % 2 else "dve", vs[:, s, 0:nvc], bank[:, 0:nvc], [bk], [vk])
                    self.dma("pool", self.vtm[t0:t0 + 512, vcol:vcol + nvc].rearrange("(s p) c -> p s c", p=128),
                             vs[:, :, 0:nvc], [vk], [])

    def rmsnorm_T_keys(self, xt, hT, xk, hk, inv_d=1.0 / D_MODEL):
        ss = self.T("ss", [128, 8], F32)
        junk = self.T("junk", [128, 1024], BF16)
        hn = self.T("hn", [128, 4, 1024], BF16)
        self.memset("dve", ss[:], 0.0, ["ss"])
        for s in range(4):
            self.act(junk[:], xt[:, s, :], AF.Square, [xk, "ss"], ["junk", "ss"], accum=ss[:, s:s + 1])
        self.ts("dve", ss[:, 4:8], ss[:, 0:4], inv_d, ALU.mult, ["ss"], ["ss"], s2=EPS, op1=ALU.add)
        self.act(ss[:, 4:8], ss[:, 4:8], AF.Sqrt, ["ss"], ["ss"])
        self.recip(ss[:, 4:8], ss[:, 4:8], ["ss"], ["ss"])
        for s in range(4):
            self.ts("dve" if s % 2 else "pool", hn[:, s, :], xt[:, s, :], ss[:, 4 + s:5 + s], ALU.mult,
                    [xk, "ss"], [f"hn{s}"])
        for c in range(8):
            for s in range(4):
                self.tr(self.pbt[:, s * 128:(s + 1) * 128], hn[:, s, c * 128:(c + 1) * 128], self.ident[:],
                        [f"hn{s}", "const"], ["pbt"])
            self.cp("act" if c % 2 else "dve", hT[:, c, :], self.pbt[:, 0:512], ["pbt"], [hk])

    def phase_m(self, l, xsrc, last):
        self.phase_reset()
        S = self.Sq
        NT = S // 512
        xts = [self.T(f"xtM{i}", [128, 4, 1024], F32) for i in range(2)]
        yts = [self.T(f"ytM{i}", [128, 4, 512], BF16) for i in range(2)]
        gts = [self.T(f"gtM{i}", [128, 8, 512], BF16) for i in range(2)]
        wts = [self.T(f"wtM{i}", [128, 8, 512], BF16) for i in range(3)]
        macc = self.T("macc", [128, 8, 512], F32)
        mtmp = [self.T(f"mtmp{i}", [128, 512], F32) for i in range(2)]
        mT = self.T("mT", [128, 8, 512], BF16)
        h2T = self.T("h2T", [128, 8, 512], BF16)
        uT = self.T("uT", [128, 32, 512], BF16)
        gfin = None
        if last:
            gfin = self.T("gfin", [128, 1024], F32)
            self.dma("sp", gfin[:], self.norm_final[0:1, :].partition_broadcast(128), [], ["gfin"])
        wn = 0
        pbn = 0
        yn = 0
        finals = []
        for tt_ in range(NT):
            t0 = tt_ * 512
            xt = xts[tt_ % 2]
            xk = f"xtM{tt_ % 2}"
            self.dma("sp", xt[:], xsrc[t0:t0 + 512, :].rearrange("(s p) d -> p s d", p=128), [], [xk])
            for n in range(4):
                yt = yts[yn % 2]
                gt = gts[yn % 2]
                yk, gk = f"ytM{yn % 2}", f"gtM{yn % 2}"
                yn += 1
                w = wts[wn % 3]
                wk = f"wtM{wn % 3}"
                wn += 1
                self.dma("sp", yt[:], self.yT[n, :, t0:t0 + 512].rearrange("(k p) t -> p k t", p=128), [], [yk])
                self.dma("sp", gt[:], self.gT[n * 1024:(n + 1) * 1024, t0:t0 + 512].rearrange("(k p) t -> p k t", p=128), [], [gk])
                wv = w[:].rearrange("p k n -> p (k n)").rearrange("p (k n) -> p k n", k=4)
                self.dma("sp", wv, self.wb_br[l][n * 512:(n + 1) * 512, :].rearrange("(k p) n -> p k n", p=128), [], [wk])
                for oc in range(8):
                    bank = self.pb[pbn % 3]
                    bk = f"pb{pbn % 3}"
                    pbn += 1
                    for kc in range(4):
                        self.mm(bank[:], wv[:, kc, oc * 128:(oc + 1) * 128], yt[:, kc, :], kc == 0, kc == 3, [wk, yk], [bk])
                    eng = "dve" if oc % 2 else "pool"
                    if n == 0:
                        self.tt("dve", macc[:, oc, :], bank[:], gt[:, oc, :], ALU.mult, [bk, gk], [f"macc{oc}"])
                    elif n < 3:
                        mt = mtmp[oc % 2]
                        self.tt("dve", mt[:], bank[:], gt[:, oc, :], ALU.mult, [bk, gk], [f"mtmp{oc % 2}"])
                        self.tt("pool", macc[:, oc, :], macc[:, oc, :], mt[:], ALU.add, [f"mtmp{oc % 2}", f"macc{oc}"], [f"macc{oc}"])
                    else:
                        mt = mtmp[oc % 2]
                        self.tt("dve", mt[:], bank[:], gt[:, oc, :], ALU.mult, [bk, gk], [f"mtmp{oc % 2}"])
                        self.tt("pool", mT[:, oc, :], macc[:, oc, :], mt[:], ALU.add, [f"mtmp{oc % 2}", f"macc{oc}"], [f"mT{oc}"])
            mTk = [f"mT{oc}" for oc in range(8)]
            for half in range(2):
                w = wts[wn % 3]
                wk = f"wtM{wn % 3}"
                wn += 1
                self.dma("sp", w[:], self.wb_out[l][:, half * 512:(half + 1) * 512].rearrange("(k p) n -> p k n", p=128), [], [wk])
                for s in range(4):
                    bank = self.pb[3 + pbn % 2]
                    bk = f"pb{3 + pbn % 2}"
                    pbn += 1
                    for kc in range(8):
                        self.mm(bank[:], mT[:, kc, s * 128:(s + 1) * 128], w[:, kc, :], kc == 0, kc == 7, [wk, f"mT{kc}"], [bk])
                    self.tt("dve", xt[:, s, half * 512:(half + 1) * 512], xt[:, s, half * 512:(half + 1) * 512], bank[:], ALU.add,
                            [bk, xk], [xk])
            mark = self.off
            self.rmsnorm_T_keys(xt, h2T, xk, "h2T")
            self.off = mark
            for ft in range(8):
                w = wts[wn % 3]
                wk = f"wtM{wn % 3}"
                wn += 1
                self.dma("sp", w[:], self.wb_f1[l][:, ft * 512:(ft + 1) * 512].rearrange("(k p) n -> p k n", p=128), [], [wk])
                for j in range(4):
                    fc = ft * 4 + j
                    bank = self.pb[pbn % 3]
                    bk = f"pb{pbn % 3}"
                    pbn += 1
                    for kc in range(8):
                        self.mm(bank[:], w[:, kc, j * 128:(j + 1) * 128], h2T[:, kc, :], kc == 0, kc == 7, [wk, "h2T"], [bk])
                    rl = mtmp[fc % 2]
                    self.act(rl[:], bank[:], AF.Relu, [bk], [f"mtmp{fc % 2}"])
                    self.tt("dve" if fc % 2 else "pool", uT[:, fc, :], rl[:], rl[:], ALU.mult, [f"mtmp{fc % 2}"], [f"uT{fc}"])
            for half in range(2):
                banks = [(self.pb[3 + s], f"pb{3 + s}") for s in range(4)]
                for fg in range(4):
                    w = wts[wn % 3]
                    wk = f"wtM{wn % 3}"
                    wn += 1
                    self.dma("sp", w[:], self.wb_f2[l][fg * 1024:(fg + 1) * 1024, half * 512:(half + 1) * 512].rearrange("(k p) n -> p k n", p=128),
                             [], [wk])
                    for s in range(4):
                        bank, bk = banks[s]
                        for kc in range(8):
                            fc = fg * 8 + kc
                            self.mm(bank[:], uT[:, fc, s * 128:(s + 1) * 128], w[:, kc, :], fg == 0 and kc == 0, fg == 3 and kc == 7,
                                    [wk, f"uT{fc}"], [bk])
                for s in range(4):
                    bank, bk = banks[s]
                    self.tt("dve", xt[:, s, half * 512:(half + 1) * 512], xt[:, s, half * 512:(half + 1) * 512], bank[:], ALU.add,
                            [bk, xk], [xk])
            if not last:
                self.dma("pool", self.xres[t0:t0 + 512, :].rearrange("(s p) d -> p s d", p=128), xt[:], [xk], [])
            else:
                ss = self.T("ssF", [128, 8], F32)
                junk = self.T("junkF", [128, 1024], BF16)
                self.memset("dve", ss[:], 0.0, ["ssF"])
                for s in range(4):
                    self.act(junk[:], xt[:, s, :], AF.Square, [xk, "ssF"], ["junkF", "ssF"], accum=ss[:, s:s + 1])
                self.ts("dve", ss[:, 4:8], ss[:, 0:4], 1.0 / D_MODEL, ALU.mult, ["ssF"], ["ssF"], s2=EPS, op1=ALU.add)
                self.act(ss[:, 4:8], ss[:, 4:8], AF.Sqrt, ["ssF"], ["ssF"])
                self.recip(ss[:, 4:8], ss[:, 4:8], ["ssF"], ["ssF"])
                for s in range(4):
                    self.stt("dve", xt[:, s, :], xt[:, s, :], ss[:, 4 + s:5 + s], gfin[:], ALU.mult, ALU.mult,
                             [xk, "ssF", "gfin"], [xk])
                finals.append(self.dma("pool", self.out[t0:t0 + 512, :].rearrange("(s p) d -> p s d", p=128), xt[:], [xk], []))
                self.off = mark
        return finals

    def mixer_d(self, l):
        self.phase_reset()
        S = self.Sq
        NKB = S // 128
        NQT = S // 512
        kT = self.T("dkT", [64, S], BF16)
        va = self.T("dva", [128, NKB, 128], BF16)
        qts = [self.T(f"dq{i}", [64, 512], BF16) for i in range(2)]
        pts = [self.T(f"dpt{i}", [128, 512], BF16) for i in range(4)]
        rec = self.T("drec", [128, 512], F32)
        yst = [self.T(f"dyst{i}", [64, 512], BF16) for i in range(2)]
        self.memset("pool", va[:, :, 64:128], 1.0, ["dva"])
        qn = 0
        pn = 0
        sn = 0
        for g in range(2):
            self.dma("sp", kT[:], self.qkT[OFF_DK + g * 64:OFF_DK + (g + 1) * 64, :], [], ["dkT"])
            self.dma("sp", va[:, :, 0:64], self.vtm[:, VC_D + g * 64:VC_D + (g + 1) * 64].rearrange("(k p) c -> p k c", p=128), [], ["dva"])
            for h in range(4 * g, 4 * g + 4):
                for qt in range(NQT):
                    q = qts[qn % 2]
                    qk_ = f"dq{qn % 2}"
                    qn += 1
                    self.dma("sp", q[:], self.qkT[OFF_DQ + h * 64:OFF_DQ + (h + 1) * 64, qt * 512:(qt + 1) * 512], [], [qk_])
                    acc = self.pb[4 + (qn % 2)]
                    ak = f"pb{4 + (qn % 2)}"
                    for kb in range(NKB):
                        sb = self.pb[sn % 4]
                        sk = f"pb{sn % 4}"
                        sn += 1
                        self.mm(sb[:], kT[:, kb * 128:(kb + 1) * 128], q[:], True, True, ["dkT", qk_], [sk])
                        pt = pts[pn % 4]
                        pk = f"dpt{pn % 4}"
                        pn += 1
                        self.act(pt[:], sb[:], AF.Exp, [sk], [pk], scale=0.125)
                        self.mm(acc[:], va[:, kb, :], pt[:], kb == 0, kb == NKB - 1, ["dva", pk], [ak])
                    ys = yst[qn % 2]
                    yk = f"dyst{qn % 2}"
                    self.recip(rec[64:128, :], acc[64:128, :], [ak], ["drec"])
                    self.tt("dve", ys[:], acc[0:64, :], rec[64:128, :], ALU.mult, [ak, "drec"], [yk])
                    self.dma("pool", self.yT[3, h * 64:(h + 1) * 64, qt * 512:(qt + 1) * 512], ys[:], [yk], [])

    def mixer_a(self, l):
        self.phase_reset()
        import os
        ADBG = int(os.environ.get("ADBG", "0"))
        KR = 64 if ADBG == 1 else 69
        S = self.Sq
        sm = self.small
        NKB = S // 128
        NQT = S // 512
        kTs = [self.T(f"akT{c}", [69, S], BF16) for c in range(2)]
        va = self.T("ava", [128, NKB, 128], BF16)
        qts = [[[self.T(f"aq{i}{c}{v}", [69, 512], BF16) for v in range(2)] for c in range(2)] for i in range(2)]
        pts = [self.T(f"apt{i}", [128, 512], BF16) for i in range(4)]
        cdt = self.T("acd", [128, 4, 512], F32)
        dgt = [self.T(f"adg{i}", [128, 512], F32) for i in range(2)]
        r0 = self.T("ar0", [128, 512], F32)
        t0_ = self.T("at0", [128, 512], F32)
        t1_ = self.T("at1", [128, 512], F32)
        sq = self.T("asq", [128, 512], F32)
        rs = self.T("ars", [128, 512], F32)
        yst = [self.T(f"ayst{i}", [128, 512], BF16) for i in range(2)]
        qn = 0
        pn = 0
        sn = 0
        dn = 0
        for h in range(4):
            for c in range(2):
                r = OFF_AK + (h * 2 + c) * 64
                self.dma("sp", kTs[c][0:64, :], self.qkT[r:r + 64, :], [], [f"akT{c}"])
                self.dma("sp", kTs[c][64:69, :], self.cd["c_kaug"][h, :, :], [], [f"akT{c}"])
            self.dma("sp", va[:], self.vtm[:, VC_A + h * 128:VC_A + (h + 1) * 128].rearrange("(k p) c -> p k c", p=128), [], ["ava"])
            self.dma("sp", cdt[:], self.cd["c_cd"][:, h, :, :], [], ["acd"])
            for qt in range(NQT):
                qi = qn % 2
                qn += 1
                for c in range(2):
                    r = OFF_AQ + (h * 2 + c) * 64
                    for v in range(2):
                        self.dma("sp", qts[qi][c][v][0:64, :], self.qkT[r:r + 64, qt * 512:(qt + 1) * 512], [], [f"aq{qi}{c}{v}"])
                        self.dma("sp", qts[qi][c][v][64:69, :], self.cd["c_qaug"][h, v, :, qt * 512:(qt + 1) * 512], [], [f"aq{qi}{c}{v}"])
                for c in range(2):
                    acc, ak = self.pb[3], "pb3"
                    den, dk_ = self.pb[4], "pb4"
                    kT = kTs[c]
                    kk = f"akT{c}"
                    for kb in range(NKB):
                        sb = self.pb[sn % 3]
                        sk = f"pb{sn % 3}"
                        sn += 1
                        pt = pts[pn % 4]
                        pk = f"apt{pn % 4}"
                        pn += 1
                        rel = kb - qt * 4
                        if rel < 0 or rel > 3 or ADBG == 2:
                            v = 0 if rel < 0 else 1
                            self.mm(sb[:], kT[0:KR, kb * 128:(kb + 1) * 128], qts[qi][c][v][0:KR, :], True, True, [kk, f"aq{qi}{c}{v}"], [sk])
                            self.act(pt[:], sb[:], AF.Exp, [sk], [pk], scale=0.125)
                        else:
                            self.mm(sb[:], kT[0:KR, kb * 128:(kb + 1) * 128], qts[qi][c][0][0:KR, :], True, True, [kk, f"aq{qi}{c}0"], [sk])
                            dg = dgt[dn % 2]
                            dgk = f"adg{dn % 2}"
                            dn += 1
                            self.ts("dve", dg[:], sb[:], 0.125, ALU.mult, [sk], [dgk])
                            self.tt("dve", dg[:], dg[:], cdt[:, rel, :], ALU.add, [dgk, "acd"], [dgk])
                            self.act(pt[:], dg[:], AF.Exp, [dgk], [pk])
                        self.mm(acc[:], va[:, kb, :], pt[:], kb == 0, kb == NKB - 1, ["ava", pk], [ak])
                        self.mm(den[:], self.ones_b[:], pt[:], kb == 0, kb == NKB - 1, ["const", pk], [dk_])
                    self.recip(r0[:], den[:], [dk_], ["ar0"])
                    if c == 0:
                        self.tt("dve", t0_[:], acc[:], r0[:], ALU.mult, [ak, "ar0"], ["at0"])
                    else:
                        self.tt("dve", t1_[:], acc[:], r0[:], ALU.mult, [ak, "ar0"], ["at1"])
                self.stt("dve", t0_[:], t1_[:], sm[:, 0:1], t0_[:], ALU.mult, ALU.add, ["at0", "at1", "small"], ["at0"])
                self.act(sq[:], t0_[:], AF.Square, ["at0"], ["asq"])
                self.mm(self.pb[5][:], self.ones_f[:], sq[:], True, True, ["asq", "const"], ["pb5"])
                self.ts("dve", rs[:], self.pb[5][:], 1.0 / 128, ALU.mult, ["pb5"], ["ars"], s2=EPS, op1=ALU.add)
                self.act(rs[:], rs[:], AF.Sqrt, ["ars"], ["ars"])
                self.recip(rs[:], rs[:], ["ars"], ["ars"])
                ys = yst[qn % 2]
                yk = f"ayst{qn % 2}"
                self.stt("dve", ys[:], t0_[:], sm[:, 1:2], rs[:], ALU.mult, ALU.mult, ["at0", "ars", "small"], [yk])
                self.dma("pool", self.yT[0, h * 128:(h + 1) * 128, qt * 512:(qt + 1) * 512], ys[:], [yk], [])

    def local_groups(self, qT, kT, va, keys, groups, mask_of, out_fn, tagp):
        pts = self._lp
        tmp = self._lt
        for gi, (qbs, rels) in enumerate(groups):
            nq = len(qbs)
            acc = self.pb[4 + gi % 2]
            ak = f"pb{4 + gi % 2}"
            used = []
            for ri, (r, kbs, m) in enumerate(rels):
                js = [j for j in range(nq) if kbs[j] is not None]
                if not js:
                    continue
                j0, j1 = js[0], js[-1] + 1
                assert js == list(range(j0, j1))
                sb = self.pb[self._sn % 4]
                sk = f"pb{self._sn % 4}"
                self._sn += 1
                for j in js:
                    self.mm(sb[:, j * 128:(j + 1) * 128], kT[:, kbs[j] * 128:(kbs[j] + 1) * 128], qT[:, qbs[j] * 128:(qbs[j] + 1) * 128],
                            True, True, keys, [sk])
                t = tmp[self._pn % 2]
                tk = f"{tagp}lt{self._pn % 2}"
                self._pn += 1
                pi = (gi % 2) * 9 + ri
                pt = pts[pi]
                pk = f"{tagp}lp{pi}"
                nj = j1 - j0
                mv = m.unsqueeze(1).broadcast_to([128, nj, 128]) if nj > 1 else m
                tv = t[:, j0 * 128:j1 * 128].rearrange("p (j q) -> p j q", j=nj) if nj > 1 else t[:, j0 * 128:j1 * 128]
                sv = sb[:, j0 * 128:j1 * 128].rearrange("p (j q) -> p j q", j=nj) if nj > 1 else sb[:, j0 * 128:j1 * 128]
                self.ts("dve", tv, sv, 0.125, ALU.mult, [sk], [tk])
                self.tt("dve", tv, tv, mv, ALU.add, [tk, "lmask"], [tk])
                self.act(pt[:, j0 * 128:j1 * 128], t[:, j0 * 128:j1 * 128], AF.Exp, [tk], [pk])
                used.append((ri, kbs, pt, pk))
            for j in range(nq):
                mine = [(ri, kbs, pt, pk) for (ri, kbs, pt, pk) in used if kbs[j] is not None]
                for n_, (ri, kbs, pt, pk) in enumerate(mine):
                    self.mm(acc[:, j * 128:(j + 1) * 128], va[:, kbs[j], :], pt[:, j * 128:(j + 1) * 128],
                            n_ == 0, n_ == len(mine) - 1, keys + [pk], [ak])
            out_fn(gi, qbs, acc, ak)

    def local_init(self, tagp):
        self._lp = [self.T(f"{tagp}lp{i}", [128, 512], BF16) for i in range(18)]
        self._lt = [self.T(f"{tagp}lt{i}", [128, 512], F32) for i in range(2)]
        self._sn = 0
        self._pn = 0

    def mixer_c(self, l):
        self.phase_reset()
        S = self.Sq
        M = S // 128
        meta = self.cmeta
        NV = len(meta)
        qT = self.T("cqT", [64, S], BF16)
        kT = self.T("ckT", [64, S], BF16)
        va = self.T("cva", [128, M, 128], BF16)
        mcv = self.T("cmcv", [128, NV, 128], F32)
        mcm = self.T("cmcm", [128, NV, 128], F32)
        TT = self.T("cTT", [128, 15, 64], F32)
        rp = self.T("crp", [120, 128], F32)
        rec = self.T("crec", [128, 512], F32)
        yst = [self.T(f"cyst{i}", [64, 512], BF16) for i in range(2)]
        self.local_init("c")
        self.memset("pool", va[:, :, 64:128], 1.0, ["cva"])
        self.dma("sp", mcv[:], self.cd["c_mcv"][:, :, :], [], ["cmcv"])
        self.memset("dve", rp[:], 0.0, ["crp"])
        self.dma("sp", rp[:, 48:79], self.na_rpb[l, :, :], [], ["crp"])
        self.dma("sp", self.rpbr[:, :], rp[:], ["crp"], ["rpbr"])
        Hd = self.T("cHd", [64, 15, 2, 64], F32)
        j64 = self.T("cj64", [64, 64], F32)
        self.dma("sp", j64[:], self.cd["c_j64"][:, :], [], ["cj64"])
        for h in range(8):
            self.dma("sp", qT[:], self.qkT[OFF_CQ + h * 64:OFF_CQ + (h + 1) * 64, :], [], ["cqT"])
            self.dma("sp", kT[:], self.qkT[OFF_CK + h * 64:OFF_CK + (h + 1) * 64, :], [], ["ckT"])
            self.dma("sp", va[:, :, 0:64], self.vtm[:, VC_C + h * 64:VC_C + (h + 1) * 64].rearrange("(k p) c -> p k c", p=128), [], ["cva"])
            for a in range(2):
                src = bass.AP(tensor=self.rpbr.tensor, offset=h * 15 * 128, ap=[[1, 64], [128, 15], [1, 64]])
                self.dma("sp", Hd[:, :, a, :], src, ["rpbr"], ["cHd"])
            for i in range(15):
                bank = self.pb[4 + i // 8]
                bk = f"pb{4 + i // 8}"
                self.mm(bank[:, (i % 8) * 64:(i % 8 + 1) * 64], Hd[:, i, :, :].rearrange("p a k -> p (a k)"), j64[:], True, True,
                        ["cHd", "cj64"], [bk])
            self.cp("dve", TT[:, 0:8, :], self.pb[4][:].rearrange("p (i c) -> p i c", i=8), ["pb4"], ["cTT"])
            self.cp("dve", TT[:, 8:15, :], self.pb[5][:, 0:448].rearrange("p (i c) -> p i c", i=7), ["pb5"], ["cTT"])
            for vi, (cls, dr) in enumerate(meta):
                for a in range(2):
                    for b in range(2):
                        ri = 2 * dr + a - b + 7
                        o = mcm[a * 64:(a + 1) * 64, vi, b * 64:(b + 1) * 64]
                        i0 = mcv[a * 64:(a + 1) * 64, vi, b * 64:(b + 1) * 64]
                        if 0 <= ri <= 14:
                            self.tt("pool", o, i0, TT[a * 64:(a + 1) * 64, ri, :], ALU.add, ["cmcv", "cTT"], ["lmask"])
                        else:
                            self.cp("pool", o, i0, ["cmcv"], ["lmask"])
            midx = {(cls, dr): vi for vi, (cls, dr) in enumerate(meta)}
            groups = []

            def mk(qbs, cls):
                rels = []
                for dr in range(-4, 5):
                    if (cls, dr) not in midx:
                        continue
                    kbs = [qb + dr if 0 <= qb + dr < M else None for qb in qbs]
                    rels.append((dr, kbs, mcm[:, midx[(cls, dr)], :]))
                return (qbs, rels)
            groups.append(mk([0], "m0"))
            groups.append(mk([1], "m1"))
            ints = list(range(2, M - 2))
            for i in range(0, len(ints), 4):
                groups.append(mk(ints[i:i + 4], "int"))
            groups.append(mk([M - 2], "mL2"))
            groups.append(mk([M - 1], "mL1"))

            def outf(gi, qbs, acc, ak, h=h):
                n = len(qbs) * 128
                ys = yst[gi % 2]
                yk = f"cyst{gi % 2}"
                self.recip(rec[64:128, 0:n], acc[64:128, 0:n], [ak], ["crec"])
                self.tt("dve", ys[:, 0:n], acc[0:64, 0:n], rec[64:128, 0:n], ALU.mult, [ak, "crec"], [yk])
                self.dma("pool", self.yT[2, h * 64:(h + 1) * 64, qbs[0] * 128:qbs[0] * 128 + n], ys[:, 0:n], [yk], [])
            import os
            if os.environ.get("CDBG", "0") != "1":
                self.local_groups(qT, kT, va, ["cqT", "ckT", "cva"], groups, None, outf, "c")

    def mixer_b(self, l):
        self.phase_reset()
        S = self.Sq
        NB = S // 128
        qn_ = self.T("bqn", [64, S], BF16)
        kn_ = self.T("bkn", [64, S], BF16)
        qp = self.T("bqp", [64, S], BF16)
        kp = self.T("bkp", [64, S], BF16)
        va = self.T("bva", [128, NB, 128], BF16)
        accB = self.T("baccB", [128, S], F32)
        mk_ = self.T("bmk", [128, 3, 128], F32)
        rec = self.T("brec", [128, 512], F32)
        yst = [self.T(f"byst{i}", [64, 512], BF16) for i in range(2)]
        self.local_init("b")
        self.memset("pool", va[:, :, 64:128], 1.0, ["bva"])
        for h in range(8):
            for g, (win, dil) in enumerate(B_PATTERNS):
                L = S // dil
                nbc = L // 128
                base = OFF_B + g * 1536
                self.dma("sp", qn_[:], self.qkT[base + h * 64:base + (h + 1) * 64, :], [], ["bqn"])
                self.dma("sp", kn_[:], self.qkT[base + 512 + h * 64:base + 512 + (h + 1) * 64, :], [], ["bkn"])
                self.dma("sp", mk_[:], self.cd["c_mb"][g * 8 + h, :, :].rearrange("p (r q) -> p r q", r=3), [], ["lmask"])
                vcol = VC_B + g * 512 + h * 64
                if dil == 1:
                    qq, kk = qn_, kn_
                    keys = ["bqn", "bkn", "bva"]
                    self.dma("sp", va[:, :, 0:64], self.vtm[:, vcol:vcol + 64].rearrange("(k p) c -> p k c", p=128), [], ["bva"])
                else:
                    qq, kk = qp, kp
                    keys = ["bqp", "bkp", "bva"]
                    self.cp("pool", qp[:].rearrange("p (j i) -> p j i", j=dil), qn_[:].rearrange("p (i j) -> p j i", j=dil), ["bqn"], ["bqp"])
                    self.cp("dve", kp[:].rearrange("p (j i) -> p j i", j=dil), kn_[:].rearrange("p (i j) -> p j i", j=dil), ["bkn"], ["bkp"])
                    for j in range(dil):
                        src = self.vtm[:, vcol:vcol + 64].rearrange("(k p j) c -> j p k c", p=128, j=dil)[j]
                        self.dma("sp", va[:, j * nbc:(j + 1) * nbc, 0:64], src, [], ["bva"])
                groups = []
                gs = min(4, nbc)
                for j in range(dil):
                    for b0 in range(0, nbc, gs):
                        qbs = [j * nbc + b0 + i for i in range(gs)]
                        rels = []
                        for r in (-1, 0, 1):
                            kbs = [(qb + r) if 0 <= (qb - j * nbc + r) < nbc else None for qb in qbs]
                            rels.append((r, kbs, mk_[:, r + 1, :]))
                        groups.append((qbs, rels))

                def outf(gi, qbs, acc, ak, g=g, dil=dil, nbc=nbc):
                    n = len(qbs) * 128
                    j = qbs[0] // nbc
                    i0 = (qbs[0] - j * nbc) * 128
                    if dil == 1:
                        dst = accB[:, i0:i0 + n]
                        self.cp("dve", dst, acc[:, 0:n], [ak], ["baccB"])
                    else:
                        dst = accB[:, i0 * dil + j:(i0 + n - 1) * dil + j + 1:dil]
                        self.tt("dve", dst, dst, acc[:, 0:n], ALU.add, [ak, "baccB"], ["baccB"])
                self.local_groups(qq, kk, va, keys, groups, None, outf, "b")
            for ct in range(S // 512):
                ys = yst[ct % 2]
                yk = f"byst{ct % 2}"
                self.recip(rec[64:128, :], accB[64:128, ct * 512:(ct + 1) * 512], ["baccB"], ["brec"])
                self.cp_psum_num(accB, ct)
                self.tt("dve", ys[:], self.pb[6][0:64, :], rec[64:128, :], ALU.mult, ["pb6", "brec"], [yk])
                self.dma("pool", self.yT[1, h * 64:(h + 1) * 64, ct * 512:(ct + 1) * 512], ys[:], [yk], [])

    def cp_psum_num(self, accB, ct):
        self.mm(self.pb[6][0:64, :], self.identf[0:64, 0:64], accB[0:64, ct * 512:(ct + 1) * 512], True, True, ["baccB", "const"], ["pb6"])


_CACHE = {}


def _get_builder(S):
    if S not in _CACHE:
        b = Builder(S)
        b.build()
        _CACHE[S] = b
    return _CACHE[S]


def make_in_map(b, x1, inputs):
    m = {"x": np.ascontiguousarray(x1, dtype=np.float32)}
    m["norm_mix"] = inputs["norm_mix"]
    m["w_in"] = inputs["w_in"]
    m["b_gate"] = inputs["b_gate"]
    m["diff_lambda"] = inputs["diff_lambda"].reshape(DEPTH, 256)
    m["diff_subln"] = inputs["diff_subln"]
    m["na_rpb"] = inputs["na_rpb"].reshape(DEPTH, 120, 31)
    m["qk_norm"] = inputs["qk_norm"]
    m["w_branch"] = inputs["w_branch"].reshape(DEPTH, 2048, D_MODEL)
    m["w_out"] = inputs["w_out"]
    m["norm_ffn"] = inputs["norm_ffn"]
    m["w_ff1"] = inputs["w_ff1"]
    m["w_ff2"] = inputs["w_ff2"]
    m["norm_final"] = inputs["norm_final"].reshape(1, D_MODEL)
    for k, v in b.consts.items():
        m[k] = v
    return {k: np.ascontiguousarray(v) for k, v in m.items()}


def kernel(**inputs):
    inputs = {k: np.asarray(v) for k, v in inputs.items()}
    x = inputs["x"]
    B, S, _ = x.shape
    b = _get_builder(S)
    in_maps = [make_in_map(b, x[i], inputs) for i in range(B)]
    res = run_bass_kernel_spmd(b.nc, in_maps, core_ids=list(range(B)))
    return np.stack([np.asarray(r["out"]) for r in res.results], axis=0).astype(np.float32)
```

```python
import math
import numpy as np
import ml_dtypes
import concourse.bass as bass
import concourse.mybir as mybir
from concourse.bass_utils import run_bass_kernel_spmd

F32 = mybir.dt.float32
BF16 = mybir.dt.bfloat16
AF = mybir.ActivationFunctionType
ALU = mybir.AluOpType
AX = mybir.AxisListType

D_MODEL = 1024
DEPTH = 2
GRID_W = 64
IN_W = 12544
NQK = 8448
EPS = 1e-6
NEG = -1e30
B_PATTERNS = ((128, 1), (512, 4), (2048, 16))
OFF_AQ, OFF_AK, OFF_AV, OFF_B, OFF_CQ, OFF_CK, OFF_CV, OFF_DQ, OFF_DK, OFF_DV, OFF_G = (
    0, 512, 1024, 1536, 6144, 6656, 7168, 7680, 8192, 8320, 8448)
VC_A, VC_B, VC_C, VC_D, VC_N = 0, 512, 2048, 2560, 2688

ENGS = ("pe", "act", "dve", "pool", "sp")
SEM_ROLL = 8000
DMA_SLOTS = 8
SB_BASE = 16512
SB_LIMIT = 228864


class Op:
    __slots__ = ("eng", "fn", "deps", "is_dma", "sig", "need_sig", "slot")

    def __init__(self, eng, fn, is_dma):
        self.eng = eng
        self.fn = fn
        self.is_dma = is_dma
        self.deps = []
        self.sig = None
        self.need_sig = False
        self.slot = None


class Sched:
    def __init__(self, nc):
        self.nc = nc
        self.ops = {e: [] for e in ENGS}
        self.last_w = {}
        self.readers = {}
        self.pending = {e: [] for e in ENGS}

    def _dep(self, op, d):
        if d is op:
            return
        if (not d.is_dma) and (not op.is_dma) and d.eng == op.eng and op.eng == "pe":
            return
        d.need_sig = True
        op.deps.append(d)

    def add(self, eng, fn, reads=(), writes=(), is_dma=False):
        op = Op(eng, fn, is_dma)
        deps = {}
        for r in reads:
            w = self.last_w.get(r)
            if w is not None:
                deps[id(w)] = w
        for r in writes:
            w = self.last_w.get(r)
            if w is not None:
                deps[id(w)] = w
            for rd in self.readers.get(r, ()):
                deps[id(rd)] = rd
        for d in self.pending[eng]:
            deps[id(d)] = d
        self.pending[eng] = []
        for d in deps.values():
            self._dep(op, d)
        for r in writes:
            self.last_w[r] = op
            self.readers[r] = []
        for r in reads:
            if r not in writes:
                self.readers.setdefault(r, []).append(op)
        self.ops[eng].append(op)
        return op

    def barrier(self):
        deps = []
        for e in ENGS:
            ops = self.ops[e]
            for op in reversed(ops):
                if not op.is_dma:
                    deps.append(op)
                    break
            k = 0
            for op in reversed(ops):
                if op.is_dma:
                    deps.append(op)
                    k += 1
                    if k >= DMA_SLOTS:
                        break
        for e in ENGS:
            self.pending[e] = list(deps)
        self.last_w.clear()
        self.readers.clear()

    def emit(self, final_waits=()):
        nc = self.nc
        sem_ctx = []

        def new_sem(name):
            cm = nc.semaphore(name)
            s = cm.__enter__()
            sem_ctx.append(cm)
            return s

        cnt = 0
        for e in ENGS:
            cur = None
            val = 0
            slots = [None] * DMA_SLOTS
            slotv = [0] * DMA_SLOTS
            nd = 0
            for op in self.ops[e]:
                if op.is_dma:
                    s = nd % DMA_SLOTS
                    nd += 1
                    if slots[s] is None or slotv[s] + 16 > SEM_ROLL:
                        slots[s] = new_sem(f"d{e}{s}_{cnt}")
                        cnt += 1
                        slotv[s] = 0
                        prev = None
                    else:
                        prev = (slots[s], slotv[s])
                    slotv[s] += 16
                    op.sig = (slots[s], slotv[s])
                    op.slot = prev
                elif op.need_sig:
                    if cur is None or val + 1 > SEM_ROLL:
                        cur = new_sem(f"c{e}_{cnt}")
                        cnt += 1
                        val = 0
                    val += 1
                    op.sig = (cur, val)
        self.n_sems = cnt
        engmap = {"pe": "tensor", "act": "scalar", "dve": "vector", "pool": "gpsimd", "sp": "sync"}
        with nc.Block() as block:
            for e in ENGS:
                ops = self.ops[e]

                def body(eng, ops=ops, e=e):
                    waited = {}

                    def wait(sem, v):
                        k = id(sem)
                        if waited.get(k, 0) >= v:
                            return
                        waited[k] = v
                        eng.wait_ge(sem, v)

                    for op in ops:
                        if op.is_dma and op.slot is not None:
                            wait(*op.slot)
                        for d in op.deps:
                            wait(*d.sig)
                        ins = op.fn(eng)
                        if op.sig is not None:
                            ins.then_inc(op.sig[0], 16 if op.is_dma else 1)
                    if e == "sp":
                        for fw in final_waits:
                            wait(*fw.sig)

                getattr(block, engmap[e])(body)
        for cm in reversed(sem_ctx):
            cm.__exit__(None, None, None)


def alibi_slopes(n):
    return [2.0 ** (-8.0 * (i + 1) / n) for i in range(n)]


def c_mask_meta(S):
    rows = S // GRID_W
    M = rows // 2
    reps = {"int": 2, "m0": 0, "m1": 1, "mL2": M - 2, "mL1": M - 1}
    meta = []
    tiles = []
    kc = np.arange(64)
    c = np.arange(64)
    qstart = np.clip(c - 8, 0, GRID_W - 16)
    colok = (kc[:, None] >= qstart[None, :]) & (kc[:, None] < qstart[None, :] + 16)
    for cls, m in reps.items():
        for dr in range(-4, 5):
            mk = m + dr
            if mk < 0 or mk >= M:
                continue
            t = np.full((128, 128), NEG, np.float32)
            anyv = False
            for a in range(2):
                for b in range(2):
                    r = 2 * m + b
                    kr = 2 * mk + a
                    rs = min(max(r - 4, 0), rows - 8)
                    if rs <= kr < rs + 8:
                        t[a * 64:(a + 1) * 64, b * 64:(b + 1) * 64] = np.where(colok, 0.0, NEG)
                        anyv = True
            if anyv:
                meta.append((cls, dr))
                tiles.append(t)
    return meta, np.stack(tiles)


def make_consts(S):
    bf = ml_dtypes.bfloat16
    c = {}
    c["c_ident"] = np.eye(128, dtype=np.float32).astype(bf)
    c["c_identf"] = np.eye(128, dtype=np.float32)
    p = np.arange(128)
    i32 = (p % 64) % 32
    partner = np.where(i32 < 16, p + 16, p - 16)
    pm = np.zeros((128, 128), np.float32)
    pm[partner, p] = 1.0
    c["c_pm"] = pm
    c["c_bones"] = (p[:, None] // 64 == p[None, :] // 64).astype(np.float32)
    c["c_ones_f"] = np.ones((128, 128), np.float32)
    c["c_j64"] = np.ascontiguousarray(np.eye(64, dtype=np.float32)[::-1])
    c["c_ones_b"] = np.ones((128, 128), np.float32).astype(bf)
    t = np.arange(S)
    d = p % 64
    half = d // 32
    idx = (d % 32) % 16
    first = (d % 32) < 16
    inv = (10000.0 ** (-np.arange(16, dtype=np.float32) / 16)).astype(np.float32)
    pos = np.where(half[:, None] == 0, (t // GRID_W)[None, :], (t % GRID_W)[None, :]).astype(np.float32)
    ang = pos * inv[idx][:, None]
    rope = np.zeros((2, 128, S), np.float32)
    rope[0] = np.cos(ang)
    rope[1] = np.where(first[:, None], -np.sin(ang), np.sin(ang))
    c["c_rope"] = rope
    sl = alibi_slopes(4)
    kaug = np.zeros((4, 5, S), np.float32)
    qaug = np.zeros((4, 2, 5, S), np.float32)
    for h in range(4):
        s8 = 8.0 * sl[h]
        kaug[h, 0:3] = 1.0
        kaug[h, 3] = s8 * (128 * (t // 128))
        kaug[h, 4] = s8 * (t % 128)
        qaug[h, 0, 0] = -s8 * (512 * (t // 512))
        qaug[h, 0, 1] = -s8 * (256 * ((t % 512) // 256))
        qaug[h, 0, 2] = -s8 * (t % 256)
        qaug[h, 0, 3:5] = 1.0
        qaug[h, 1] = -qaug[h, 0]
    c["c_kaug"] = kaug.astype(bf)
    c["c_qaug"] = qaug.astype(bf)
    assert np.array_equal(c["c_kaug"].astype(np.float32), kaug) and np.array_equal(c["c_qaug"].astype(np.float32), qaug)
    k = np.arange(128)[:, None]
    q = np.arange(128)[None, :]
    q5 = np.arange(512)[None, :]
    cd = np.zeros((128, 4, 4, 512), np.float32)
    for h in range(4):
        for rel in range(4):
            dd = q5 - (k + 128 * rel)
            cd[:, h, rel, :] = 2.0 * sl[h] * np.minimum(dd, 0)
    c["c_cd"] = cd
    slb = alibi_slopes(8)
    mb = np.zeros((24, 128, 3, 128), np.float32)
    for g, (win, dil) in enumerate(B_PATTERNS):
        for h in range(8):
            for r in range(3):
                rel = (128 * (r - 1) + k) - q
                mb[g * 8 + h, :, r, :] = np.where(np.abs(rel) <= 64, -slb[h] * np.abs(rel) * dil, NEG)
    c["c_mb"] = mb.reshape(24, 128, 384)
    meta, mcv = c_mask_meta(S)
    c["c_mcv"] = np.ascontiguousarray(mcv.transpose(1, 0, 2))
    return c, meta


class Builder:
    def __init__(self, S, nl=DEPTH, dbg=None):
        self.Sq = S
        self.nl = nl
        self.dbg = dbg or {}
        self.nc = bass.Bass("TRN2", target_bir_lowering=False)
        self.S = Sched(self.nc)
        self.off = SB_BASE
        self.ncnt = 0
        self.consts, self.cmeta = make_consts(S)
        self.outs = []

    def T(self, name, shape, dt):
        sz = 4 if dt == F32 else 2
        n = 1
        for s in shape[1:]:
            n *= s
        nbytes = (n * sz + 63) // 64 * 64
        self.ncnt += 1
        t = self.nc.alloc_sbuf_tensor_at(f"{name}_{self.ncnt}", list(shape), dt, offset=self.off)
        self.off += nbytes
        assert self.off <= SB_LIMIT, (name, self.off)
        return t

    def dram(self, name, shape, dt, kind="Internal"):
        return self.nc.dram_tensor(name, list(shape), dt, kind=kind).ap()

    def mm(self, out, lhsT, rhs, start, stop, rd, wr):
        return self.S.add("pe", lambda e: e.matmul(out, lhsT=lhsT, rhs=rhs, start=start, stop=stop, skip_group_check=True), rd, wr)

    def tr(self, out, in_, ident, rd, wr):
        return self.S.add("pe", lambda e: e.transpose(out, in_, ident), rd, wr)

    def act(self, out, in_, func, rd, wr, bias=None, scale=1.0, accum=None):
        def f(e):
            kw = {}
            if bias is not None:
                kw["bias"] = bias
            if accum is not None:
                kw["accum_out"] = accum
            return e.activation(out=out, in_=in_, func=func, scale=scale, **kw)
        return self.S.add("act", f, rd, wr)

    def tt(self, eng, out, in0, in1, op, rd, wr):
        return self.S.add(eng, lambda e: e.tensor_tensor(out=out, in0=in0, in1=in1, op=op), rd, wr)

    def ts(self, eng, out, in0, s1, op0, rd, wr, s2=None, op1=None):
        if op1 is None:
            return self.S.add(eng, lambda e: e.tensor_scalar(out=out, in0=in0, scalar1=s1, scalar2=None, op0=op0), rd, wr)
        return self.S.add(eng, lambda e: e.tensor_scalar(out=out, in0=in0, scalar1=s1, scalar2=s2, op0=op0, op1=op1), rd, wr)

    def stt(self, eng, out, in0, scalar, in1, op0, op1, rd, wr):
        return self.S.add(eng, lambda e: e.scalar_tensor_tensor(out=out, in0=in0, scalar=scalar, in1=in1, op0=op0, op1=op1), rd, wr)

    def cp(self, eng, out, in_, rd, wr):
        if eng == "act":
            return self.S.add("act", lambda e: e.copy(out=out, in_=in_), rd, wr)
        return self.S.add(eng, lambda e: e.tensor_copy(out=out, in_=in_), rd, wr)

    def recip(self, out, in_, rd, wr):
        return self.S.add("dve", lambda e: e.reciprocal(out=out, in_=in_), rd, wr)

    def memset(self, eng, ap, v, wr):
        return self.S.add(eng, lambda e: e.memset(ap, v), [], wr)

    def dma(self, q, out, in_, rd, wr, slow=False):
        if slow:
            return self.S.add(q, lambda e: e.dma_start(out=out, in_=in_, allow_slow_non_contiguous=True), rd, wr, is_dma=True)
        return self.S.add(q, lambda e: e.dma_start(out=out, in_=in_), rd, wr, is_dma=True)

    def build(self, stages="WPABCDM"):
        nc = self.nc
        S = self.Sq
        nl = self.nl
        dbg = self.dbg
        di = lambda n, sh, dt=F32: nc.dram_tensor(n, list(sh), dt, kind="ExternalInput").ap()
        self.x_in = di("x", [S, D_MODEL])
        self.norm_mix = di("norm_mix", [DEPTH, D_MODEL])
        self.w_in = di("w_in", [DEPTH, D_MODEL, IN_W])
        self.b_gate = di("b_gate", [DEPTH, 4096])
        self.diff_lambda = di("diff_lambda", [DEPTH, 256])
        self.diff_subln = di("diff_subln", [DEPTH, 128])
        self.na_rpb = di("na_rpb", [DEPTH, 120, 31])
        self.qk_norm = di("qk_norm", [DEPTH, 2, 64])
        self.w_branch = di("w_branch", [DEPTH, 2048, D_MODEL])
        self.w_out = di("w_out", [DEPTH, D_MODEL, D_MODEL])
        self.norm_ffn = di("norm_ffn", [DEPTH, D_MODEL])
        self.w_ff1 = di("w_ff1", [DEPTH, D_MODEL, 4096])
        self.w_ff2 = di("w_ff2", [DEPTH, 4096, D_MODEL])
        self.norm_final = di("norm_final", [1, D_MODEL])
        self.cd = {}
        for k, v in self.consts.items():
            self.cd[k] = di(k, v.shape, BF16 if v.dtype == ml_dtypes.bfloat16 else F32)
        okind = "ExternalOutput"
        self.out = nc.dram_tensor("out", [S, D_MODEL], F32, kind=okind).ap()
        dk = lambda n: okind if dbg.get(n) else "Internal"
        self.wb_in = [self.dram(f"wb_in{l}", [D_MODEL, IN_W], BF16) for l in range(nl)]
        self.wb_br = [self.dram(f"wb_br{l}", [2048, D_MODEL], BF16) for l in range(nl)]
        self.wb_out = [self.dram(f"wb_out{l}", [D_MODEL, D_MODEL], BF16) for l in range(nl)]
        self.wb_f1 = [self.dram(f"wb_f1{l}", [D_MODEL, 4096], BF16) for l in range(nl)]
        self.wb_f2 = [self.dram(f"wb_f2{l}", [4096, D_MODEL], BF16) for l in range(nl)]
        self.qkT = self.dram("qkT", [NQK, S], BF16, dk("qkT"))
        self.vtm = self.dram("vtm", [S, VC_N], BF16, dk("vtm"))
        self.gT = self.dram("gT", [4096, S], BF16, dk("gT"))
        if dbg.get("yT_in"):
            self.yT = di("yT", [4, 512, S], BF16)
        else:
            self.yT = self.dram("yT", [4, 512, S], BF16, dk("yT"))
        self.xres = self.dram("xres", [S, D_MODEL], F32, dk("xres"))
        self.rpbr = self.dram("rpbr", [120, 128], F32)
        for n in ("qkT", "vtm", "gT", "yT", "xres"):
            if dbg.get(n):
                self.outs.append(n)

        self.ident = self.T("ident", [128, 128], BF16)
        self.pm = self.T("pm", [128, 128], F32)
        self.identf = self.T("identf", [128, 128], F32)
        self.bones = self.T("bones", [128, 128], F32)
        self.ones_f = self.T("ones_f", [128, 128], F32)
        self.ones_b = self.T("ones_b", [128, 128], BF16)
        self.small = self.T("small", [128, 64], F32)
        for t, n in ((self.ident, "c_ident"), (self.pm, "c_pm"), (self.identf, "c_identf"), (self.bones, "c_bones"),
                     (self.ones_f, "c_ones_f"), (self.ones_b, "c_ones_b")):
            self.dma("sp", t[:], self.cd[n][:, :], [], ["const"])
        self.pb = [nc.alloc_psum_tensor(f"pb{i}", [128, 512], F32) for i in range(7)]
        self.pbt = nc.alloc_psum_tensor("pbt", [128, 1024], BF16)
        self.base_off = self.off
        self.S.barrier()

        if "W" in stages:
            self.phase_w()
        final = []
        for l in range(nl):
            self.layer_smalls(l)
            xsrc = self.x_in if l == 0 else self.xres
            if "P" in stages:
                self.phase_p(l, xsrc)
            if "D" in stages:
                self.mixer_d(l)
            if "A" in stages:
                self.mixer_a(l)
            if "C" in stages:
                self.mixer_c(l)
            if "B" in stages:
                self.mixer_b(l)
            if "M" in stages:
                final = self.phase_m(l, xsrc, last=(l == nl - 1))
        self.S.barrier()
        self.S.emit(final_waits=final)
        return nc

    def phase_reset(self):
        self.S.barrier()
        self.off = self.base_off

    def phase_w(self):
        self.phase_reset()
        nl = self.nl
        gcol = self.T("gcol", [128, 32], F32)
        for l in range(nl):
            self.dma("sp", gcol[:, l * 16:l * 16 + 8], self.norm_mix[l, :].rearrange("(c p) -> p c", p=128), [], ["gcol"], slow=True)
            self.dma("sp", gcol[:, l * 16 + 8:l * 16 + 16], self.norm_ffn[l, :].rearrange("(c p) -> p c", p=128), [], ["gcol"], slow=True)
        bi = [self.T(f"wci{i}", [128, 4096], F32) for i in range(2)]
        bo = [self.T(f"wco{i}", [128, 4096], BF16) for i in range(2)]
        n = 0
        for l in range(nl):
            jobs = [(self.w_in[l], self.wb_in[l], D_MODEL, IN_W, l * 16),
                    (self.w_branch[l], self.wb_br[l], 2048, D_MODEL, None),
                    (self.w_out[l], self.wb_out[l], D_MODEL, D_MODEL, None),
                    (self.w_ff1[l], self.wb_f1[l], D_MODEL, 4096, l * 16 + 8),
                    (self.w_ff2[l], self.wb_f2[l], 4096, D_MODEL, None)]
            for src, dst, R, C, gc in jobs:
                for rc in range(R // 128):
                    for c0 in range(0, C, 4096):
                        cw = min(4096, C - c0)
                        i = n % 2
                        n += 1
                        self.dma("sp", bi[i][:, 0:cw], src[rc * 128:(rc + 1) * 128, c0:c0 + cw], [], [f"wci{i}"])
                        eng = "dve" if n % 2 else "pool"
                        if gc is not None:
                            self.ts(eng, bo[i][:, 0:cw], bi[i][:, 0:cw], gcol[:, gc + rc:gc + rc + 1], ALU.mult,
                                    [f"wci{i}", "gcol"], [f"wco{i}"])
                        else:
                            self.cp(eng, bo[i][:, 0:cw], bi[i][:, 0:cw], [f"wci{i}"], [f"wco{i}"])
                        self.dma("pool", dst[rc * 128:(rc + 1) * 128, c0:c0 + cw], bo[i][:, 0:cw], [f"wco{i}"], [])

    def layer_smalls(self, l):
        self.phase_reset()
        sm = self.small
        lam_init = 0.8 - 0.6 * math.exp(-0.3 * l)
        lt = self.T("lamt", [128, 256], F32)
        pr = self.T("lampr", [128, 128], F32)
        sv = self.T("lamsv", [128, 4], F32)
        self.dma("sp", lt[:], self.diff_lambda[l:l + 1, :].partition_broadcast(128), [], ["lamt"])
        self.tt("dve", pr[:, 0:64], lt[:, 0:64], lt[:, 64:128], ALU.mult, ["lamt"], ["lampr"])
        self.tt("dve", pr[:, 64:128], lt[:, 128:192], lt[:, 192:256], ALU.mult, ["lamt"], ["lampr"])
        self.S.add("dve", lambda e: e.reduce_sum(out=sv[:, 0:2], in_=pr[:].rearrange("p (a b) -> p a b", a=2), axis=AX.X),
                   ["lampr"], ["lamsv"])
        self.act(sv[:, 2:4], sv[:, 0:2], AF.Exp, ["lamsv"], ["lamsv"])
        self.tt("dve", sm[:, 0:1], sv[:, 3:4], sv[:, 2:3], ALU.subtract, ["lamsv"], ["small"])
        self.ts("dve", sm[:, 0:1], sm[:, 0:1], -lam_init, ALU.add, ["small"], ["small"])
        self.dma("sp", sm[:, 1:2], self.diff_subln[l, :].rearrange("(p o) -> p o", o=1), [], ["small"])
        self.ts("dve", sm[:, 1:2], sm[:, 1:2], 1.0 - lam_init, ALU.mult, ["small"], ["small"])
        for i in range(2):
            for hf in range(2):
                self.dma("sp", sm[hf * 64:(hf + 1) * 64, 2 + i:3 + i], self.qk_norm[l, i, :].rearrange("(p o) -> p o", o=1), [], ["small"])
        self.dma("sp", sm[:, 8:40], self.b_gate[l, :].rearrange("(c p) -> p c", p=128), [], ["small"], slow=True)

    def phase_p(self, l, xsrc):
        self.phase_reset()
        S = self.Sq
        sm = self.small
        NT = S // 512
        xts = [self.T(f"xtP{i}", [128, 4, 1024], F32) for i in range(2)]
        hTs = [self.T(f"hTP{i}", [128, 8, 512], BF16) for i in range(2)]
        wts = [self.T(f"wtP{i}", [128, 8, 512], BF16) for i in range(3)]
        stg = [self.T(f"stgP{i}", [128, 512], BF16) for i in range(4)]
        vst = [self.T(f"vstP{i}", [128, 4, 512], BF16) for i in range(2)]
        rope = [self.T(f"ropeP{i}", [128, 2, 512], F32) for i in range(2)]
        dtmp = [self.T(f"dtmpP{i}", [128, 512], F32) for i in range(5)]
        plan = []
        for c0 in range(0, OFF_G, 512):
            if c0 in (OFF_AV, OFF_B + 1024, OFF_B + 1536 + 1024, OFF_B + 3072 + 1024, OFF_CV):
                vc = {OFF_AV: VC_A, OFF_B + 1024: VC_B, OFF_B + 2560: VC_B + 512, OFF_B + 4096: VC_B + 1024, OFF_CV: VC_C}[c0]
                plan.append((c0, 512, "tm", vc))
            elif c0 == OFF_DQ:
                plan.append((c0, 512, "dq", None))
            elif c0 == OFF_DK:
                plan.append((c0, 256, "dkv", None))
            else:
                plan.append((c0, 512, "fm", None))
        for c0 in range(OFF_G, IN_W, 512):
            plan.append((c0, 512, "gate", None))
        wn = 0
        sn = 0
        pbn = 0
        for tt_ in range(NT):
            t0 = tt_ * 512
            xt = xts[tt_ % 2]
            hT = hTs[tt_ % 2]
            xk, hk = f"xtP{tt_ % 2}", f"hTP{tt_ % 2}"
            rp = rope[tt_ % 2]
            rk = f"ropeP{tt_ % 2}"
            self.dma("sp", xt[:], xsrc[t0:t0 + 512, :].rearrange("(s p) d -> p s d", p=128), [], [xk])
            self.dma("sp", rp[:], self.cd["c_rope"][:, :, t0:t0 + 512].rearrange("a p t -> p a t"), [], [rk])
            mark = self.off
            self.rmsnorm_T_keys(xt, hT, xk, hk)
            self.off = mark
            for (c0, ncol, kind, vc) in plan:
                w = wts[wn % 3]
                wk = f"wtP{wn % 3}"
                wn += 1
                self.dma("sp", w[:, :, 0:ncol], self.wb_in[l][:, c0:c0 + ncol].rearrange("(k p) n -> p k n", p=128), [], [wk])
                if kind in ("fm", "gate", "dq", "dkv"):
                    nfm = {"fm": 4, "gate": 4, "dq": 4, "dkv": 1}[kind]
                    for j in range(nfm):
                        bank = self.pb[pbn % 4]
                        bk = f"pb{pbn % 4}"
                        pbn += 1
                        for kc in range(8):
                            self.mm(bank[:], w[:, kc, j * 128:(j + 1) * 128], hT[:, kc, :], kc == 0, kc == 7, [wk, hk], [bk])
                        st = stg[sn % 4]
                        sk = f"stgP{sn % 4}"
                        sn += 1
                        col = c0 + j * 128
                        if kind == "fm":
                            self.cp("act" if sn % 2 else "dve", st[:], bank[:], [bk], [sk])
                            self.dma("pool", self.qkT[col:col + 128, t0:t0 + 512], st[:], [sk], [])
                        elif kind == "gate":
                            gc = (col - OFF_G) // 128
                            self.act(st[:], bank[:], AF.Sigmoid, [bk, "small"], [sk], bias=sm[:, 8 + gc:9 + gc])
                            self.dma("pool", self.gT[col - OFF_G:col - OFF_G + 128, t0:t0 + 512], st[:], [sk], [])
                        else:
                            gi = 2 if kind == "dq" else 3
                            sq, rst, xg, t1, t2 = dtmp
                            self.act(sq[:], bank[:], AF.Square, [bk], ["dsq"])
                            self.mm(self.pb[4][:], self.bones[:], sq[:], True, True, ["dsq", "const"], ["pb4"])
                            self.ts("dve", rst[:], self.pb[4][:], 1.0 / 64, ALU.mult, ["pb4"], ["drst"], s2=EPS, op1=ALU.add)
                            self.act(rst[:], rst[:], AF.Sqrt, ["drst"], ["drst"])
                            self.recip(rst[:], rst[:], ["drst"], ["drst"])
                            self.ts("dve", xg[:], bank[:], sm[:, gi:gi + 1], ALU.mult, [bk, "small"], ["dxg"])
                            self.mm(self.pb[5][:], self.pm[:], xg[:], True, True, ["dxg", "const"], ["pb5"])
                            self.tt("pool", t1[:], xg[:], rp[:, 0, :], ALU.mult, ["dxg", rk], ["dt1"])
                            self.tt("dve", t2[:], self.pb[5][:], rp[:, 1, :], ALU.mult, ["pb5", rk], ["dt2"])
                            self.tt("pool", t1[:], t1[:], t2[:], ALU.add, ["dt1", "dt2"], ["dt1"])
                            self.tt("dve", st[:], t1[:], rst[:], ALU.mult, ["dt1", "drst"], [sk])
                            self.dma("pool", self.qkT[col:col + 128, t0:t0 + 512], st[:], [sk], [])
                if kind in ("tm", "dkv"):
                    if kind == "tm":
                        wc0, nvc, vcol = 0, 512, vc
                    else:
                        wc0, nvc, vcol = 128, 128, VC_D
                    vs = vst[wn % 2]
                    vk = f"vstP{wn % 2}"
                    for s in range(4):
                        bank = self.pb[pbn % 4]
                        bk = f"pb{pbn % 4}"
                        pbn += 1
                        for kc in range(8):
                            self.mm(bank[:, 0:nvc], hT[:, kc, s * 128:(s + 1) * 128], w[:, kc, wc0:wc0 + nvc], kc == 0, kc == 7, [wk, hk], [bk])
                        self.cp("act" if s % 2 else "dve", vs[:, s, 0:nvc], bank[:, 0:nvc], [bk], [vk])
                    self.dma("pool", self.vtm[t0:t0 + 512, vcol:vcol + nvc].rearrange("(s p) c -> p s c", p=128),
                             vs[:, :, 0:nvc], [vk], [])

    def rmsnorm_T_keys(self, xt, hT, xk, hk, inv_d=1.0 / D_MODEL):
        ss = self.T("ss", [128, 8], F32)
        junk = self.T("junk", [128, 1024], BF16)
        hn = self.T("hn", [128, 4, 1024], BF16)
        self.memset("dve", ss[:], 0.0, ["ss"])
        for s in range(4):
            self.act(junk[:], xt[:, s, :], AF.Square, [xk, "ss"], ["junk", "ss"], accum=ss[:, s:s + 1])
        self.ts("dve", ss[:, 4:8], ss[:, 0:4], inv_d, ALU.mult, ["ss"], ["ss"], s2=EPS, op1=ALU.add)
        self.act(ss[:, 4:8], ss[:, 4:8], AF.Sqrt, ["ss"], ["ss"])
        self.recip(ss[:, 4:8], ss[:, 4:8], ["ss"], ["ss"])
        for s in range(4):
            self.ts("dve" if s % 2 else "pool", hn[:, s, :], xt[:, s, :], ss[:, 4 + s:5 + s], ALU.mult,
                    [xk, "ss"], [f"hn{s}"])
        for c in range(8):
            for s in range(4):
                self.tr(self.pbt[:, s * 128:(s + 1) * 128], hn[:, s, c * 128:(c + 1) * 128], self.ident[:],
                        [f"hn{s}", "const"], ["pbt"])
            self.cp("act" if c % 2 else "dve", hT[:, c, :], self.pbt[:, 0:512], ["pbt"], [hk])

    def phase_m(self, l, xsrc, last):
        self.phase_reset()
        S = self.Sq
        NT = S // 512
        xts = [self.T(f"xtM{i}", [128, 4, 1024], F32) for i in range(2)]
        yts = [self.T(f"ytM{i}", [128, 4, 512], BF16) for i in range(2)]
        gts = [self.T(f"gtM{i}", [128, 8, 512], BF16) for i in range(2)]
        wts = [self.T(f"wtM{i}", [128, 8, 512], BF16) for i in range(3)]
        macc = self.T("macc", [128, 8, 512], F32)
        mtmp = [self.T(f"mtmp{i}", [128, 512], F32) for i in range(2)]
        mT = self.T("mT", [128, 8, 512], BF16)
        h2T = self.T("h2T", [128, 8, 512], BF16)
        uT = self.T("uT", [128, 32, 512], BF16)
        gfin = None
        if last:
            gfin = self.T("gfin", [128, 1024], F32)
            self.dma("sp", gfin[:], self.norm_final[0:1, :].partition_broadcast(128), [], ["gfin"])
        wn = 0
        pbn = 0
        yn = 0
        finals = []
        for tt_ in range(NT):
            t0 = tt_ * 512
            xt = xts[tt_ % 2]
            xk = f"xtM{tt_ % 2}"
            self.dma("sp", xt[:], xsrc[t0:t0 + 512, :].rearrange("(s p) d -> p s d", p=128), [], [xk])
            for n in range(4):
                yt = yts[yn % 2]
                gt = gts[yn % 2]
                yk, gk = f"ytM{yn % 2}", f"gtM{yn % 2}"
                yn += 1
                w = wts[wn % 3]
                wk = f"wtM{wn % 3}"
                wn += 1
                self.dma("sp", yt[:], self.yT[n, :, t0:t0 + 512].rearrange("(k p) t -> p k t", p=128), [], [yk])
                self.dma("sp", gt[:], self.gT[n * 1024:(n + 1) * 1024, t0:t0 + 512].rearrange("(k p) t -> p k t", p=128), [], [gk])
                wv = w[:].rearrange("p k n -> p (k n)").rearrange("p (k n) -> p k n", k=4)
                self.dma("sp", wv, self.wb_br[l][n * 512:(n + 1) * 512, :].rearrange("(k p) n -> p k n", p=128), [], [wk])
                for oc in range(8):
                    bank = self.pb[pbn % 3]
                    bk = f"pb{pbn % 3}"
                    pbn += 1
                    for kc in range(4):
                        self.mm(bank[:], wv[:, kc, oc * 128:(oc + 1) * 128], yt[:, kc, :], kc == 0, kc == 3, [wk, yk], [bk])
                    eng = "dve" if oc % 2 else "pool"
                    if n == 0:
                        self.tt("dve", macc[:, oc, :], bank[:], gt[:, oc, :], ALU.mult, [bk, gk], [f"macc{oc}"])
                    elif n < 3:
                        mt = mtmp[oc % 2]
                        self.tt("dve", mt[:], bank[:], gt[:, oc, :], ALU.mult, [bk, gk], [f"mtmp{oc % 2}"])
                        self.tt("pool", macc[:, oc, :], macc[:, oc, :], mt[:], ALU.add, [f"mtmp{oc % 2}", f"macc{oc}"], [f"macc{oc}"])
                    else:
                        mt = mtmp[oc % 2]
                        self.tt("dve", mt[:], bank[:], gt[:, oc, :], ALU.mult, [bk, gk], [f"mtmp{oc % 2}"])
                        self.tt("pool", mT[:, oc, :], macc[:, oc, :], mt[:], ALU.add, [f"mtmp{oc % 2}", f"macc{oc}"], [f"mT{oc}"])
            mTk = [f"mT{oc}" for oc in range(8)]
            for half in range(2):
                w = wts[wn % 3]
                wk = f"wtM{wn % 3}"
                wn += 1
                self.dma("sp", w[:], self.wb_out[l][:, half * 512:(half + 1) * 512].rearrange("(k p) n -> p k n", p=128), [], [wk])
                for s in range(4):
                    bank = self.pb[3 + pbn % 2]
                    bk = f"pb{3 + pbn % 2}"
                    pbn += 1
                    for kc in range(8):
                        self.mm(bank[:], mT[:, kc, s * 128:(s + 1) * 128], w[:, kc, :], kc == 0, kc == 7, [wk, f"mT{kc}"], [bk])
                    self.tt("dve", xt[:, s, half * 512:(half + 1) * 512], xt[:, s, half * 512:(half + 1) * 512], bank[:], ALU.add,
                            [bk, xk], [xk])
            mark = self.off
            self.rmsnorm_T_keys(xt, h2T, xk, "h2T")
            self.off = mark
            for ft in range(8):
                w = wts[wn % 3]
                wk = f"wtM{wn % 3}"
                wn += 1
                self.dma("sp", w[:], self.wb_f1[l][:, ft * 512:(ft + 1) * 512].rearrange("(k p) n -> p k n", p=128), [], [wk])
                for j in range(4):
                    fc = ft * 4 + j
                    bank = self.pb[pbn % 3]
                    bk = f"pb{pbn % 3}"
                    pbn += 1
                    for kc in range(8):
                        self.mm(bank[:], w[:, kc, j * 128:(j + 1) * 128], h2T[:, kc, :], kc == 0, kc == 7, [wk, "h2T"], [bk])
                    rl = mtmp[fc % 2]
                    self.act(rl[:], bank[:], AF.Relu, [bk], [f"mtmp{fc % 2}"])
                    self.tt("dve" if fc % 2 else "pool", uT[:, fc, :], rl[:], rl[:], ALU.mult, [f"mtmp{fc % 2}"], [f"uT{fc}"])
            for half in range(2):
                banks = [(self.pb[3 + s], f"pb{3 + s}") for s in range(4)]
                for fg in range(4):
                    w = wts[wn % 3]
                    wk = f"wtM{wn % 3}"
                    wn += 1
                    self.dma("sp", w[:], self.wb_f2[l][fg * 1024:(fg + 1) * 1024, half * 512:(half + 1) * 512].rearrange("(k p) n -> p k n", p=128),
                             [], [wk])
                    for s in range(4):
                        bank, bk = banks[s]
                        for kc in range(8):
                            fc = fg * 8 + kc
                            self.mm(bank[:], uT[:, fc, s * 128:(s + 1) * 128], w[:, kc, :], fg == 0 and kc == 0, fg == 3 and kc == 7,
                                    [wk, f"uT{fc}"], [bk])
                for s in range(4):
                    bank, bk = banks[s]
                    self.tt("dve", xt[:, s, half * 512:(half + 1) * 512], xt[:, s, half * 512:(half + 1) * 512], bank[:], ALU.add,
                            [bk, xk], [xk])
            if not last:
                self.dma("pool", self.xres[t0:t0 + 512, :].rearrange("(s p) d -> p s d", p=128), xt[:], [xk], [])
            else:
                ss = self.T("ssF", [128, 8], F32)
                junk = self.T("junkF", [128, 1024], BF16)
                self.memset("dve", ss[:], 0.0, ["ssF"])
                for s in range(4):
                    self.act(junk[:], xt[:, s, :], AF.Square, [xk, "ssF"], ["junkF", "ssF"], accum=ss[:, s:s + 1])
                self.ts("dve", ss[:, 4:8], ss[:, 0:4], 1.0 / D_MODEL, ALU.mult, ["ssF"], ["ssF"], s2=EPS, op1=ALU.add)
                self.act(ss[:, 4:8], ss[:, 4:8], AF.Sqrt, ["ssF"], ["ssF"])
                self.recip(ss[:, 4:8], ss[:, 4:8], ["ssF"], ["ssF"])
                for s in range(4):
                    self.stt("dve", xt[:, s, :], xt[:, s, :], ss[:, 4 + s:5 + s], gfin[:], ALU.mult, ALU.mult,
                             [xk, "ssF", "gfin"], [xk])
                finals.append(self.dma("pool", self.out[t0:t0 + 512, :].rearrange("(s p) d -> p s d", p=128), xt[:], [xk], []))
                self.off = mark
        return finals

    def mixer_d(self, l):
        self.phase_reset()
        S = self.Sq
        NKB = S // 128
        NQT = S // 512
        LA = 3
        kT = self.T("dkT", [64, S], BF16)
        va = self.T("dva", [128, NKB, 128], BF16)
        qts = [self.T(f"dq{i}", [64, 512], BF16) for i in range(3)]
        pts = [self.T(f"dpt{i}", [128, 512], BF16) for i in range(4)]
        rec = [self.T(f"drec{i}", [128, 512], F32) for i in range(2)]
        yst = [self.T(f"dyst{i}", [64, 512], BF16) for i in range(2)]
        self.memset("pool", va[:, :, 64:128], 1.0, ["dva"])
        gq = 0
        for g in range(2):
            self.dma("sp", kT[:], self.qkT[OFF_DK + g * 64:OFF_DK + (g + 1) * 64, :], [], ["dkT"])
            self.dma("sp", va[:, :, 0:64], self.vtm[:, VC_D + g * 64:VC_D + (g + 1) * 64].rearrange("(k p) c -> p k c", p=128), [], ["dva"])
            items = [(h, qt, kb) for h in range(4 * g, 4 * g + 4) for qt in range(NQT) for kb in range(NKB)]
            n = len(items)

            def qk(i, gq=gq):
                h, qt, kb = items[i]
                qi = gq + i // NKB
                q = qts[qi % 3]
                qk_ = f"dq{qi % 3}"
                if kb == 0:
                    self.dma("sp", q[:], self.qkT[OFF_DQ + h * 64:OFF_DQ + (h + 1) * 64, qt * 512:(qt + 1) * 512], [], [qk_])
                self.mm(self.pb[i % 4][:], kT[:, kb * 128:(kb + 1) * 128], q[:], True, True, ["dkT", qk_], [f"pb{i % 4}"])

            def ex_av(i, gq=gq):
                h, qt, kb = items[i]
                qi = gq + i // NKB
                acc = self.pb[4 + qi % 2]
                ak = f"pb{4 + qi % 2}"
                pt = pts[i % 4]
                pk = f"dpt{i % 4}"
                self.act(pt[:], self.pb[i % 4][:], AF.Exp, [f"pb{i % 4}"], [pk], scale=0.125)
                self.mm(acc[:], va[:, kb, :], pt[:], kb == 0, kb == NKB - 1, ["dva", pk], [ak])
                if kb == NKB - 1:
                    ys = yst[qi % 2]
                    yk = f"dyst{qi % 2}"
                    rc = rec[qi % 2]
                    rk = f"drec{qi % 2}"
                    self.recip(rc[64:128, :], acc[64:128, :], [ak], [rk])
                    self.tt("dve", ys[:], acc[0:64, :], rc[64:128, :], ALU.mult, [ak, rk], [yk])
                    self.dma("pool", self.yT[3, h * 64:(h + 1) * 64, qt * 512:(qt + 1) * 512], ys[:], [yk], [])

            for i in range(n + LA):
                if i < n:
                    qk(i)
                if i >= LA:
                    ex_av(i - LA)
            gq += n // NKB

    def mixer_a(self, l):
        self.phase_reset()
        S = self.Sq
        sm = self.small
        NKB = S // 128
        NQT = S // 512
        LA = 2
        KR = 69
        kTs = [self.T(f"akT{c}", [69, S], BF16) for c in range(2)]
        va = self.T("ava", [128, NKB, 128], BF16)
        qts = [[[self.T(f"aq{i}{c}{v}", [69, 512], BF16) for v in range(2)] for c in range(2)] for i in range(2)]
        pts = [self.T(f"apt{i}", [128, 512], BF16) for i in range(4)]
        cdt = self.T("acd", [128, 4, 512], F32)
        dgt = [self.T(f"adg{i}", [128, 512], F32) for i in range(2)]
        r0 = [self.T(f"ar0{i}", [128, 512], F32) for i in range(2)]
        t0s = [self.T(f"at0{i}", [128, 512], F32) for i in range(2)]
        t1_ = self.T("at1", [128, 512], F32)
        sq = self.T("asq", [128, 512], F32)
        rs = self.T("ars", [128, 512], F32)
        yst = [self.T(f"ayst{i}", [128, 512], BF16) for i in range(2)]
        pssum = self.pbt.bitcast(F32)
        gq = 0
        for h in range(4):
            for c in range(2):
                r = OFF_AK + (h * 2 + c) * 64
                self.dma("sp", kTs[c][0:64, :], self.qkT[r:r + 64, :], [], [f"akT{c}"])
                self.dma("sp", kTs[c][64:69, :], self.cd["c_kaug"][h, :, :], [], [f"akT{c}"])
            self.dma("sp", va[:], self.vtm[:, VC_A + h * 128:VC_A + (h + 1) * 128].rearrange("(k p) c -> p k c", p=128), [], ["ava"])
            self.dma("sp", cdt[:], self.cd["c_cd"][:, h, :, :], [], ["acd"])
            items = [(qt, c, kb) for qt in range(NQT) for c in range(2) for kb in range(NKB)]
            n = len(items)

            def qk(i, h=h, gq=gq):
                qt, c, kb = items[i]
                qi = (gq + qt) % 2
                if c == 0 and kb == 0:
                    for c2 in range(2):
                        r = OFF_AQ + (h * 2 + c2) * 64
                        for v in range(2):
                            self.dma("sp", qts[qi][c2][v][0:64, :], self.qkT[r:r + 64, qt * 512:(qt + 1) * 512], [], [f"aq{qi}{c2}{v}"])
                            self.dma("sp", qts[qi][c2][v][64:69, :], self.cd["c_qaug"][h, v, :, qt * 512:(qt + 1) * 512], [], [f"aq{qi}{c2}{v}"])
                rel = kb - qt * 4
                v = 1 if rel > 3 else 0
                self.mm(self.pb[i % 3][:], kTs[c][0:KR, kb * 128:(kb + 1) * 128], qts[qi][c][v][0:KR, :], True, True,
                        [f"akT{c}", f"aq{qi}{c}{v}"], [f"pb{i % 3}"])

            def ex_av(i, h=h, gq=gq):
                qt, c, kb = items[i]
                gi = gq * 2 + i // NKB
                acc, ak = self.pb[3 + 2 * (gi % 2)], f"pb{3 + 2 * (gi % 2)}"
                den, dk_ = self.pb[4 + 2 * (gi % 2)], f"pb{4 + 2 * (gi % 2)}"
                sb, sk = self.pb[i % 3], f"pb{i % 3}"
                pt, pk = pts[i % 4], f"apt{i % 4}"
                rel = kb - qt * 4
                if rel < 0 or rel > 3:
                    self.act(pt[:], sb[:], AF.Exp, [sk], [pk], scale=0.125)
                else:
                    dg, dgk = dgt[i % 2], f"adg{i % 2}"
                    self.ts("dve", dg[:], sb[:], 0.125, ALU.mult, [sk], [dgk])
                    self.tt("dve", dg[:], dg[:], cdt[:, rel, :], ALU.add, [dgk, "acd"], [dgk])
                    self.act(pt[:], dg[:], AF.Exp, [dgk], [pk])
                self.mm(acc[:], va[:, kb, :], pt[:], kb == 0, kb == NKB - 1, ["ava", pk], [ak])
                self.mm(den[:], self.ones_b[:], pt[:], kb == 0, kb == NKB - 1, ["const", pk], [dk_])
                if kb != NKB - 1:
                    return
                qg = gq + qt
                t0_, t0k = t0s[qg % 2], f"at0{qg % 2}"
                rr, rrk = r0[gi % 2], f"ar0{gi % 2}"
                self.recip(rr[:], den[:], [dk_], [rrk])
                if c == 0:
                    self.tt("dve", t0_[:], acc[:], rr[:], ALU.mult, [ak, rrk], [t0k])
                    return
                self.tt("dve", t1_[:], acc[:], rr[:], ALU.mult, [ak, rrk], ["at1"])
                self.stt("dve", t0_[:], t1_[:], sm[:, 0:1], t0_[:], ALU.mult, ALU.add, [t0k, "at1", "small"], [t0k])
                self.act(sq[:], t0_[:], AF.Square, [t0k], ["asq"])
                self.mm(pssum[:], self.ones_f[:], sq[:], True, True, ["asq", "const"], ["pbt"])
                self.ts("dve", rs[:], pssum[:], 1.0 / 128, ALU.mult, ["pbt"], ["ars"], s2=EPS, op1=ALU.add)
                self.act(rs[:], rs[:], AF.Sqrt, ["ars"], ["ars"])
                self.recip(rs[:], rs[:], ["ars"], ["ars"])
                ys, yk = yst[qg % 2], f"ayst{qg % 2}"
                self.stt("dve", ys[:], t0_[:], sm[:, 1:2], rs[:], ALU.mult, ALU.mult, [t0k, "ars", "small"], [yk])
                self.dma("pool", self.yT[0, h * 128:(h + 1) * 128, qt * 512:(qt + 1) * 512], ys[:], [yk], [])

            for i in range(n + LA):
                if i < n:
                    qk(i)
                if i >= LA:
                    ex_av(i - LA)
            gq += NQT

    def local_groups(self, qT, kT, va, keys, groups, mask_of, out_fn, tagp):
        pts = self._lp
        tmp = self._lt
        for gi, (qbs, rels) in enumerate(groups):
            nq = len(qbs)
            acc = self.pb[4 + gi % 2]
            ak = f"pb{4 + gi % 2}"
            used = []
            for ri, (r, kbs, m) in enumerate(rels):
                js = [j for j in range(nq) if kbs[j] is not None]
                if not js:
                    continue
                j0, j1 = js[0], js[-1] + 1
                assert js == list(range(j0, j1))
                sb = self.pb[self._sn % 4]
                sk = f"pb{self._sn % 4}"
                self._sn += 1
                for j in js:
                    self.mm(sb[:, j * 128:(j + 1) * 128], kT[:, kbs[j] * 128:(kbs[j] + 1) * 128], qT[:, qbs[j] * 128:(qbs[j] + 1) * 128],
                            True, True, keys, [sk])
                t = tmp[self._pn % 2]
                tk = f"{tagp}lt{self._pn % 2}"
                self._pn += 1
                pi = (gi % 2) * 9 + ri
                pt = pts[pi]
                pk = f"{tagp}lp{pi}"
                nj = j1 - j0
                mv = m.unsqueeze(1).broadcast_to([128, nj, 128]) if nj > 1 else m
                tv = t[:, j0 * 128:j1 * 128].rearrange("p (j q) -> p j q", j=nj) if nj > 1 else t[:, j0 * 128:j1 * 128]
                sv = sb[:, j0 * 128:j1 * 128].rearrange("p (j q) -> p j q", j=nj) if nj > 1 else sb[:, j0 * 128:j1 * 128]
                self.ts("dve", tv, sv, 0.125, ALU.mult, [sk], [tk])
                self.tt("dve", tv, tv, mv, ALU.add, [tk, "lmask"], [tk])
                self.act(pt[:, j0 * 128:j1 * 128], t[:, j0 * 128:j1 * 128], AF.Exp, [tk], [pk])
                used.append((ri, kbs, pt, pk))
            for j in range(nq):
                mine = [(ri, kbs, pt, pk) for (ri, kbs, pt, pk) in used if kbs[j] is not None]
                for n_, (ri, kbs, pt, pk) in enumerate(mine):
                    self.mm(acc[:, j * 128:(j + 1) * 128], va[:, kbs[j], :], pt[:, j * 128:(j + 1) * 128],
                            n_ == 0, n_ == len(mine) - 1, keys + [pk], [ak])
            out_fn(gi, qbs, acc, ak)

    def local_init(self, tagp):
        self._lp = [self.T(f"{tagp}lp{i}", [128, 512], BF16) for i in range(18)]
        self._lt = [self.T(f"{tagp}lt{i}", [128, 512], F32) for i in range(2)]
        self._sn = 0
        self._pn = 0

    def mixer_c(self, l):
        self.phase_reset()
        S = self.Sq
        M = S // 128
        meta = self.cmeta
        NV = len(meta)
        qT = self.T("cqT", [64, S], BF16)
        kT = self.T("ckT", [64, S], BF16)
        va = self.T("cva", [128, M, 128], BF16)
        mcv = self.T("cmcv", [128, NV, 128], F32)
        mcm = self.T("cmcm", [128, NV, 128], F32)
        TT = self.T("cTT", [128, 15, 64], F32)
        rp = self.T("crp", [120, 128], F32)
        rec = self.T("crec", [128, 512], F32)
        yst = [self.T(f"cyst{i}", [64, 512], BF16) for i in range(2)]
        self.local_init("c")
        self.memset("pool", va[:, :, 64:128], 1.0, ["cva"])
        self.dma("sp", mcv[:], self.cd["c_mcv"][:, :, :], [], ["cmcv"])
        self.memset("dve", rp[:], 0.0, ["crp"])
        self.dma("sp", rp[:, 48:79], self.na_rpb[l, :, :], [], ["crp"])
        self.dma("sp", self.rpbr[:, :], rp[:], ["crp"], ["rpbr"])
        Hd = self.T("cHd", [64, 15, 2, 64], F32)
        j64 = self.T("cj64", [64, 64], F32)
        self.dma("sp", j64[:], self.cd["c_j64"][:, :], [], ["cj64"])
        for h in range(8):
            self.dma("sp", qT[:], self.qkT[OFF_CQ + h * 64:OFF_CQ + (h + 1) * 64, :], [], ["cqT"])
            self.dma("sp", kT[:], self.qkT[OFF_CK + h * 64:OFF_CK + (h + 1) * 64, :], [], ["ckT"])
            self.dma("sp", va[:, :, 0:64], self.vtm[:, VC_C + h * 64:VC_C + (h + 1) * 64].rearrange("(k p) c -> p k c", p=128), [], ["cva"])
            for a in range(2):
                src = bass.AP(tensor=self.rpbr.tensor, offset=h * 15 * 128, ap=[[1, 64], [128, 15], [1, 64]])
                self.dma("sp", Hd[:, :, a, :], src, ["rpbr"], ["cHd"])
            for i in range(15):
                bank = self.pb[4 + i // 8]
                bk = f"pb{4 + i // 8}"
                self.mm(bank[:, (i % 8) * 64:(i % 8 + 1) * 64], Hd[:, i, :, :].rearrange("p a k -> p (a k)"), j64[:], True, True,
                        ["cHd", "cj64"], [bk])
            self.cp("dve", TT[:, 0:8, :], self.pb[4][:].rearrange("p (i c) -> p i c", i=8), ["pb4"], ["cTT"])
            self.cp("dve", TT[:, 8:15, :], self.pb[5][:, 0:448].rearrange("p (i c) -> p i c", i=7), ["pb5"], ["cTT"])
            for vi, (cls, dr) in enumerate(meta):
                for a in range(2):
                    for b in range(2):
                        ri = 2 * dr + a - b + 7
                        o = mcm[a * 64:(a + 1) * 64, vi, b * 64:(b + 1) * 64]
                        i0 = mcv[a * 64:(a + 1) * 64, vi, b * 64:(b + 1) * 64]
                        if 0 <= ri <= 14:
                            self.tt("pool", o, i0, TT[a * 64:(a + 1) * 64, ri, :], ALU.add, ["cmcv", "cTT"], ["lmask"])
                        else:
                            self.cp("pool", o, i0, ["cmcv"], ["lmask"])
            midx = {(cls, dr): vi for vi, (cls, dr) in enumerate(meta)}
            groups = []

            def mk(qbs, cls):
                rels = []
                for dr in range(-4, 5):
                    if (cls, dr) not in midx:
                        continue
                    kbs = [qb + dr if 0 <= qb + dr < M else None for qb in qbs]
                    rels.append((dr, kbs, mcm[:, midx[(cls, dr)], :]))
                return (qbs, rels)
            groups.append(mk([0], "m0"))
            groups.append(mk([1], "m1"))
            ints = list(range(2, M - 2))
            for i in range(0, len(ints), 4):
                groups.append(mk(ints[i:i + 4], "int"))
            groups.append(mk([M - 2], "mL2"))
            groups.append(mk([M - 1], "mL1"))

            def outf(gi, qbs, acc, ak, h=h):
                n = len(qbs) * 128
                ys = yst[gi % 2]
                yk = f"cyst{gi % 2}"
                self.recip(rec[64:128, 0:n], acc[64:128, 0:n], [ak], ["crec"])
                self.tt("dve", ys[:, 0:n], acc[0:64, 0:n], rec[64:128, 0:n], ALU.mult, [ak, "crec"], [yk])
                self.dma("pool", self.yT[2, h * 64:(h + 1) * 64, qbs[0] * 128:qbs[0] * 128 + n], ys[:, 0:n], [yk], [])
            import os
            if os.environ.get("CDBG", "0") != "1":
                self.local_groups(qT, kT, va, ["cqT", "ckT", "cva"], groups, None, outf, "c")

    def mixer_b(self, l):
        self.phase_reset()
        S = self.Sq
        NB = S // 128
        qn_ = self.T("bqn", [64, S], BF16)
        kn_ = self.T("bkn", [64, S], BF16)
        qp = self.T("bqp", [64, S], BF16)
        kp = self.T("bkp", [64, S], BF16)
        va = self.T("bva", [128, NB, 128], BF16)
        accB = self.T("baccB", [128, S], F32)
        mk_ = self.T("bmk", [128, 3, 128], F32)
        rec = self.T("brec", [128, 512], F32)
        yst = [self.T(f"byst{i}", [64, 512], BF16) for i in range(2)]
        self.local_init("b")
        self.memset("pool", va[:, :, 64:128], 1.0, ["bva"])
        for h in range(8):
            for g, (win, dil) in enumerate(B_PATTERNS):
                L = S // dil
                nbc = L // 128
                base = OFF_B + g * 1536
                self.dma("sp", qn_[:], self.qkT[base + h * 64:base + (h + 1) * 64, :], [], ["bqn"])
                self.dma("sp", kn_[:], self.qkT[base + 512 + h * 64:base + 512 + (h + 1) * 64, :], [], ["bkn"])
                self.dma("sp", mk_[:], self.cd["c_mb"][g * 8 + h, :, :].rearrange("p (r q) -> p r q", r=3), [], ["lmask"])
                vcol = VC_B + g * 512 + h * 64
                if dil == 1:
                    qq, kk = qn_, kn_
                    keys = ["bqn", "bkn", "bva"]
                    self.dma("sp", va[:, :, 0:64], self.vtm[:, vcol:vcol + 64].rearrange("(k p) c -> p k c", p=128), [], ["bva"])
                else:
                    qq, kk = qp, kp
                    keys = ["bqp", "bkp", "bva"]
                    self.cp("pool", qp[:].rearrange("p (j i) -> p j i", j=dil), qn_[:].rearrange("p (i j) -> p j i", j=dil), ["bqn"], ["bqp"])
                    self.cp("dve", kp[:].rearrange("p (j i) -> p j i", j=dil), kn_[:].rearrange("p (i j) -> p j i", j=dil), ["bkn"], ["bkp"])
                    for j in range(dil):
                        src = self.vtm[:, vcol:vcol + 64].rearrange("(k p j) c -> j p k c", p=128, j=dil)[j]
                        self.dma("sp", va[:, j * nbc:(j + 1) * nbc, 0:64], src, [], ["bva"])
                groups = []
                gs = min(4, nbc)
                for j in range(dil):
                    for b0 in range(0, nbc, gs):
                        qbs = [j * nbc + b0 + i for i in range(gs)]
                        rels = []
                        for r in (-1, 0, 1):
                            kbs = [(qb + r) if 0 <= (qb - j * nbc + r) < nbc else None for qb in qbs]
                            rels.append((r, kbs, mk_[:, r + 1, :]))
                        groups.append((qbs, rels))

                def outf(gi, qbs, acc, ak, g=g, dil=dil, nbc=nbc):
                    n = len(qbs) * 128
                    j = qbs[0] // nbc
                    i0 = (qbs[0] - j * nbc) * 128
                    if dil == 1:
                        dst = accB[:, i0:i0 + n]
                        self.cp("dve", dst, acc[:, 0:n], [ak], ["baccB"])
                    else:
                        dst = accB[:, i0 * dil + j:(i0 + n - 1) * dil + j + 1:dil]
                        self.tt("dve", dst, dst, acc[:, 0:n], ALU.add, [ak, "baccB"], ["baccB"])
                self.local_groups(qq, kk, va, keys, groups, None, outf, "b")
            for ct in range(S // 512):
                ys = yst[ct % 2]
                yk = f"byst{ct % 2}"
                self.recip(rec[64:128, :], accB[64:128, ct * 512:(ct + 1) * 512], ["baccB"], ["brec"])
                self.cp_psum_num(accB, ct)
                self.tt("dve", ys[:], self.pb[6][0:64, :], rec[64:128, :], ALU.mult, ["pb6", "brec"], [yk])
                self.dma("pool", self.yT[1, h * 64:(h + 1) * 64, ct * 512:(ct + 1) * 512], ys[:], [yk], [])

    def cp_psum_num(self, accB, ct):
        self.mm(self.pb[6][0:64, :], self.identf[0:64, 0:64], accB[0:64, ct * 512:(ct + 1) * 512], True, True, ["baccB", "const"], ["pb6"])


_CACHE = {}


def _get_builder(S):
    if S not in _CACHE:
        b = Builder(S)
        b.build()
        _CACHE[S] = b
    return _CACHE[S]


def make_in_map(b, x1, inputs):
    m = {"x": np.ascontiguousarray(x1, dtype=np.float32)}
    m["norm_mix"] = inputs["norm_mix"]
    m["w_in"] = inputs["w_in"]
    m["b_gate"] = inputs["b_gate"]
    m["diff_lambda"] = inputs["diff_lambda"].reshape(DEPTH, 256)
    m["diff_subln"] = inputs["diff_subln"]
    m["na_rpb"] = inputs["na_rpb"].reshape(DEPTH, 120, 31)
    m["qk_norm"] = inputs["qk_norm"]
    m["w_branch"] = inputs["w_branch"].reshape(DEPTH, 2048, D_MODEL)
    m["w_out"] = inputs["w_out"]
    m["norm_ffn"] = inputs["norm_ffn"]
    m["w_ff1"] = inputs["w_ff1"]
    m["w_ff2"] = inputs["w_ff2"]
    m["norm_final"] = inputs["norm_final"].reshape(1, D_MODEL)
    for k, v in b.consts.items():
        m[k] = v
    return {k: np.ascontiguousarray(v) for k, v in m.items()}


def kernel(**inputs):
    inputs = {k: np.asarray(v) for k, v in inputs.items()}
    x = inputs["x"]
    B, S, _ = x.shape
    b = _get_builder(S)
    in_maps = [make_in_map(b, x[i], inputs) for i in range(B)]
    res = run_bass_kernel_spmd(b.nc, in_maps, core_ids=list(range(B)))
    return np.stack([np.asarray(r["out"]) for r in res.results], axis=0).astype(np.float32)
```

```python
import math
import os
import numpy as np
import ml_dtypes
import concourse.bass as bass
import concourse.mybir as mybir
from concourse.bass_utils import run_bass_kernel_spmd

F32 = mybir.dt.float32
BF16 = mybir.dt.bfloat16
AF = mybir.ActivationFunctionType
ALU = mybir.AluOpType
AX = mybir.AxisListType

D_MODEL = 1024
DEPTH = 2
GRID_W = 64
IN_W = 12544
NQK = 8448
EPS = 1e-6
NEG = -1e30
B_PATTERNS = ((128, 1), (512, 4), (2048, 16))
OFF_AQ, OFF_AK, OFF_AV, OFF_B, OFF_CQ, OFF_CK, OFF_CV, OFF_DQ, OFF_DK, OFF_DV, OFF_G = (
    0, 512, 1024, 1536, 6144, 6656, 7168, 7680, 8192, 8320, 8448)
VC_A, VC_B, VC_C, VC_D, VC_N = 0, 512, 2048, 2560, 2688

ENGS = ("pe", "act", "dve", "pool", "sp")
SEM_ROLL = 8000
DMA_SLOTS = 8
SB_BASE = 16512
SB_LIMIT = 228864


class Op:
    __slots__ = ("eng", "fn", "deps", "is_dma", "sig", "need_sig", "slot")

    def __init__(self, eng, fn, is_dma):
        self.eng = eng
        self.fn = fn
        self.is_dma = is_dma
        self.deps = []
        self.sig = None
        self.need_sig = False
        self.slot = None


class Sched:
    def __init__(self, nc):
        self.nc = nc
        self.ops = {e: [] for e in ENGS}
        self.last_w = {}
        self.readers = {}
        self.pending = {e: [] for e in ENGS}

    def _dep(self, op, d):
        if d is op:
            return
        if (not d.is_dma) and (not op.is_dma) and d.eng == op.eng and op.eng == "pe":
            return
        d.need_sig = True
        op.deps.append(d)

    def add(self, eng, fn, reads=(), writes=(), is_dma=False):
        op = Op(eng, fn, is_dma)
        deps = {}
        for r in reads:
            w = self.last_w.get(r)
            if w is not None:
                deps[id(w)] = w
        for r in writes:
            w = self.last_w.get(r)
            if w is not None:
                deps[id(w)] = w
            for rd in self.readers.get(r, ()):
                deps[id(rd)] = rd
        for d in self.pending[eng]:
            deps[id(d)] = d
        self.pending[eng] = []
        for d in deps.values():
            self._dep(op, d)
        for r in writes:
            self.last_w[r] = op
            self.readers[r] = []
        for r in reads:
            if r not in writes:
                self.readers.setdefault(r, []).append(op)
        self.ops[eng].append(op)
        return op

    def barrier(self):
        deps = []
        for e in ENGS:
            ops = self.ops[e]
            for op in reversed(ops):
                if not op.is_dma:
                    deps.append(op)
                    break
            k = 0
            for op in reversed(ops):
                if op.is_dma:
                    deps.append(op)
                    k += 1
                    if k >= DMA_SLOTS:
                        break
        for e in ENGS:
            self.pending[e] = list(deps)
        self.last_w.clear()
        self.readers.clear()

    def emit(self, final_waits=()):
        nc = self.nc
        sem_ctx = []

        def new_sem(name):
            cm = nc.semaphore(name)
            s = cm.__enter__()
            sem_ctx.append(cm)
            return s

        cnt = 0
        for e in ENGS:
            cur = None
            val = 0
            slots = [None] * DMA_SLOTS
            slotv = [0] * DMA_SLOTS
            nd = 0
            for op in self.ops[e]:
                if op.is_dma:
                    s = nd % DMA_SLOTS
                    nd += 1
                    if slots[s] is None or slotv[s] + 16 > SEM_ROLL:
                        slots[s] = new_sem(f"d{e}{s}_{cnt}")
                        cnt += 1
                        slotv[s] = 0
                        prev = None
                    else:
                        prev = (slots[s], slotv[s])
                    slotv[s] += 16
                    op.sig = (slots[s], slotv[s])
                    op.slot = prev
                elif op.need_sig:
                    if cur is None or val + 1 > SEM_ROLL:
                        cur = new_sem(f"c{e}_{cnt}")
                        cnt += 1
                        val = 0
                    val += 1
                    op.sig = (cur, val)
        self.n_sems = cnt
        engmap = {"pe": "tensor", "act": "scalar", "dve": "vector", "pool": "gpsimd", "sp": "sync"}
        with nc.Block() as block:
            for e in ENGS:
                ops = self.ops[e]

                def body(eng, ops=ops, e=e):
                    waited = {}

                    def wait(sem, v):
                        k = id(sem)
                        if waited.get(k, 0) >= v:
                            return
                        waited[k] = v
                        eng.wait_ge(sem, v)

                    for op in ops:
                        if op.is_dma and op.slot is not None:
                            wait(*op.slot)
                        for d in op.deps:
                            wait(*d.sig)
                        ins = op.fn(eng)
                        if op.sig is not None:
                            ins.then_inc(op.sig[0], 16 if op.is_dma else 1)
                    if e == "sp":
                        for fw in final_waits:
                            wait(*fw.sig)

                getattr(block, engmap[e])(body)
        for cm in reversed(sem_ctx):
            cm.__exit__(None, None, None)


def alibi_slopes(n):
    return [2.0 ** (-8.0 * (i + 1) / n) for i in range(n)]


def c_mask_meta(S):
    rows = S // GRID_W
    M = rows // 2
    reps = {"int": 2, "m0": 0, "m1": 1, "mL2": M - 2, "mL1": M - 1}
    meta = []
    tiles = []
    kc = np.arange(64)
    c = np.arange(64)
    qstart = np.clip(c - 8, 0, GRID_W - 16)
    colok = (kc[:, None] >= qstart[None, :]) & (kc[:, None] < qstart[None, :] + 16)
    for cls, m in reps.items():
        for dr in range(-4, 5):
            mk = m + dr
            if mk < 0 or mk >= M:
                continue
            t = np.full((128, 128), NEG, np.float32)
            anyv = False
            for a in range(2):
                for b in range(2):
                    r = 2 * m + b
                    kr = 2 * mk + a
                    rs = min(max(r - 4, 0), rows - 8)
                    if rs <= kr < rs + 8:
                        t[a * 64:(a + 1) * 64, b * 64:(b + 1) * 64] = np.where(colok, 0.0, NEG)
                        anyv = True
            if anyv:
                meta.append((cls, dr))
                tiles.append(t)
    return meta, np.stack(tiles)


def make_consts(S):
    bf = ml_dtypes.bfloat16
    c = {}
    c["c_ident"] = np.eye(128, dtype=np.float32).astype(bf)
    c["c_identf"] = np.eye(128, dtype=np.float32)
    p = np.arange(128)
    i32 = (p % 64) % 32
    partner = np.where(i32 < 16, p + 16, p - 16)
    pm = np.zeros((128, 128), np.float32)
    pm[partner, p] = 1.0
    c["c_pm"] = pm
    c["c_bones"] = (p[:, None] // 64 == p[None, :] // 64).astype(np.float32)
    c["c_ones_f"] = np.ones((128, 128), np.float32)
    c["c_j64"] = np.ascontiguousarray(np.eye(64, dtype=np.float32)[::-1])
    c["c_ones_b"] = np.ones((128, 128), np.float32).astype(bf)
    t = np.arange(S)
    d = p % 64
    half = d // 32
    idx = (d % 32) % 16
    first = (d % 32) < 16
    inv = (10000.0 ** (-np.arange(16, dtype=np.float32) / 16)).astype(np.float32)
    pos = np.where(half[:, None] == 0, (t // GRID_W)[None, :], (t % GRID_W)[None, :]).astype(np.float32)
    ang = pos * inv[idx][:, None]
    rope = np.zeros((2, 128, S), np.float32)
    rope[0] = np.cos(ang)
    rope[1] = np.where(first[:, None], -np.sin(ang), np.sin(ang))
    c["c_rope"] = rope
    sl = alibi_slopes(4)
    kaug = np.zeros((4, 5, S), np.float32)
    qaug = np.zeros((4, 2, 5, S), np.float32)
    for h in range(4):
        s8 = 8.0 * sl[h]
        kaug[h, 0:3] = 1.0
        kaug[h, 3] = s8 * (128 * (t // 128))
        kaug[h, 4] = s8 * (t % 128)
        qaug[h, 0, 0] = -s8 * (512 * (t // 512))
        qaug[h, 0, 1] = -s8 * (256 * ((t % 512) // 256))
        qaug[h, 0, 2] = -s8 * (t % 256)
        qaug[h, 0, 3:5] = 1.0
        qaug[h, 1] = -qaug[h, 0]
    c["c_kaug"] = kaug.astype(bf)
    c["c_qaug"] = qaug.astype(bf)
    assert np.array_equal(c["c_kaug"].astype(np.float32), kaug) and np.array_equal(c["c_qaug"].astype(np.float32), qaug)
    k = np.arange(128)[:, None]
    q = np.arange(128)[None, :]
    q5 = np.arange(512)[None, :]
    cd = np.zeros((128, 4, 4, 512), np.float32)
    for h in range(4):
        for rel in range(4):
            dd = q5 - (k + 128 * rel)
            cd[:, h, rel, :] = 2.0 * sl[h] * np.minimum(dd, 0)
    c["c_cd"] = cd
    slb = alibi_slopes(8)
    mb = np.zeros((24, 128, 3, 128), np.float32)
    for g, (win, dil) in enumerate(B_PATTERNS):
        for h in range(8):
            for r in range(3):
                rel = (128 * (r - 1) + k) - q
                mb[g * 8 + h, :, r, :] = np.where(np.abs(rel) <= 64, -slb[h] * np.abs(rel) * dil, NEG)
    c["c_mb"] = mb.reshape(24, 128, 384)
    meta, mcv = c_mask_meta(S)
    c["c_mcv"] = np.ascontiguousarray(mcv.transpose(1, 0, 2))
    return c, meta


class Builder:
    def __init__(self, S, nl=DEPTH, dbg=None):
        self.Sq = S
        self.nl = nl
        self.dbg = dbg or {}
        self.nc = bass.Bass("TRN2", target_bir_lowering=False)
        self.S = Sched(self.nc)
        self.off = SB_BASE
        self.ncnt = 0
        self.consts, self.cmeta = make_consts(S)
        self.outs = []

    def T(self, name, shape, dt):
        sz = 4 if dt == F32 else 2
        n = 1
        for s in shape[1:]:
            n *= s
        nbytes = (n * sz + 63) // 64 * 64
        self.ncnt += 1
        t = self.nc.alloc_sbuf_tensor_at(f"{name}_{self.ncnt}", list(shape), dt, offset=self.off)
        self.off += nbytes
        assert self.off <= SB_LIMIT, (name, self.off)
        return t

    def dram(self, name, shape, dt, kind="Internal"):
        return self.nc.dram_tensor(name, list(shape), dt, kind=kind).ap()

    def mm(self, out, lhsT, rhs, start, stop, rd, wr):
        return self.S.add("pe", lambda e: e.matmul(out, lhsT=lhsT, rhs=rhs, start=start, stop=stop, skip_group_check=True), rd, wr)

    def tr(self, out, in_, ident, rd, wr):
        return self.S.add("pe", lambda e: e.transpose(out, in_, ident), rd, wr)

    def act(self, out, in_, func, rd, wr, bias=None, scale=1.0, accum=None):
        def f(e):
            kw = {}
            if bias is not None:
                kw["bias"] = bias
            if accum is not None:
                kw["accum_out"] = accum
            return e.activation(out=out, in_=in_, func=func, scale=scale, **kw)
        return self.S.add("act", f, rd, wr)

    def tt(self, eng, out, in0, in1, op, rd, wr):
        return self.S.add(eng, lambda e: e.tensor_tensor(out=out, in0=in0, in1=in1, op=op), rd, wr)

    def ts(self, eng, out, in0, s1, op0, rd, wr, s2=None, op1=None):
        if op1 is None:
            return self.S.add(eng, lambda e: e.tensor_scalar(out=out, in0=in0, scalar1=s1, scalar2=None, op0=op0), rd, wr)
        return self.S.add(eng, lambda e: e.tensor_scalar(out=out, in0=in0, scalar1=s1, scalar2=s2, op0=op0, op1=op1), rd, wr)

    def stt(self, eng, out, in0, scalar, in1, op0, op1, rd, wr):
        return self.S.add(eng, lambda e: e.scalar_tensor_tensor(out=out, in0=in0, scalar=scalar, in1=in1, op0=op0, op1=op1), rd, wr)

    def cp(self, eng, out, in_, rd, wr):
        if eng == "act":
            return self.S.add("act", lambda e: e.copy(out=out, in_=in_), rd, wr)
        return self.S.add(eng, lambda e: e.tensor_copy(out=out, in_=in_), rd, wr)

    def recip(self, out, in_, rd, wr):
        return self.S.add("dve", lambda e: e.reciprocal(out=out, in_=in_), rd, wr)

    def memset(self, eng, ap, v, wr):
        return self.S.add(eng, lambda e: e.memset(ap, v), [], wr)

    def dma(self, q, out, in_, rd, wr, slow=False):
        if slow:
            return self.S.add(q, lambda e: e.dma_start(out=out, in_=in_, allow_slow_non_contiguous=True), rd, wr, is_dma=True)
        return self.S.add(q, lambda e: e.dma_start(out=out, in_=in_), rd, wr, is_dma=True)

    def build(self, stages="WPABCDM"):
        nc = self.nc
        S = self.Sq
        nl = self.nl
        dbg = self.dbg
        di = lambda n, sh, dt=F32: nc.dram_tensor(n, list(sh), dt, kind="ExternalInput").ap()
        self.x_in = di("x", [S, D_MODEL])
        self.norm_mix = di("norm_mix", [DEPTH, D_MODEL])
        self.w_in = di("w_in", [DEPTH, D_MODEL, IN_W])
        self.b_gate = di("b_gate", [DEPTH, 4096])
        self.diff_lambda = di("diff_lambda", [DEPTH, 256])
        self.diff_subln = di("diff_subln", [DEPTH, 128])
        self.na_rpb = di("na_rpb", [DEPTH, 120, 31])
        self.qk_norm = di("qk_norm", [DEPTH, 2, 64])
        self.w_branch = di("w_branch", [DEPTH, 2048, D_MODEL])
        self.w_out = di("w_out", [DEPTH, D_MODEL, D_MODEL])
        self.norm_ffn = di("norm_ffn", [DEPTH, D_MODEL])
        self.w_ff1 = di("w_ff1", [DEPTH, D_MODEL, 4096])
        self.w_ff2 = di("w_ff2", [DEPTH, 4096, D_MODEL])
        self.norm_final = di("norm_final", [1, D_MODEL])
        self.cd = {}
        for k, v in self.consts.items():
            self.cd[k] = di(k, v.shape, BF16 if v.dtype == ml_dtypes.bfloat16 else F32)
        okind = "ExternalOutput"
        self.out = nc.dram_tensor("out", [S, D_MODEL], F32, kind=okind).ap()
        dk = lambda n: okind if dbg.get(n) else "Internal"
        self.wb_in = [self.dram(f"wb_in{l}", [D_MODEL, IN_W], BF16) for l in range(nl)]
        self.wb_br = [self.dram(f"wb_br{l}", [2048, D_MODEL], BF16) for l in range(nl)]
        self.wb_out = [self.dram(f"wb_out{l}", [D_MODEL, D_MODEL], BF16) for l in range(nl)]
        self.wb_f1 = [self.dram(f"wb_f1{l}", [D_MODEL, 4096], BF16) for l in range(nl)]
        self.wb_f2 = [self.dram(f"wb_f2{l}", [4096, D_MODEL], BF16) for l in range(nl)]
        self.qkT = self.dram("qkT", [NQK, S], BF16, dk("qkT"))
        self.vtm = self.dram("vtm", [S, VC_N], BF16, dk("vtm"))
        self.gT = self.dram("gT", [4096, S], BF16, dk("gT"))
        if dbg.get("yT_in"):
            self.yT = di("yT", [4, 512, S], BF16)
        else:
            self.yT = self.dram("yT", [4, 512, S], BF16, dk("yT"))
        self.xres = self.dram("xres", [S, D_MODEL], F32, dk("xres"))
        self.rpbr = self.dram("rpbr", [120, 128], F32)
        for n in ("qkT", "vtm", "gT", "yT", "xres"):
            if dbg.get(n):
                self.outs.append(n)

        self.ident = self.T("ident", [128, 128], BF16)
        self.pm = self.T("pm", [128, 128], F32)
        self.identf = self.T("identf", [128, 128], F32)
        self.bones = self.T("bones", [128, 128], F32)
        self.ones_f = self.T("ones_f", [128, 128], F32)
        self.ones_b = self.T("ones_b", [128, 128], BF16)
        self.small = self.T("small", [128, 64], F32)
        for t, n in ((self.ident, "c_ident"), (self.pm, "c_pm"), (self.identf, "c_identf"), (self.bones, "c_bones"),
                     (self.ones_f, "c_ones_f"), (self.ones_b, "c_ones_b")):
            self.dma("sp", t[:], self.cd[n][:, :], [], ["const"])
        self.pbig = nc.alloc_psum_tensor("pbig", [128, 2048], F32)
        self.pb = [self.pbig[:, i * 512:(i + 1) * 512] for i in range(4)] + \
                  [nc.alloc_psum_tensor(f"pb{i}", [128, 512], F32) for i in range(4, 7)]
        self.pbt = nc.alloc_psum_tensor("pbt", [128, 1024], BF16)
        self.base_off = self.off
        self.S.barrier()

        if "W" in stages:
            self.phase_w()
        final = []
        for l in range(nl):
            self.layer_smalls(l)
            xsrc = self.x_in if l == 0 else self.xres
            if "P" in stages:
                self.phase_p(l, xsrc)
            if "D" in stages:
                self.mixer_d(l)
            if "A" in stages:
                self.mixer_a(l)
            if "C" in stages:
                self.mixer_c(l)
            if "B" in stages:
                self.mixer_b(l)
            if "M" in stages:
                final = self.phase_m(l, xsrc, last=(l == nl - 1))
        self.S.barrier()
        self.S.emit(final_waits=final)
        return nc

    def phase_reset(self):
        self.S.barrier()
        self.off = self.base_off

    def phase_w(self):
        self.phase_reset()
        nl = self.nl
        gcol = self.T("gcol", [128, 32], F32)
        for l in range(nl):
            self.dma("sp", gcol[:, l * 16:l * 16 + 8], self.norm_mix[l, :].rearrange("(c p) -> p c", p=128), [], ["gcol"], slow=True)
            self.dma("sp", gcol[:, l * 16 + 8:l * 16 + 16], self.norm_ffn[l, :].rearrange("(c p) -> p c", p=128), [], ["gcol"], slow=True)
        bi = [self.T(f"wci{i}", [128, 4096], F32) for i in range(2)]
        bo = [self.T(f"wco{i}", [128, 4096], BF16) for i in range(2)]
        n = 0
        for l in range(nl):
            jobs = [(self.w_in[l], self.wb_in[l], D_MODEL, IN_W, l * 16),
                    (self.w_branch[l], self.wb_br[l], 2048, D_MODEL, None),
                    (self.w_out[l], self.wb_out[l], D_MODEL, D_MODEL, None),
                    (self.w_ff1[l], self.wb_f1[l], D_MODEL, 4096, l * 16 + 8),
                    (self.w_ff2[l], self.wb_f2[l], 4096, D_MODEL, None)]
            for src, dst, R, C, gc in jobs:
                for rc in range(R // 128):
                    for c0 in range(0, C, 4096):
                        cw = min(4096, C - c0)
                        i = n % 2
                        n += 1
                        self.dma("sp", bi[i][:, 0:cw], src[rc * 128:(rc + 1) * 128, c0:c0 + cw], [], [f"wci{i}"])
                        eng = "dve" if n % 2 else "pool"
                        if gc is not None:
                            self.ts(eng, bo[i][:, 0:cw], bi[i][:, 0:cw], gcol[:, gc + rc:gc + rc + 1], ALU.mult,
                                    [f"wci{i}", "gcol"], [f"wco{i}"])
                        else:
                            self.cp(eng, bo[i][:, 0:cw], bi[i][:, 0:cw], [f"wci{i}"], [f"wco{i}"])
                        self.dma("pool", dst[rc * 128:(rc + 1) * 128, c0:c0 + cw], bo[i][:, 0:cw], [f"wco{i}"], [])

    def layer_smalls(self, l):
        self.phase_reset()
        sm = self.small
        lam_init = 0.8 - 0.6 * math.exp(-0.3 * l)
        lt = self.T("lamt", [128, 256], F32)
        pr = self.T("lampr", [128, 128], F32)
        sv = self.T("lamsv", [128, 4], F32)
        self.dma("sp", lt[:], self.diff_lambda[l:l + 1, :].partition_broadcast(128), [], ["lamt"])
        self.tt("dve", pr[:, 0:64], lt[:, 0:64], lt[:, 64:128], ALU.mult, ["lamt"], ["lampr"])
        self.tt("dve", pr[:, 64:128], lt[:, 128:192], lt[:, 192:256], ALU.mult, ["lamt"], ["lampr"])
        self.S.add("dve", lambda e: e.reduce_sum(out=sv[:, 0:2], in_=pr[:].rearrange("p (a b) -> p a b", a=2), axis=AX.X),
                   ["lampr"], ["lamsv"])
        self.act(sv[:, 2:4], sv[:, 0:2], AF.Exp, ["lamsv"], ["lamsv"])
        self.tt("dve", sm[:, 0:1], sv[:, 3:4], sv[:, 2:3], ALU.subtract, ["lamsv"], ["small"])
        self.ts("dve", sm[:, 0:1], sm[:, 0:1], -lam_init, ALU.add, ["small"], ["small"])
        self.dma("sp", sm[:, 1:2], self.diff_subln[l, :].rearrange("(p o) -> p o", o=1), [], ["small"])
        self.ts("dve", sm[:, 1:2], sm[:, 1:2], 1.0 - lam_init, ALU.mult, ["small"], ["small"])
        for i in range(2):
            for hf in range(2):
                self.dma("sp", sm[hf * 64:(hf + 1) * 64, 2 + i:3 + i], self.qk_norm[l, i, :].rearrange("(p o) -> p o", o=1), [], ["small"])
        self.dma("sp", sm[:, 8:40], self.b_gate[l, :].rearrange("(c p) -> p c", p=128), [], ["small"], slow=True)

    def phase_p(self, l, xsrc):
        self.phase_reset()
        S = self.Sq
        sm = self.small
        NT = S // 512
        xts = [self.T(f"xtP{i}", [128, 4, 1024], F32) for i in range(2)]
        hTs = [self.T(f"hTP{i}", [128, 8, 512], BF16) for i in range(2)]
        wts = [self.T(f"wtP{i}", [128, 8, 512], BF16) for i in range(3)]
        stg = [self.T(f"stgP{i}", [128, 512], BF16) for i in range(4)]
        vst = [self.T(f"vstP{i}", [128, 4, 512], BF16) for i in range(2)]
        rope = [self.T(f"ropeP{i}", [128, 2, 512], F32) for i in range(2)]
        dtmp = [self.T(f"dtmpP{i}", [128, 512], F32) for i in range(5)]
        plan = []
        for c0 in range(0, OFF_G, 512):
            if c0 in (OFF_AV, OFF_B + 1024, OFF_B + 1536 + 1024, OFF_B + 3072 + 1024, OFF_CV):
                vc = {OFF_AV: VC_A, OFF_B + 1024: VC_B, OFF_B + 2560: VC_B + 512, OFF_B + 4096: VC_B + 1024, OFF_CV: VC_C}[c0]
                plan.append((c0, 512, "tm", vc))
            elif c0 == OFF_DQ:
                plan.append((c0, 512, "dq", None))
            elif c0 == OFF_DK:
                plan.append((c0, 256, "dkv", None))
            else:
                plan.append((c0, 512, "fm", None))
        for c0 in range(OFF_G, IN_W, 512):
            plan.append((c0, 512, "gate", None))
        wn = 0
        sn = 0
        pbn = 0
        for tt_ in range(NT):
            t0 = tt_ * 512
            xt = xts[tt_ % 2]
            hT = hTs[tt_ % 2]
            xk, hk = f"xtP{tt_ % 2}", f"hTP{tt_ % 2}"
            rp = rope[tt_ % 2]
            rk = f"ropeP{tt_ % 2}"
            self.dma("sp", xt[:], xsrc[t0:t0 + 512, :].rearrange("(s p) d -> p s d", p=128), [], [xk])
            self.dma("sp", rp[:], self.cd["c_rope"][:, :, t0:t0 + 512].rearrange("a p t -> p a t"), [], [rk])
            mark = self.off
            self.rmsnorm_T_keys(xt, hT, xk, hk)
            self.off = mark
            for (c0, ncol, kind, vc) in plan:
                w = wts[wn % 3]
                wk = f"wtP{wn % 3}"
                wn += 1
                self.dma("sp", w[:, :, 0:ncol], self.wb_in[l][:, c0:c0 + ncol].rearrange("(k p) n -> p k n", p=128), [], [wk])
                if kind in ("fm", "gate", "dq", "dkv"):
                    nfm = {"fm": 4, "gate": 4, "dq": 4, "dkv": 1}[kind]
                    for j in range(nfm):
                        bank = self.pb[pbn % 4]
                        bk = f"pb{pbn % 4}"
                        pbn += 1
                        for kc in range(8):
                            self.mm(bank[:], w[:, kc, j * 128:(j + 1) * 128], hT[:, kc, :], kc == 0, kc == 7, [wk, hk], [bk])
                        st = stg[sn % 4]
                        sk = f"stgP{sn % 4}"
                        sn += 1
                        col = c0 + j * 128
                        if kind == "fm":
                            self.cp("act" if sn % 2 else "dve", st[:], bank[:], [bk], [sk])
                            self.dma("pool", self.qkT[col:col + 128, t0:t0 + 512], st[:], [sk], [])
                        elif kind == "gate":
                            gc = (col - OFF_G) // 128
                            self.act(st[:], bank[:], AF.Sigmoid, [bk, "small"], [sk], bias=sm[:, 8 + gc:9 + gc])
                            self.dma("pool", self.gT[col - OFF_G:col - OFF_G + 128, t0:t0 + 512], st[:], [sk], [])
                        else:
                            gi = 2 if kind == "dq" else 3
                            sq, rst, xg, t1, t2 = dtmp
                            self.act(sq[:], bank[:], AF.Square, [bk], ["dsq"])
                            self.mm(self.pb[4][:], self.bones[:], sq[:], True, True, ["dsq", "const"], ["pb4"])
                            self.ts("dve", rst[:], self.pb[4][:], 1.0 / 64, ALU.mult, ["pb4"], ["drst"], s2=EPS, op1=ALU.add)
                            self.act(rst[:], rst[:], AF.Sqrt, ["drst"], ["drst"])
                            self.recip(rst[:], rst[:], ["drst"], ["drst"])
                            self.ts("dve", xg[:], bank[:], sm[:, gi:gi + 1], ALU.mult, [bk, "small"], ["dxg"])
                            self.mm(self.pb[5][:], self.pm[:], xg[:], True, True, ["dxg", "const"], ["pb5"])
                            self.tt("pool", t1[:], xg[:], rp[:, 0, :], ALU.mult, ["dxg", rk], ["dt1"])
                            self.tt("dve", t2[:], self.pb[5][:], rp[:, 1, :], ALU.mult, ["pb5", rk], ["dt2"])
                            self.tt("pool", t1[:], t1[:], t2[:], ALU.add, ["dt1", "dt2"], ["dt1"])
                            self.tt("dve", st[:], t1[:], rst[:], ALU.mult, ["dt1", "drst"], [sk])
                            self.dma("pool", self.qkT[col:col + 128, t0:t0 + 512], st[:], [sk], [])
                if kind in ("tm", "dkv"):
                    if kind == "tm":
                        wc0, nvc, vcol = 0, 512, vc
                    else:
                        wc0, nvc, vcol = 128, 128, VC_D
                    vs = vst[wn % 2]
                    vk = f"vstP{wn % 2}"
                    for s in range(4):
                        bank = self.pb[pbn % 4]
                        bk = f"pb{pbn % 4}"
                        pbn += 1
                        for kc in range(8):
                            self.mm(bank[:, 0:nvc], hT[:, kc, s * 128:(s + 1) * 128], w[:, kc, wc0:wc0 + nvc], kc == 0, kc == 7, [wk, hk], [bk])
                        self.cp("act" if s % 2 else "dve", vs[:, s, 0:nvc], bank[:, 0:nvc], [bk], [vk])
                    self.dma("pool", self.vtm[t0:t0 + 512, vcol:vcol + nvc].rearrange("(s p) c -> p s c", p=128),
                             vs[:, :, 0:nvc], [vk], [])

    def rmsnorm_T_keys(self, xt, hT, xk, hk, inv_d=1.0 / D_MODEL):
        ss = self.T("ss", [128, 8], F32)
        junk = self.T("junk", [128, 1024], BF16)
        hn = self.T("hn", [128, 4, 1024], BF16)
        self.memset("dve", ss[:], 0.0, ["ss"])
        for s in range(4):
            self.act(junk[:], xt[:, s, :], AF.Square, [xk, "ss"], ["junk", "ss"], accum=ss[:, s:s + 1])
        self.ts("dve", ss[:, 4:8], ss[:, 0:4], inv_d, ALU.mult, ["ss"], ["ss"], s2=EPS, op1=ALU.add)
        self.act(ss[:, 4:8], ss[:, 4:8], AF.Sqrt, ["ss"], ["ss"])
        self.recip(ss[:, 4:8], ss[:, 4:8], ["ss"], ["ss"])
        for s in range(4):
            self.ts("dve" if s % 2 else "pool", hn[:, s, :], xt[:, s, :], ss[:, 4 + s:5 + s], ALU.mult,
                    [xk, "ss"], [f"hn{s}"])
        for c in range(8):
            for s in range(4):
                self.tr(self.pbt[:, s * 128:(s + 1) * 128], hn[:, s, c * 128:(c + 1) * 128], self.ident[:],
                        [f"hn{s}", "const"], ["pbt"])
            self.cp("act" if c % 2 else "dve", hT[:, c, :], self.pbt[:, 0:512], ["pbt"], [hk])

    def phase_m(self, l, xsrc, last):
        self.phase_reset()
        S = self.Sq
        NT = S // 512
        xts = [self.T(f"xtM{i}", [128, 4, 1024], F32) for i in range(2)]
        yts = [self.T(f"ytM{i}", [128, 4, 512], BF16) for i in range(2)]
        gts = [self.T(f"gtM{i}", [128, 8, 512], BF16) for i in range(2)]
        wts = [self.T(f"wtM{i}", [128, 8, 512], BF16) for i in range(3)]
        macc = self.T("macc", [128, 8, 512], F32)
        mtmp = [self.T(f"mtmp{i}", [128, 512], F32) for i in range(2)]
        mT = self.T("mT", [128, 8, 512], BF16)
        h2T = self.T("h2T", [128, 8, 512], BF16)
        uT = self.T("uT", [128, 32, 512], BF16)
        gfin = None
        if last:
            gfin = self.T("gfin", [128, 1024], F32)
            self.dma("sp", gfin[:], self.norm_final[0:1, :].partition_broadcast(128), [], ["gfin"])
        wn = 0
        pbn = 0
        yn = 0
        finals = []
        for tt_ in range(NT):
            t0 = tt_ * 512
            xt = xts[tt_ % 2]
            xk = f"xtM{tt_ % 2}"
            self.dma("sp", xt[:], xsrc[t0:t0 + 512, :].rearrange("(s p) d -> p s d", p=128), [], [xk])
            for n in range(4):
                yt = yts[yn % 2]
                gt = gts[yn % 2]
                yk, gk = f"ytM{yn % 2}", f"gtM{yn % 2}"
                yn += 1
                w = wts[wn % 3]
                wk = f"wtM{wn % 3}"
                wn += 1
                self.dma("sp", yt[:], self.yT[n, :, t0:t0 + 512].rearrange("(k p) t -> p k t", p=128), [], [yk])
                self.dma("sp", gt[:], self.gT[n * 1024:(n + 1) * 1024, t0:t0 + 512].rearrange("(k p) t -> p k t", p=128), [], [gk])
                wv = w[:].rearrange("p k n -> p (k n)").rearrange("p (k n) -> p k n", k=4)
                self.dma("sp", wv, self.wb_br[l][n * 512:(n + 1) * 512, :].rearrange("(k p) n -> p k n", p=128), [], [wk])
                for oc in range(8):
                    bank = self.pb[pbn % 3]
                    bk = f"pb{pbn % 3}"
                    pbn += 1
                    for kc in range(4):
                        self.mm(bank[:], wv[:, kc, oc * 128:(oc + 1) * 128], yt[:, kc, :], kc == 0, kc == 3, [wk, yk], [bk])
                    eng = "dve" if oc % 2 else "pool"
                    if n == 0:
                        self.tt("dve", macc[:, oc, :], bank[:], gt[:, oc, :], ALU.mult, [bk, gk], [f"macc{oc}"])
                    elif n < 3:
                        mt = mtmp[oc % 2]
                        self.tt("dve", mt[:], bank[:], gt[:, oc, :], ALU.mult, [bk, gk], [f"mtmp{oc % 2}"])
                        self.tt("pool", macc[:, oc, :], macc[:, oc, :], mt[:], ALU.add, [f"mtmp{oc % 2}", f"macc{oc}"], [f"macc{oc}"])
                    else:
                        mt = mtmp[oc % 2]
                        self.tt("dve", mt[:], bank[:], gt[:, oc, :], ALU.mult, [bk, gk], [f"mtmp{oc % 2}"])
                        self.tt("pool", mT[:, oc, :], macc[:, oc, :], mt[:], ALU.add, [f"mtmp{oc % 2}", f"macc{oc}"], [f"mT{oc}"])
            mTk = [f"mT{oc}" for oc in range(8)]
            for half in range(2):
                w = wts[wn % 3]
                wk = f"wtM{wn % 3}"
                wn += 1
                self.dma("sp", w[:], self.wb_out[l][:, half * 512:(half + 1) * 512].rearrange("(k p) n -> p k n", p=128), [], [wk])
                for s in range(4):
                    bank = self.pb[3 + pbn % 2]
                    bk = f"pb{3 + pbn % 2}"
                    pbn += 1
                    for kc in range(8):
                        self.mm(bank[:], mT[:, kc, s * 128:(s + 1) * 128], w[:, kc, :], kc == 0, kc == 7, [wk, f"mT{kc}"], [bk])
                    self.tt("dve", xt[:, s, half * 512:(half + 1) * 512], xt[:, s, half * 512:(half + 1) * 512], bank[:], ALU.add,
                            [bk, xk], [xk])
            mark = self.off
            self.rmsnorm_T_keys(xt, h2T, xk, "h2T")
            self.off = mark
            for ft in range(8):
                w = wts[wn % 3]
                wk = f"wtM{wn % 3}"
                wn += 1
                self.dma("sp", w[:], self.wb_f1[l][:, ft * 512:(ft + 1) * 512].rearrange("(k p) n -> p k n", p=128), [], [wk])
                for j in range(4):
                    fc = ft * 4 + j
                    bank = self.pb[pbn % 3]
                    bk = f"pb{pbn % 3}"
                    pbn += 1
                    for kc in range(8):
                        self.mm(bank[:], w[:, kc, j * 128:(j + 1) * 128], h2T[:, kc, :], kc == 0, kc == 7, [wk, "h2T"], [bk])
                    rl = mtmp[fc % 2]
                    self.act(rl[:], bank[:], AF.Relu, [bk], [f"mtmp{fc % 2}"])
                    self.tt("dve" if fc % 2 else "pool", uT[:, fc, :], rl[:], rl[:], ALU.mult, [f"mtmp{fc % 2}"], [f"uT{fc}"])
            for half in range(2):
                banks = [(self.pb[3 + s], f"pb{3 + s}") for s in range(4)]
                for fg in range(4):
                    w = wts[wn % 3]
                    wk = f"wtM{wn % 3}"
                    wn += 1
                    self.dma("sp", w[:], self.wb_f2[l][fg * 1024:(fg + 1) * 1024, half * 512:(half + 1) * 512].rearrange("(k p) n -> p k n", p=128),
                             [], [wk])
                    for s in range(4):
                        bank, bk = banks[s]
                        for kc in range(8):
                            fc = fg * 8 + kc
                            self.mm(bank[:], uT[:, fc, s * 128:(s + 1) * 128], w[:, kc, :], fg == 0 and kc == 0, fg == 3 and kc == 7,
                                    [wk, f"uT{fc}"], [bk])
                for s in range(4):
                    bank, bk = banks[s]
                    self.tt("dve", xt[:, s, half * 512:(half + 1) * 512], xt[:, s, half * 512:(half + 1) * 512], bank[:], ALU.add,
                            [bk, xk], [xk])
            if not last:
                self.dma("pool", self.xres[t0:t0 + 512, :].rearrange("(s p) d -> p s d", p=128), xt[:], [xk], [])
            else:
                ss = self.T("ssF", [128, 8], F32)
                junk = self.T("junkF", [128, 1024], BF16)
                self.memset("dve", ss[:], 0.0, ["ssF"])
                for s in range(4):
                    self.act(junk[:], xt[:, s, :], AF.Square, [xk, "ssF"], ["junkF", "ssF"], accum=ss[:, s:s + 1])
                self.ts("dve", ss[:, 4:8], ss[:, 0:4], 1.0 / D_MODEL, ALU.mult, ["ssF"], ["ssF"], s2=EPS, op1=ALU.add)
                self.act(ss[:, 4:8], ss[:, 4:8], AF.Sqrt, ["ssF"], ["ssF"])
                self.recip(ss[:, 4:8], ss[:, 4:8], ["ssF"], ["ssF"])
                for s in range(4):
                    self.stt("dve", xt[:, s, :], xt[:, s, :], ss[:, 4 + s:5 + s], gfin[:], ALU.mult, ALU.mult,
                             [xk, "ssF", "gfin"], [xk])
                finals.append(self.dma("pool", self.out[t0:t0 + 512, :].rearrange("(s p) d -> p s d", p=128), xt[:], [xk], []))
                self.off = mark
        return finals

    def mixer_d(self, l):
        self.phase_reset()
        S = self.Sq
        NKB = S // 128
        NQT = S // 512
        ND = NKB // 2
        LA = 1
        kT = self.T("dkT", [128, S], BF16)
        va = self.T("dva", [128, NKB, 128], BF16)
        qts = [self.T(f"dq{i}", [128, 512], BF16) for i in range(3)]
        self.memset("dve", kT[64:128, :], 0.0, ["dkT"])
        for i in range(3):
            self.memset("pool", qts[i][64:128, :], 0.0, [f"dq{i}"])
        pts = [self.T(f"dpt{i}", [128, 1024], BF16) for i in range(3)]
        rec = [self.T(f"drec{i}", [128, 512], F32) for i in range(2)]
        yst = [self.T(f"dyst{i}", [64, 512], BF16) for i in range(2)]
        self.memset("pool", va[:, :, 64:128], 1.0, ["dva"])
        gq = 0
        for g in range(2):
            self.dma("sp", kT[0:64, :], self.qkT[OFF_DK + g * 64:OFF_DK + (g + 1) * 64, :], [], ["dkT"])
            self.dma("sp", va[:, :, 0:64], self.vtm[:, VC_D + g * 64:VC_D + (g + 1) * 64].rearrange("(k p) c -> p k c", p=128), [], ["dva"])
            items = [(h, qt, kd) for h in range(4 * g, 4 * g + 4) for qt in range(NQT) for kd in range(ND)]
            n = len(items)

            def qk(i, gq=gq):
                h, qt, kd = items[i]
                qi = gq + i // ND
                q = qts[qi % 3]
                qk_ = f"dq{qi % 3}"
                if kd == 0:
                    self.dma("sp", q[0:64, :], self.qkT[OFF_DQ + h * 64:OFF_DQ + (h + 1) * 64, qt * 512:(qt + 1) * 512], [], [qk_])
                d = i % 2
                for u in range(2):
                    kb = 2 * kd + u
                    self.mm(self.pb[2 * d + u][:], kT[:, kb * 128:(kb + 1) * 128], q[:], True, True, ["dkT", qk_], [f"pb{2 * d + u}"])

            def ex_av(i, gq=gq):
                h, qt, kd = items[i]
                qi = gq + i // ND
                acc = self.pb[4 + qi % 2]
                ak = f"pb{4 + qi % 2}"
                pt = pts[i % 3]
                pk = f"dpt{i % 3}"
                d = i % 2
                self.act(pt[:], self.pbig[:, d * 1024:(d + 1) * 1024], AF.Exp, [f"pb{2 * d}", f"pb{2 * d + 1}"], [pk], scale=0.125)
                for u in range(2):
                    kb = 2 * kd + u
                    self.mm(acc[:], va[:, kb, :], pt[:, u * 512:(u + 1) * 512], kb == 0, kb == NKB - 1, ["dva", pk], [ak])
                if kd == ND - 1:
                    ys = yst[qi % 2]
                    yk = f"dyst{qi % 2}"
                    rc = rec[qi % 2]
                    rk = f"drec{qi % 2}"
                    self.recip(rc[64:128, :], acc[64:128, :], [ak], [rk])
                    self.tt("dve", ys[:], acc[0:64, :], rc[64:128, :], ALU.mult, [ak, rk], [yk])
                    self.dma("pool", self.yT[3, h * 64:(h + 1) * 64, qt * 512:(qt + 1) * 512], ys[:], [yk], [])

            for i in range(n + LA):
                if i < n:
                    qk(i)
                if i >= LA:
                    ex_av(i - LA)
            gq += n // ND

    def mixer_a(self, l):
        self.phase_reset()
        S = self.Sq
        sm = self.small
        NKB = S // 128
        NQT = S // 512
        ND = NKB // 2
        LA = 1
        KR = 128
        kTs = [self.T(f"akT{c}", [128, S], BF16) for c in range(2)]
        va = self.T("ava", [128, NKB, 128], BF16)
        qts = [[[self.T(f"aq{i}{c}{v}", [128, 512], BF16) for v in range(2)] for c in range(2)] for i in range(2)]
        pts = [self.T(f"apt{i}", [128, 1024], BF16) for i in range(3)]
        for c in range(2):
            self.memset("dve", kTs[c][64:128, :], 0.0, [f"akT{c}"])
            for i in range(2):
                for v in range(2):
                    self.memset("pool", qts[i][c][v][64:128, :], 0.0, [f"aq{i}{c}{v}"])
        cdt = self.T("acd", [128, 4, 512], F32)
        dgt = [self.T(f"adg{i}", [128, 1024], F32) for i in range(2)]
        r0 = [self.T(f"ar0{i}", [128, 512], F32) for i in range(2)]
        t0s = [self.T(f"at0{i}", [128, 512], F32) for i in range(2)]
        t1_ = self.T("at1", [128, 512], F32)
        sq = self.T("asq", [128, 512], F32)
        rs = self.T("ars", [128, 512], F32)
        yst = [self.T(f"ayst{i}", [128, 512], BF16) for i in range(2)]
        pssum = self.pbt.bitcast(F32)
        gq = 0
        for h in range(4):
            for c in range(2):
                r = OFF_AK + (h * 2 + c) * 64
                self.dma("sp", kTs[c][0:64, :], self.qkT[r:r + 64, :], [], [f"akT{c}"])
                self.dma("sp", kTs[c][64:69, :], self.cd["c_kaug"][h, :, :], [], [f"akT{c}"])
            self.dma("sp", va[:], self.vtm[:, VC_A + h * 128:VC_A + (h + 1) * 128].rearrange("(k p) c -> p k c", p=128), [], ["ava"])
            self.dma("sp", cdt[:], self.cd["c_cd"][:, h, :, :], [], ["acd"])
            items = [(qt, c, kd) for qt in range(NQT) for c in range(2) for kd in range(ND)]
            n = len(items)

            def qk(i, h=h, gq=gq):
                qt, c, kd = items[i]
                qi = (gq + qt) % 2
                if c == 0 and kd == 0:
                    for c2 in range(2):
                        r = OFF_AQ + (h * 2 + c2) * 64
                        for v in range(2):
                            self.dma("sp", qts[qi][c2][v][0:64, :], self.qkT[r:r + 64, qt * 512:(qt + 1) * 512], [], [f"aq{qi}{c2}{v}"])
                            self.dma("sp", qts[qi][c2][v][64:69, :], self.cd["c_qaug"][h, v, :, qt * 512:(qt + 1) * 512], [], [f"aq{qi}{c2}{v}"])
                d = i % 2
                for u in range(2):
                    kb = 2 * kd + u
                    rel = kb - qt * 4
                    v = 1 if rel > 3 else 0
                    self.mm(self.pb[2 * d + u][:], kTs[c][0:KR, kb * 128:(kb + 1) * 128], qts[qi][c][v][0:KR, :], True, True,
                            [f"akT{c}", f"aq{qi}{c}{v}"], [f"pb{2 * d + u}"])

            def ex_av(i, h=h, gq=gq):
                qt, c, kd = items[i]
                gi = gq * 2 + i // ND
                acc, ak = self.pb[4], "pb4"
                den, dk_ = self.pb[5], "pb5"
                d = i % 2
                sb2 = self.pbig[:, d * 1024:(d + 1) * 1024]
                sks = [f"pb{2 * d}", f"pb{2 * d + 1}"]
                pt, pk = pts[i % 3], f"apt{i % 3}"
                rel = 2 * kd - qt * 4
                if rel < 0 or rel > 3 or os.environ.get("ANODIAG"):
                    self.act(pt[:], sb2, AF.Exp, sks, [pk], scale=0.125)
                else:
                    dg, dgk = dgt[(i // 2) % 2], f"adg{(i // 2) % 2}"
                    self.ts("dve", dg[:], sb2, 0.125, ALU.mult, sks, [dgk])
                    self.tt("dve", dg[:], dg[:], cdt[:, rel:rel + 2, :].rearrange("p a b -> p (a b)"), ALU.add, [dgk, "acd"], [dgk])
                    self.act(pt[:], dg[:], AF.Exp, [dgk], [pk])
                for u in range(2):
                    kb = 2 * kd + u
                    self.mm(acc[:], va[:, kb, :], pt[:, u * 512:(u + 1) * 512], kb == 0, kb == NKB - 1, ["ava", pk], [ak])
                    if not (os.environ.get("ANODEN") and kb not in (0, NKB - 1)):
                        self.mm(den[:], self.ones_b[:], pt[:, u * 512:(u + 1) * 512], kb == 0, kb == NKB - 1, ["const", pk], [dk_])
                if kd != ND - 1:
                    return
                qg = gq + qt
                t0_, t0k = t0s[qg % 2], f"at0{qg % 2}"
                rr, rrk = r0[gi % 2], f"ar0{gi % 2}"
                self.recip(rr[:], den[:], [dk_], [rrk])
                if c == 0:
                    self.tt("dve", t0_[:], acc[:], rr[:], ALU.mult, [ak, rrk], [t0k])
                    return
                self.tt("dve", t1_[:], acc[:], rr[:], ALU.mult, [ak, rrk], ["at1"])
                self.stt("dve", t0_[:], t1_[:], sm[:, 0:1], t0_[:], ALU.mult, ALU.add, [t0k, "at1", "small"], [t0k])
                self.act(sq[:], t0_[:], AF.Square, [t0k], ["asq"])
                self.mm(pssum[:], self.ones_f[:], sq[:], True, True, ["asq", "const"], ["pbt"])
                self.ts("dve", rs[:], pssum[:], 1.0 / 128, ALU.mult, ["pbt"], ["ars"], s2=EPS, op1=ALU.add)
                self.act(rs[:], rs[:], AF.Sqrt, ["ars"], ["ars"])
                self.recip(rs[:], rs[:], ["ars"], ["ars"])
                ys, yk = yst[qg % 2], f"ayst{qg % 2}"
                self.stt("dve", ys[:], t0_[:], sm[:, 1:2], rs[:], ALU.mult, ALU.mult, [t0k, "ars", "small"], [yk])
                self.dma("pool", self.yT[0, h * 128:(h + 1) * 128, qt * 512:(qt + 1) * 512], ys[:], [yk], [])

            for i in range(n + LA):
                if i < n:
                    qk(i)
                if i >= LA:
                    ex_av(i - LA)
            gq += NQT

    def local_groups(self, qT, kT, va, keys, groups, mask_of, out_fn, tagp):
        pts = self._lp
        tmp = self._lt
        for gi, (qbs, rels) in enumerate(groups):
            nq = len(qbs)
            acc = self.pb[4 + gi % 2]
            ak = f"pb{4 + gi % 2}"
            used = []
            for ri, (r, kbs, m) in enumerate(rels):
                js = [j for j in range(nq) if kbs[j] is not None]
                if not js:
                    continue
                j0, j1 = js[0], js[-1] + 1
                assert js == list(range(j0, j1))
                sb = self.pb[self._sn % 4]
                sk = f"pb{self._sn % 4}"
                self._sn += 1
                for j in js:
                    self.mm(sb[:, j * 128:(j + 1) * 128], kT[:, kbs[j] * 128:(kbs[j] + 1) * 128], qT[:, qbs[j] * 128:(qbs[j] + 1) * 128],
                            True, True, keys, [sk])
                t = tmp[self._pn % 2]
                tk = f"{tagp}lt{self._pn % 2}"
                self._pn += 1
                pi = (gi % 2) * 9 + ri
                pt = pts[pi]
                pk = f"{tagp}lp{pi}"
                nj = j1 - j0
                mv = m.unsqueeze(1).broadcast_to([128, nj, 128]) if nj > 1 else m
                tv = t[:, j0 * 128:j1 * 128].rearrange("p (j q) -> p j q", j=nj) if nj > 1 else t[:, j0 * 128:j1 * 128]
                sv = sb[:, j0 * 128:j1 * 128].rearrange("p (j q) -> p j q", j=nj) if nj > 1 else sb[:, j0 * 128:j1 * 128]
                self.ts("dve", tv, sv, 0.125, ALU.mult, [sk], [tk])
                self.tt("dve", tv, tv, mv, ALU.add, [tk, "lmask"], [tk])
                self.act(pt[:, j0 * 128:j1 * 128], t[:, j0 * 128:j1 * 128], AF.Exp, [tk], [pk])
                used.append((ri, kbs, pt, pk))
            for j in range(nq):
                mine = [(ri, kbs, pt, pk) for (ri, kbs, pt, pk) in used if kbs[j] is not None]
                for n_, (ri, kbs, pt, pk) in enumerate(mine):
                    self.mm(acc[:, j * 128:(j + 1) * 128], va[:, kbs[j], :], pt[:, j * 128:(j + 1) * 128],
                            n_ == 0, n_ == len(mine) - 1, keys + [pk], [ak])
            out_fn(gi, qbs, acc, ak)

    def local_init(self, tagp):
        self._lp = [self.T(f"{tagp}lp{i}", [128, 512], BF16) for i in range(18)]
        self._lt = [self.T(f"{tagp}lt{i}", [128, 512], F32) for i in range(2)]
        self._sn = 0
        self._pn = 0

    def mixer_c(self, l):
        self.phase_reset()
        S = self.Sq
        M = S // 128
        meta = self.cmeta
        NV = len(meta)
        qT = self.T("cqT", [128, S], BF16)
        kT = self.T("ckT", [128, S], BF16)
        self.memset("dve", qT[64:128, :], 0.0, ["cqT"])
        self.memset("pool", kT[64:128, :], 0.0, ["ckT"])
        va = self.T("cva", [128, M, 128], BF16)
        mcv = self.T("cmcv", [128, NV, 128], F32)
        mcm = self.T("cmcm", [128, NV, 128], F32)
        TT = self.T("cTT", [128, 15, 64], F32)
        rp = self.T("crp", [120, 128], F32)
        rec = self.T("crec", [128, 512], F32)
        yst = [self.T(f"cyst{i}", [64, 512], BF16) for i in range(2)]
        self.local_init("c")
        self.memset("pool", va[:, :, 64:128], 1.0, ["cva"])
        self.dma("sp", mcv[:], self.cd["c_mcv"][:, :, :], [], ["cmcv"])
        self.memset("dve", rp[:], 0.0, ["crp"])
        self.dma("sp", rp[:, 48:79], self.na_rpb[l, :, :], [], ["crp"])
        self.dma("sp", self.rpbr[:, :], rp[:], ["crp"], ["rpbr"])
        Hd = self.T("cHd", [64, 15, 2, 64], F32)
        j64 = self.T("cj64", [64, 64], F32)
        self.dma("sp", j64[:], self.cd["c_j64"][:, :], [], ["cj64"])
        for h in range(8):
            self.dma("sp", qT[0:64, :], self.qkT[OFF_CQ + h * 64:OFF_CQ + (h + 1) * 64, :], [], ["cqT"])
            self.dma("sp", kT[0:64, :], self.qkT[OFF_CK + h * 64:OFF_CK + (h + 1) * 64, :], [], ["ckT"])
            self.dma("sp", va[:, :, 0:64], self.vtm[:, VC_C + h * 64:VC_C + (h + 1) * 64].rearrange("(k p) c -> p k c", p=128), [], ["cva"])
            for a in range(2):
                src = bass.AP(tensor=self.rpbr.tensor, offset=h * 15 * 128, ap=[[1, 64], [128, 15], [1, 64]])
                self.dma("sp", Hd[:, :, a, :], src, ["rpbr"], ["cHd"])
            for i in range(15):
                bank = self.pb[4 + i // 8]
                bk = f"pb{4 + i // 8}"
                self.mm(bank[:, (i % 8) * 64:(i % 8 + 1) * 64], Hd[:, i, :, :].rearrange("p a k -> p (a k)"), j64[:], True, True,
                        ["cHd", "cj64"], [bk])
            self.cp("dve", TT[:, 0:8, :], self.pb[4][:].rearrange("p (i c) -> p i c", i=8), ["pb4"], ["cTT"])
            self.cp("dve", TT[:, 8:15, :], self.pb[5][:, 0:448].rearrange("p (i c) -> p i c", i=7), ["pb5"], ["cTT"])
            for vi, (cls, dr) in enumerate(meta):
                for a in range(2):
                    for b in range(2):
                        ri = 2 * dr + a - b + 7
                        o = mcm[a * 64:(a + 1) * 64, vi, b * 64:(b + 1) * 64]
                        i0 = mcv[a * 64:(a + 1) * 64, vi, b * 64:(b + 1) * 64]
                        if 0 <= ri <= 14:
                            self.tt("pool", o, i0, TT[a * 64:(a + 1) * 64, ri, :], ALU.add, ["cmcv", "cTT"], ["lmask"])
                        else:
                            self.cp("pool", o, i0, ["cmcv"], ["lmask"])
            midx = {(cls, dr): vi for vi, (cls, dr) in enumerate(meta)}
            groups = []

            def mk(qbs, cls):
                rels = []
                for dr in range(-4, 5):
                    if (cls, dr) not in midx:
                        continue
                    kbs = [qb + dr if 0 <= qb + dr < M else None for qb in qbs]
                    rels.append((dr, kbs, mcm[:, midx[(cls, dr)], :]))
                return (qbs, rels)
            groups.append(mk([0], "m0"))
            groups.append(mk([1], "m1"))
            ints = list(range(2, M - 2))
            for i in range(0, len(ints), 4):
                groups.append(mk(ints[i:i + 4], "int"))
            groups.append(mk([M - 2], "mL2"))
            groups.append(mk([M - 1], "mL1"))

            def outf(gi, qbs, acc, ak, h=h):
                n = len(qbs) * 128
                ys = yst[gi % 2]
                yk = f"cyst{gi % 2}"
                self.recip(rec[64:128, 0:n], acc[64:128, 0:n], [ak], ["crec"])
                self.tt("dve", ys[:, 0:n], acc[0:64, 0:n], rec[64:128, 0:n], ALU.mult, [ak, "crec"], [yk])
                self.dma("pool", self.yT[2, h * 64:(h + 1) * 64, qbs[0] * 128:qbs[0] * 128 + n], ys[:, 0:n], [yk], [])
            import os
            if os.environ.get("CDBG", "0") != "1":
                self.local_groups(qT, kT, va, ["cqT", "ckT", "cva"], groups, None, outf, "c")

    def mixer_b(self, l):
        self.phase_reset()
        S = self.Sq
        NB = S // 128
        qn_ = self.T("bqn", [128, S], BF16)
        kn_ = self.T("bkn", [128, S], BF16)
        qp = self.T("bqp", [128, S], BF16)
        kp = self.T("bkp", [128, S], BF16)
        for t_, k_ in ((qn_, "bqn"), (kn_, "bkn"), (qp, "bqp"), (kp, "bkp")):
            self.memset("dve", t_[64:128, :], 0.0, [k_])
        va = self.T("bva", [128, NB, 128], BF16)
        accB = self.T("baccB", [128, S], F32)
        mk_ = self.T("bmk", [128, 3, 128], F32)
        rec = self.T("brec", [128, 512], F32)
        yst = [self.T(f"byst{i}", [64, 512], BF16) for i in range(2)]
        self.local_init("b")
        self.memset("pool", va[:, :, 64:128], 1.0, ["bva"])
        for h in range(8):
            for g, (win, dil) in enumerate(B_PATTERNS):
                L = S // dil
                nbc = L // 128
                base = OFF_B + g * 1536
                self.dma("sp", qn_[0:64, :], self.qkT[base + h * 64:base + (h + 1) * 64, :], [], ["bqn"])
                self.dma("sp", kn_[0:64, :], self.qkT[base + 512 + h * 64:base + 512 + (h + 1) * 64, :], [], ["bkn"])
                self.dma("sp", mk_[:], self.cd["c_mb"][g * 8 + h, :, :].rearrange("p (r q) -> p r q", r=3), [], ["lmask"])
                vcol = VC_B + g * 512 + h * 64
                if dil == 1:
                    qq, kk = qn_, kn_
                    keys = ["bqn", "bkn", "bva"]
                    self.dma("sp", va[:, :, 0:64], self.vtm[:, vcol:vcol + 64].rearrange("(k p) c -> p k c", p=128), [], ["bva"])
                else:
                    qq, kk = qp, kp
                    keys = ["bqp", "bkp", "bva"]
                    self.cp("pool", qp[0:64, :].rearrange("p (j i) -> p j i", j=dil), qn_[0:64, :].rearrange("p (i j) -> p j i", j=dil), ["bqn"], ["bqp"])
                    self.cp("dve", kp[0:64, :].rearrange("p (j i) -> p j i", j=dil), kn_[0:64, :].rearrange("p (i j) -> p j i", j=dil), ["bkn"], ["bkp"])
                    for j in range(dil):
                        src = self.vtm[:, vcol:vcol + 64].rearrange("(k p j) c -> j p k c", p=128, j=dil)[j]
                        self.dma("sp", va[:, j * nbc:(j + 1) * nbc, 0:64], src, [], ["bva"])
                groups = []
                gs = min(4, nbc)
                for j in range(dil):
                    for b0 in range(0, nbc, gs):
                        qbs = [j * nbc + b0 + i for i in range(gs)]
                        rels = []
                        for r in (-1, 0, 1):
                            kbs = [(qb + r) if 0 <= (qb - j * nbc + r) < nbc else None for qb in qbs]
                            rels.append((r, kbs, mk_[:, r + 1, :]))
                        groups.append((qbs, rels))

                def outf(gi, qbs, acc, ak, g=g, dil=dil, nbc=nbc):
                    n = len(qbs) * 128
                    j = qbs[0] // nbc
                    i0 = (qbs[0] - j * nbc) * 128
                    if dil == 1:
                        dst = accB[:, i0:i0 + n]
                        self.cp("dve", dst, acc[:, 0:n], [ak], ["baccB"])
                    else:
                        dst = accB[:, i0 * dil + j:(i0 + n - 1) * dil + j + 1:dil]
                        self.tt("dve", dst, dst, acc[:, 0:n], ALU.add, [ak, "baccB"], ["baccB"])
                self.local_groups(qq, kk, va, keys, groups, None, outf, "b")
            for ct in range(S // 512):
                ys = yst[ct % 2]
                yk = f"byst{ct % 2}"
                self.recip(rec[64:128, :], accB[64:128, ct * 512:(ct + 1) * 512], ["baccB"], ["brec"])
                self.cp_psum_num(accB, ct)
                self.tt("dve", ys[:], self.pb[6][0:64, :], rec[64:128, :], ALU.mult, ["pb6", "brec"], [yk])
                self.dma("pool", self.yT[1, h * 64:(h + 1) * 64, ct * 512:(ct + 1) * 512], ys[:], [yk], [])

    def cp_psum_num(self, accB, ct):
        self.mm(self.pb[6][0:64, :], self.identf[0:64, 0:64], accB[0:64, ct * 512:(ct + 1) * 512], True, True, ["baccB", "const"], ["pb6"])


_CACHE = {}


def _get_builder(S):
    if S not in _CACHE:
        b = Builder(S)
        b.build()
        _CACHE[S] = b
    return _CACHE[S]


def make_in_map(b, x1, inputs):
    m = {"x": np.ascontiguousarray(x1, dtype=np.float32)}
    m["norm_mix"] = inputs["norm_mix"]
    m["w_in"] = inputs["w_in"]
    m["b_gate"] = inputs["b_gate"]
    m["diff_lambda"] = inputs["diff_lambda"].reshape(DEPTH, 256)
    m["diff_subln"] = inputs["diff_subln"]
    m["na_rpb"] = inputs["na_rpb"].reshape(DEPTH, 120, 31)
    m["qk_norm"] = inputs["qk_norm"]
    m["w_branch"] = inputs["w_branch"].reshape(DEPTH, 2048, D_MODEL)
    m["w_out"] = inputs["w_out"]
    m["norm_ffn"] = inputs["norm_ffn"]
    m["w_ff1"] = inputs["w_ff1"]
    m["w_ff2"] = inputs["w_ff2"]
    m["norm_final"] = inputs["norm_final"].reshape(1, D_MODEL)
    for k, v in b.consts.items():
        m[k] = v
    return {k: np.ascontiguousarray(v) for k, v in m.items()}


def kernel(**inputs):
    inputs = {k: np.asarray(v) for k, v in inputs.items()}
    x = inputs["x"]
    B, S, _ = x.shape
    b = _get_builder(S)
    in_maps = [make_in_map(b, x[i], inputs) for i in range(B)]
    res = run_bass_kernel_spmd(b.nc, in_maps, core_ids=list(range(B)))
    return np.stack([np.asarray(r["out"]) for r in res.results], axis=0).astype(np.float32)
```

```python
import math
import os
import numpy as np
import ml_dtypes
import concourse.bass as bass
import concourse.mybir as mybir
from concourse.bass_utils import run_bass_kernel_spmd

F32 = mybir.dt.float32
BF16 = mybir.dt.bfloat16
AF = mybir.ActivationFunctionType
ALU = mybir.AluOpType
AX = mybir.AxisListType

D_MODEL = 1024
DEPTH = 2
GRID_W = 64
IN_W = 12544
NQK = 8448
EPS = 1e-6
NEG = -1e30
B_PATTERNS = ((128, 1), (512, 4), (2048, 16))
OFF_AQ, OFF_AK, OFF_AV, OFF_B, OFF_CQ, OFF_CK, OFF_CV, OFF_DQ, OFF_DK, OFF_DV, OFF_G = (
    0, 512, 1024, 1536, 6144, 6656, 7168, 7680, 8192, 8320, 8448)
VC_A, VC_B, VC_C, VC_D, VC_N = 0, 512, 2048, 2560, 2688

ENGS = ("pe", "act", "dve", "pool", "sp")
SEM_ROLL = 8000
DMA_SLOTS = 8
SB_BASE = 16512
SB_LIMIT = 228864


class Op:
    __slots__ = ("eng", "fn", "deps", "is_dma", "sig", "need_sig", "slot")

    def __init__(self, eng, fn, is_dma):
        self.eng = eng
        self.fn = fn
        self.is_dma = is_dma
        self.deps = []
        self.sig = None
        self.need_sig = False
        self.slot = None


class Sched:
    def __init__(self, nc):
        self.nc = nc
        self.ops = {e: [] for e in ENGS}
        self.last_w = {}
        self.readers = {}
        self.pending = {e: [] for e in ENGS}

    def _dep(self, op, d):
        if d is op:
            return
        if (not d.is_dma) and (not op.is_dma) and d.eng == op.eng and op.eng == "pe":
            return
        d.need_sig = True
        op.deps.append(d)

    def add(self, eng, fn, reads=(), writes=(), is_dma=False):
        op = Op(eng, fn, is_dma)
        deps = {}
        for r in reads:
            w = self.last_w.get(r)
            if w is not None:
                deps[id(w)] = w
        for r in writes:
            w = self.last_w.get(r)
            if w is not None:
                deps[id(w)] = w
            for rd in self.readers.get(r, ()):
                deps[id(rd)] = rd
        for d in self.pending[eng]:
            deps[id(d)] = d
        self.pending[eng] = []
        for d in deps.values():
            self._dep(op, d)
        for r in writes:
            self.last_w[r] = op
            self.readers[r] = []
        for r in reads:
            if r not in writes:
                self.readers.setdefault(r, []).append(op)
        self.ops[eng].append(op)
        return op

    def barrier(self):
        deps = []
        for e in ENGS:
            ops = self.ops[e]
            for op in reversed(ops):
                if not op.is_dma:
                    deps.append(op)
                    break
            k = 0
            for op in reversed(ops):
                if op.is_dma:
                    deps.append(op)
                    k += 1
                    if k >= DMA_SLOTS:
                        break
        for e in ENGS:
            self.pending[e] = list(deps)
        self.last_w.clear()
        self.readers.clear()

    def emit(self, final_waits=()):
        nc = self.nc
        sem_ctx = []

        def new_sem(name):
            cm = nc.semaphore(name)
            s = cm.__enter__()
            sem_ctx.append(cm)
            return s

        cnt = 0
        for e in ENGS:
            cur = None
            val = 0
            slots = [None] * DMA_SLOTS
            slotv = [0] * DMA_SLOTS
            nd = 0
            for op in self.ops[e]:
                if op.is_dma:
                    s = nd % DMA_SLOTS
                    nd += 1
                    if slots[s] is None or slotv[s] + 16 > SEM_ROLL:
                        slots[s] = new_sem(f"d{e}{s}_{cnt}")
                        cnt += 1
                        slotv[s] = 0
                        prev = None
                    else:
                        prev = (slots[s], slotv[s])
                    slotv[s] += 16
                    op.sig = (slots[s], slotv[s])
                    op.slot = prev
                elif op.need_sig:
                    if cur is None or val + 1 > SEM_ROLL:
                        cur = new_sem(f"c{e}_{cnt}")
                        cnt += 1
                        val = 0
                    val += 1
                    op.sig = (cur, val)
        self.n_sems = cnt
        engmap = {"pe": "tensor", "act": "scalar", "dve": "vector", "pool": "gpsimd", "sp": "sync"}
        with nc.Block() as block:
            for e in ENGS:
                ops = self.ops[e]

                def body(eng, ops=ops, e=e):
                    waited = {}

                    def wait(sem, v):
                        k = id(sem)
                        if waited.get(k, 0) >= v:
                            return
                        waited[k] = v
                        eng.wait_ge(sem, v)

                    for op in ops:
                        if op.is_dma and op.slot is not None:
                            wait(*op.slot)
                        for d in op.deps:
                            wait(*d.sig)
                        ins = op.fn(eng)
                        if op.sig is not None:
                            ins.then_inc(op.sig[0], 16 if op.is_dma else 1)
                    if e == "sp":
                        for fw in final_waits:
                            wait(*fw.sig)

                getattr(block, engmap[e])(body)
        for cm in reversed(sem_ctx):
            cm.__exit__(None, None, None)


def alibi_slopes(n):
    return [2.0 ** (-8.0 * (i + 1) / n) for i in range(n)]


def c_mask_meta(S):
    rows = S // GRID_W
    M = rows // 2
    reps = {"int": 2, "m0": 0, "m1": 1, "mL2": M - 2, "mL1": M - 1}
    meta = []
    tiles = []
    kc = np.arange(64)
    c = np.arange(64)
    qstart = np.clip(c - 8, 0, GRID_W - 16)
    colok = (kc[:, None] >= qstart[None, :]) & (kc[:, None] < qstart[None, :] + 16)
    for cls, m in reps.items():
        for dr in range(-4, 5):
            mk = m + dr
            if mk < 0 or mk >= M:
                continue
            t = np.full((128, 128), NEG, np.float32)
            anyv = False
            for a in range(2):
                for b in range(2):
                    r = 2 * m + b
                    kr = 2 * mk + a
                    rs = min(max(r - 4, 0), rows - 8)
                    if rs <= kr < rs + 8:
                        t[a * 64:(a + 1) * 64, b * 64:(b + 1) * 64] = np.where(colok, 0.0, NEG)
                        anyv = True
            if anyv:
                meta.append((cls, dr))
                tiles.append(t)
    return meta, np.stack(tiles)


def make_consts(S):
    bf = ml_dtypes.bfloat16
    c = {}
    c["c_ident"] = np.eye(128, dtype=np.float32).astype(bf)
    c["c_identf"] = np.eye(128, dtype=np.float32)
    p = np.arange(128)
    i32 = (p % 64) % 32
    partner = np.where(i32 < 16, p + 16, p - 16)
    pm = np.zeros((128, 128), np.float32)
    pm[partner, p] = 1.0
    c["c_pm"] = pm
    c["c_bones"] = (p[:, None] // 64 == p[None, :] // 64).astype(np.float32)
    c["c_ones_f"] = np.ones((128, 128), np.float32)
    c["c_j64"] = np.ascontiguousarray(np.eye(64, dtype=np.float32)[::-1])
    c["c_ones_b"] = np.ones((128, 128), np.float32).astype(bf)
    t = np.arange(S)
    d = p % 64
    half = d // 32
    idx = (d % 32) % 16
    first = (d % 32) < 16
    inv = (10000.0 ** (-np.arange(16, dtype=np.float32) / 16)).astype(np.float32)
    pos = np.where(half[:, None] == 0, (t // GRID_W)[None, :], (t % GRID_W)[None, :]).astype(np.float32)
    ang = pos * inv[idx][:, None]
    rope = np.zeros((2, 128, S), np.float32)
    rope[0] = np.cos(ang)
    rope[1] = np.where(first[:, None], -np.sin(ang), np.sin(ang))
    c["c_rope"] = rope
    sl = alibi_slopes(4)
    kaug = np.zeros((4, 5, S), np.float32)
    qaug = np.zeros((4, 2, 5, S), np.float32)
    for h in range(4):
        s8 = 8.0 * sl[h]
        kaug[h, 0:3] = 1.0
        kaug[h, 3] = s8 * (128 * (t // 128))
        kaug[h, 4] = s8 * (t % 128)
        qaug[h, 0, 0] = -s8 * (512 * (t // 512))
        qaug[h, 0, 1] = -s8 * (256 * ((t % 512) // 256))
        qaug[h, 0, 2] = -s8 * (t % 256)
        qaug[h, 0, 3:5] = 1.0
        qaug[h, 1] = -qaug[h, 0]
    c["c_kaug"] = kaug.astype(bf)
    c["c_qaug"] = qaug.astype(bf)
    assert np.array_equal(c["c_kaug"].astype(np.float32), kaug) and np.array_equal(c["c_qaug"].astype(np.float32), qaug)
    k = np.arange(128)[:, None]
    q = np.arange(128)[None, :]
    q5 = np.arange(512)[None, :]
    cd = np.zeros((128, 4, 4, 512), np.float32)
    for h in range(4):
        for rel in range(4):
            dd = q5 - (k + 128 * rel)
            cd[:, h, rel, :] = 2.0 * sl[h] * np.minimum(dd, 0)
    c["c_cd"] = cd
    slb = alibi_slopes(8)
    mb = np.zeros((24, 128, 3, 128), np.float32)
    for g, (win, dil) in enumerate(B_PATTERNS):
        for h in range(8):
            for r in range(3):
                rel = (128 * (r - 1) + k) - q
                mb[g * 8 + h, :, r, :] = np.where(np.abs(rel) <= 64, -slb[h] * np.abs(rel) * dil, NEG)
    c["c_mb"] = mb.reshape(24, 128, 384)
    meta, mcv = c_mask_meta(S)
    c["c_mcv"] = np.ascontiguousarray(mcv.transpose(1, 0, 2))
    return c, meta


class Builder:
    def __init__(self, S, nl=DEPTH, dbg=None):
        self.Sq = S
        self.nl = nl
        self.dbg = dbg or {}
        self.nc = bass.Bass("TRN2", target_bir_lowering=False)
        self.S = Sched(self.nc)
        self.off = SB_BASE
        self.ncnt = 0
        self.consts, self.cmeta = make_consts(S)
        self.outs = []

    def T(self, name, shape, dt):
        sz = 4 if dt == F32 else 2
        n = 1
        for s in shape[1:]:
            n *= s
        nbytes = (n * sz + 63) // 64 * 64
        self.ncnt += 1
        t = self.nc.alloc_sbuf_tensor_at(f"{name}_{self.ncnt}", list(shape), dt, offset=self.off)
        self.off += nbytes
        assert self.off <= SB_LIMIT, (name, self.off)
        return t

    def dram(self, name, shape, dt, kind="Internal"):
        return self.nc.dram_tensor(name, list(shape), dt, kind=kind).ap()

    def mm(self, out, lhsT, rhs, start, stop, rd, wr):
        return self.S.add("pe", lambda e: e.matmul(out, lhsT=lhsT, rhs=rhs, start=start, stop=stop, skip_group_check=True), rd, wr)

    def tr(self, out, in_, ident, rd, wr):
        return self.S.add("pe", lambda e: e.transpose(out, in_, ident), rd, wr)

    def act(self, out, in_, func, rd, wr, bias=None, scale=1.0, accum=None):
        def f(e):
            kw = {}
            if bias is not None:
                kw["bias"] = bias
            if accum is not None:
                kw["accum_out"] = accum
            return e.activation(out=out, in_=in_, func=func, scale=scale, **kw)
        return self.S.add("act", f, rd, wr)

    def tt(self, eng, out, in0, in1, op, rd, wr):
        return self.S.add(eng, lambda e: e.tensor_tensor(out=out, in0=in0, in1=in1, op=op), rd, wr)

    def ts(self, eng, out, in0, s1, op0, rd, wr, s2=None, op1=None):
        if op1 is None:
            return self.S.add(eng, lambda e: e.tensor_scalar(out=out, in0=in0, scalar1=s1, scalar2=None, op0=op0), rd, wr)
        return self.S.add(eng, lambda e: e.tensor_scalar(out=out, in0=in0, scalar1=s1, scalar2=s2, op0=op0, op1=op1), rd, wr)

    def stt(self, eng, out, in0, scalar, in1, op0, op1, rd, wr):
        return self.S.add(eng, lambda e: e.scalar_tensor_tensor(out=out, in0=in0, scalar=scalar, in1=in1, op0=op0, op1=op1), rd, wr)

    def cp(self, eng, out, in_, rd, wr):
        if eng == "act":
            return self.S.add("act", lambda e: e.copy(out=out, in_=in_), rd, wr)
        return self.S.add(eng, lambda e: e.tensor_copy(out=out, in_=in_), rd, wr)

    def recip(self, out, in_, rd, wr):
        return self.S.add("dve", lambda e: e.reciprocal(out=out, in_=in_), rd, wr)

    def memset(self, eng, ap, v, wr):
        return self.S.add(eng, lambda e: e.memset(ap, v), [], wr)

    def dma(self, q, out, in_, rd, wr, slow=False):
        if slow:
            return self.S.add(q, lambda e: e.dma_start(out=out, in_=in_, allow_slow_non_contiguous=True), rd, wr, is_dma=True)
        return self.S.add(q, lambda e: e.dma_start(out=out, in_=in_), rd, wr, is_dma=True)

    def build(self, stages="WPABCDM"):
        nc = self.nc
        S = self.Sq
        nl = self.nl
        dbg = self.dbg
        di = lambda n, sh, dt=F32: nc.dram_tensor(n, list(sh), dt, kind="ExternalInput").ap()
        self.x_in = di("x", [S, D_MODEL])
        self.norm_mix = di("norm_mix", [DEPTH, D_MODEL])
        self.w_in = di("w_in", [DEPTH, D_MODEL, IN_W])
        self.b_gate = di("b_gate", [DEPTH, 4096])
        self.diff_lambda = di("diff_lambda", [DEPTH, 256])
        self.diff_subln = di("diff_subln", [DEPTH, 128])
        self.na_rpb = di("na_rpb", [DEPTH, 120, 31])
        self.qk_norm = di("qk_norm", [DEPTH, 2, 64])
        self.w_branch = di("w_branch", [DEPTH, 2048, D_MODEL])
        self.w_out = di("w_out", [DEPTH, D_MODEL, D_MODEL])
        self.norm_ffn = di("norm_ffn", [DEPTH, D_MODEL])
        self.w_ff1 = di("w_ff1", [DEPTH, D_MODEL, 4096])
        self.w_ff2 = di("w_ff2", [DEPTH, 4096, D_MODEL])
        self.norm_final = di("norm_final", [1, D_MODEL])
        self.cd = {}
        for k, v in self.consts.items():
            self.cd[k] = di(k, v.shape, BF16 if v.dtype == ml_dtypes.bfloat16 else F32)
        okind = "ExternalOutput"
        self.out = nc.dram_tensor("out", [S, D_MODEL], F32, kind=okind).ap()
        dk = lambda n: okind if dbg.get(n) else "Internal"
        self.wb_in = [self.dram(f"wb_in{l}", [D_MODEL, IN_W], BF16) for l in range(nl)]
        self.wb_br = [self.dram(f"wb_br{l}", [2048, D_MODEL], BF16) for l in range(nl)]
        self.wb_out = [self.dram(f"wb_out{l}", [D_MODEL, D_MODEL], BF16) for l in range(nl)]
        self.wb_f1 = [self.dram(f"wb_f1{l}", [D_MODEL, 4096], BF16) for l in range(nl)]
        self.wb_f2 = [self.dram(f"wb_f2{l}", [4096, D_MODEL], BF16) for l in range(nl)]
        self.qkT = self.dram("qkT", [NQK, S], BF16, dk("qkT"))
        self.vtm = self.dram("vtm", [S, VC_N], BF16, dk("vtm"))
        self.gT = self.dram("gT", [4096, S], BF16, dk("gT"))
        if dbg.get("yT_in"):
            self.yT = di("yT", [4, 512, S], BF16)
        else:
            self.yT = self.dram("yT", [4, 512, S], BF16, dk("yT"))
        self.xres = self.dram("xres", [S, D_MODEL], F32, dk("xres"))
        self.rpbr = self.dram("rpbr", [120, 128], F32)
        for n in ("qkT", "vtm", "gT", "yT", "xres"):
            if dbg.get(n):
                self.outs.append(n)

        self.ident = self.T("ident", [128, 128], BF16)
        self.pm = self.T("pm", [128, 128], F32)
        self.identf = self.T("identf", [128, 128], F32)
        self.bones = self.T("bones", [128, 128], F32)
        self.ones_f = self.T("ones_f", [128, 128], F32)
        self.ones_b = self.T("ones_b", [128, 128], BF16)
        self.small = self.T("small", [128, 64], F32)
        for t, n in ((self.ident, "c_ident"), (self.pm, "c_pm"), (self.identf, "c_identf"), (self.bones, "c_bones"),
                     (self.ones_f, "c_ones_f"), (self.ones_b, "c_ones_b")):
            self.dma("sp", t[:], self.cd[n][:, :], [], ["const"])
        self.pall = nc.alloc_psum_tensor("pall", [128, 4096], F32)
        self.pbig = self.pall[:, 0:2048]
        self.pb = [self.pall[:, i * 512:(i + 1) * 512] for i in range(8)]
        self.pbt = self.pall.bitcast(BF16)[:, 7 * 1024:8 * 1024]
        self.base_off = self.off
        self.S.barrier()

        if "W" in stages:
            self.phase_w()
        final = []
        for l in range(nl):
            self.layer_smalls(l)
            xsrc = self.x_in if l == 0 else self.xres
            if "P" in stages:
                self.phase_p(l, xsrc)
            if "D" in stages:
                self.mixer_d(l)
            if "A" in stages:
                self.mixer_a(l)
            if "C" in stages:
                self.mixer_c(l)
            if "B" in stages:
                self.mixer_b(l)
            if "M" in stages:
                final = self.phase_m(l, xsrc, last=(l == nl - 1))
        self.S.barrier()
        self.S.emit(final_waits=final)
        return nc

    def phase_reset(self):
        self.S.barrier()
        self.off = self.base_off

    def phase_w(self):
        self.phase_reset()
        nl = self.nl
        gcol = self.T("gcol", [128, 32], F32)
        for l in range(nl):
            self.dma("sp", gcol[:, l * 16:l * 16 + 8], self.norm_mix[l, :].rearrange("(c p) -> p c", p=128), [], ["gcol"], slow=True)
            self.dma("sp", gcol[:, l * 16 + 8:l * 16 + 16], self.norm_ffn[l, :].rearrange("(c p) -> p c", p=128), [], ["gcol"], slow=True)
        bi = [self.T(f"wci{i}", [128, 4096], F32) for i in range(2)]
        bo = [self.T(f"wco{i}", [128, 4096], BF16) for i in range(2)]
        n = 0
        for l in range(nl):
            jobs = [(self.w_in[l], self.wb_in[l], D_MODEL, IN_W, l * 16),
                    (self.w_branch[l], self.wb_br[l], 2048, D_MODEL, None),
                    (self.w_out[l], self.wb_out[l], D_MODEL, D_MODEL, None),
                    (self.w_ff1[l], self.wb_f1[l], D_MODEL, 4096, l * 16 + 8),
                    (self.w_ff2[l], self.wb_f2[l], 4096, D_MODEL, None)]
            for src, dst, R, C, gc in jobs:
                for rc in range(R // 128):
                    for c0 in range(0, C, 4096):
                        cw = min(4096, C - c0)
                        i = n % 2
                        n += 1
                        self.dma("sp", bi[i][:, 0:cw], src[rc * 128:(rc + 1) * 128, c0:c0 + cw], [], [f"wci{i}"])
                        eng = "dve" if n % 2 else "pool"
                        if gc is not None:
                            self.ts(eng, bo[i][:, 0:cw], bi[i][:, 0:cw], gcol[:, gc + rc:gc + rc + 1], ALU.mult,
                                    [f"wci{i}", "gcol"], [f"wco{i}"])
                        else:
                            self.cp(eng, bo[i][:, 0:cw], bi[i][:, 0:cw], [f"wci{i}"], [f"wco{i}"])
                        self.dma("pool", dst[rc * 128:(rc + 1) * 128, c0:c0 + cw], bo[i][:, 0:cw], [f"wco{i}"], [])

    def layer_smalls(self, l):
        self.phase_reset()
        sm = self.small
        lam_init = 0.8 - 0.6 * math.exp(-0.3 * l)
        lt = self.T("lamt", [128, 256], F32)
        pr = self.T("lampr", [128, 128], F32)
        sv = self.T("lamsv", [128, 4], F32)
        self.dma("sp", lt[:], self.diff_lambda[l:l + 1, :].partition_broadcast(128), [], ["lamt"])
        self.tt("dve", pr[:, 0:64], lt[:, 0:64], lt[:, 64:128], ALU.mult, ["lamt"], ["lampr"])
        self.tt("dve", pr[:, 64:128], lt[:, 128:192], lt[:, 192:256], ALU.mult, ["lamt"], ["lampr"])
        self.S.add("dve", lambda e: e.reduce_sum(out=sv[:, 0:2], in_=pr[:].rearrange("p (a b) -> p a b", a=2), axis=AX.X),
                   ["lampr"], ["lamsv"])
        self.act(sv[:, 2:4], sv[:, 0:2], AF.Exp, ["lamsv"], ["lamsv"])
        self.tt("dve", sm[:, 0:1], sv[:, 3:4], sv[:, 2:3], ALU.subtract, ["lamsv"], ["small"])
        self.ts("dve", sm[:, 0:1], sm[:, 0:1], -lam_init, ALU.add, ["small"], ["small"])
        self.dma("sp", sm[:, 1:2], self.diff_subln[l, :].rearrange("(p o) -> p o", o=1), [], ["small"])
        self.ts("dve", sm[:, 1:2], sm[:, 1:2], 1.0 - lam_init, ALU.mult, ["small"], ["small"])
        for i in range(2):
            for hf in range(2):
                self.dma("sp", sm[hf * 64:(hf + 1) * 64, 2 + i:3 + i], self.qk_norm[l, i, :].rearrange("(p o) -> p o", o=1), [], ["small"])
        self.dma("sp", sm[:, 8:40], self.b_gate[l, :].rearrange("(c p) -> p c", p=128), [], ["small"], slow=True)

    def phase_p(self, l, xsrc):
        self.phase_reset()
        S = self.Sq
        sm = self.small
        NT = S // 512
        xts = [self.T(f"xtP{i}", [128, 4, 1024], F32) for i in range(2)]
        hTs = [self.T(f"hTP{i}", [128, 8, 512], BF16) for i in range(2)]
        wts = [self.T(f"wtP{i}", [128, 8, 512], BF16) for i in range(3)]
        stg = [self.T(f"stgP{i}", [128, 512], BF16) for i in range(4)]
        vst = [self.T(f"vstP{i}", [128, 4, 512], BF16) for i in range(2)]
        rope = [self.T(f"ropeP{i}", [128, 2, 512], F32) for i in range(2)]
        dtmp = [self.T(f"dtmpP{i}", [128, 512], F32) for i in range(5)]
        plan = []
        for c0 in range(0, OFF_G, 512):
            if c0 in (OFF_AV, OFF_B + 1024, OFF_B + 1536 + 1024, OFF_B + 3072 + 1024, OFF_CV):
                vc = {OFF_AV: VC_A, OFF_B + 1024: VC_B, OFF_B + 2560: VC_B + 512, OFF_B + 4096: VC_B + 1024, OFF_CV: VC_C}[c0]
                plan.append((c0, 512, "tm", vc))
            elif c0 == OFF_DQ:
                plan.append((c0, 512, "dq", None))
            elif c0 == OFF_DK:
                plan.append((c0, 256, "dkv", None))
            else:
                plan.append((c0, 512, "fm", None))
        for c0 in range(OFF_G, IN_W, 512):
            plan.append((c0, 512, "gate", None))
        wn = 0
        sn = 0
        pbn = 0
        for tt_ in range(NT):
            t0 = tt_ * 512
            xt = xts[tt_ % 2]
            hT = hTs[tt_ % 2]
            xk, hk = f"xtP{tt_ % 2}", f"hTP{tt_ % 2}"
            rp = rope[tt_ % 2]
            rk = f"ropeP{tt_ % 2}"
            self.dma("sp", xt[:], xsrc[t0:t0 + 512, :].rearrange("(s p) d -> p s d", p=128), [], [xk])
            self.dma("sp", rp[:], self.cd["c_rope"][:, :, t0:t0 + 512].rearrange("a p t -> p a t"), [], [rk])
            mark = self.off
            self.rmsnorm_T_keys(xt, hT, xk, hk)
            self.off = mark
            for (c0, ncol, kind, vc) in plan:
                w = wts[wn % 3]
                wk = f"wtP{wn % 3}"
                wn += 1
                self.dma("sp", w[:, :, 0:ncol], self.wb_in[l][:, c0:c0 + ncol].rearrange("(k p) n -> p k n", p=128), [], [wk])
                if kind in ("fm", "gate", "dq", "dkv"):
                    nfm = {"fm": 4, "gate": 4, "dq": 4, "dkv": 1}[kind]
                    for j in range(nfm):
                        bank = self.pb[pbn % 4]
                        bk = f"pb{pbn % 4}"
                        pbn += 1
                        for kc in range(8):
                            self.mm(bank[:], w[:, kc, j * 128:(j + 1) * 128], hT[:, kc, :], kc == 0, kc == 7, [wk, hk], [bk])
                        st = stg[sn % 4]
                        sk = f"stgP{sn % 4}"
                        sn += 1
                        col = c0 + j * 128
                        if kind == "fm":
                            self.cp("act" if sn % 2 else "dve", st[:], bank[:], [bk], [sk])
                            self.dma("pool", self.qkT[col:col + 128, t0:t0 + 512], st[:], [sk], [])
                        elif kind == "gate":
                            gc = (col - OFF_G) // 128
                            self.act(st[:], bank[:], AF.Sigmoid, [bk, "small"], [sk], bias=sm[:, 8 + gc:9 + gc])
                            self.dma("pool", self.gT[col - OFF_G:col - OFF_G + 128, t0:t0 + 512], st[:], [sk], [])
                        else:
                            gi = 2 if kind == "dq" else 3
                            sq, rst, xg, t1, t2 = dtmp
                            self.act(sq[:], bank[:], AF.Square, [bk], ["dsq"])
                            self.mm(self.pb[4][:], self.bones[:], sq[:], True, True, ["dsq", "const"], ["pb4"])
                            self.ts("dve", rst[:], self.pb[4][:], 1.0 / 64, ALU.mult, ["pb4"], ["drst"], s2=EPS, op1=ALU.add)
                            self.act(rst[:], rst[:], AF.Sqrt, ["drst"], ["drst"])
                            self.recip(rst[:], rst[:], ["drst"], ["drst"])
                            self.ts("dve", xg[:], bank[:], sm[:, gi:gi + 1], ALU.mult, [bk, "small"], ["dxg"])
                            self.mm(self.pb[5][:], self.pm[:], xg[:], True, True, ["dxg", "const"], ["pb5"])
                            self.tt("pool", t1[:], xg[:], rp[:, 0, :], ALU.mult, ["dxg", rk], ["dt1"])
                            self.tt("dve", t2[:], self.pb[5][:], rp[:, 1, :], ALU.mult, ["pb5", rk], ["dt2"])
                            self.tt("pool", t1[:], t1[:], t2[:], ALU.add, ["dt1", "dt2"], ["dt1"])
                            self.tt("dve", st[:], t1[:], rst[:], ALU.mult, ["dt1", "drst"], [sk])
                            self.dma("pool", self.qkT[col:col + 128, t0:t0 + 512], st[:], [sk], [])
                if kind in ("tm", "dkv"):
                    if kind == "tm":
                        wc0, nvc, vcol = 0, 512, vc
                    else:
                        wc0, nvc, vcol = 128, 128, VC_D
                    vs = vst[wn % 2]
                    vk = f"vstP{wn % 2}"
                    for s in range(4):
                        bank = self.pb[pbn % 4]
                        bk = f"pb{pbn % 4}"
                        pbn += 1
                        for kc in range(8):
                            self.mm(bank[:, 0:nvc], hT[:, kc, s * 128:(s + 1) * 128], w[:, kc, wc0:wc0 + nvc], kc == 0, kc == 7, [wk, hk], [bk])
                        self.cp("act" if s % 2 else "dve", vs[:, s, 0:nvc], bank[:, 0:nvc], [bk], [vk])
                    self.dma("pool", self.vtm[t0:t0 + 512, vcol:vcol + nvc].rearrange("(s p) c -> p s c", p=128),
                             vs[:, :, 0:nvc], [vk], [])

    def rmsnorm_T_keys(self, xt, hT, xk, hk, inv_d=1.0 / D_MODEL):
        ss = self.T("ss", [128, 8], F32)
        junk = self.T("junk", [128, 1024], BF16)
        hn = self.T("hn", [128, 4, 1024], BF16)
        self.memset("dve", ss[:], 0.0, ["ss"])
        for s in range(4):
            self.act(junk[:], xt[:, s, :], AF.Square, [xk, "ss"], ["junk", "ss"], accum=ss[:, s:s + 1])
        self.ts("dve", ss[:, 4:8], ss[:, 0:4], inv_d, ALU.mult, ["ss"], ["ss"], s2=EPS, op1=ALU.add)
        self.act(ss[:, 4:8], ss[:, 4:8], AF.Sqrt, ["ss"], ["ss"])
        self.recip(ss[:, 4:8], ss[:, 4:8], ["ss"], ["ss"])
        for s in range(4):
            self.ts("dve" if s % 2 else "pool", hn[:, s, :], xt[:, s, :], ss[:, 4 + s:5 + s], ALU.mult,
                    [xk, "ss"], [f"hn{s}"])
        for c in range(8):
            for s in range(4):
                self.tr(self.pbt[:, s * 128:(s + 1) * 128], hn[:, s, c * 128:(c + 1) * 128], self.ident[:],
                        [f"hn{s}", "const"], ["pb7"])
            self.cp("act" if c % 2 else "dve", hT[:, c, :], self.pbt[:, 0:512], ["pb7"], [hk])

    def phase_m(self, l, xsrc, last):
        self.phase_reset()
        S = self.Sq
        NT = S // 512
        xts = [self.T(f"xtM{i}", [128, 4, 1024], F32) for i in range(2)]
        yts = [self.T(f"ytM{i}", [128, 4, 512], BF16) for i in range(2)]
        gts = [self.T(f"gtM{i}", [128, 8, 512], BF16) for i in range(2)]
        wts = [self.T(f"wtM{i}", [128, 8, 512], BF16) for i in range(3)]
        macc = self.T("macc", [128, 8, 512], F32)
        mtmp = [self.T(f"mtmp{i}", [128, 512], F32) for i in range(2)]
        mT = self.T("mT", [128, 8, 512], BF16)
        h2T = self.T("h2T", [128, 8, 512], BF16)
        uT = self.T("uT", [128, 32, 512], BF16)
        gfin = None
        if last:
            gfin = self.T("gfin", [128, 1024], F32)
            self.dma("sp", gfin[:], self.norm_final[0:1, :].partition_broadcast(128), [], ["gfin"])
        wn = 0
        pbn = 0
        yn = 0
        finals = []
        for tt_ in range(NT):
            t0 = tt_ * 512
            xt = xts[tt_ % 2]
            xk = f"xtM{tt_ % 2}"
            self.dma("sp", xt[:], xsrc[t0:t0 + 512, :].rearrange("(s p) d -> p s d", p=128), [], [xk])
            for n in range(4):
                yt = yts[yn % 2]
                gt = gts[yn % 2]
                yk, gk = f"ytM{yn % 2}", f"gtM{yn % 2}"
                yn += 1
                w = wts[wn % 3]
                wk = f"wtM{wn % 3}"
                wn += 1
                self.dma("sp", yt[:], self.yT[n, :, t0:t0 + 512].rearrange("(k p) t -> p k t", p=128), [], [yk])
                self.dma("sp", gt[:], self.gT[n * 1024:(n + 1) * 1024, t0:t0 + 512].rearrange("(k p) t -> p k t", p=128), [], [gk])
                wv = w[:].rearrange("p k n -> p (k n)").rearrange("p (k n) -> p k n", k=4)
                self.dma("sp", wv, self.wb_br[l][n * 512:(n + 1) * 512, :].rearrange("(k p) n -> p k n", p=128), [], [wk])
                for oc in range(8):
                    bank = self.pb[pbn % 3]
                    bk = f"pb{pbn % 3}"
                    pbn += 1
                    for kc in range(4):
                        self.mm(bank[:], wv[:, kc, oc * 128:(oc + 1) * 128], yt[:, kc, :], kc == 0, kc == 3, [wk, yk], [bk])
                    eng = "dve" if oc % 2 else "pool"
                    if n == 0:
                        self.tt("dve", macc[:, oc, :], bank[:], gt[:, oc, :], ALU.mult, [bk, gk], [f"macc{oc}"])
                    elif n < 3:
                        mt = mtmp[oc % 2]
                        self.tt("dve", mt[:], bank[:], gt[:, oc, :], ALU.mult, [bk, gk], [f"mtmp{oc % 2}"])
                        self.tt("pool", macc[:, oc, :], macc[:, oc, :], mt[:], ALU.add, [f"mtmp{oc % 2}", f"macc{oc}"], [f"macc{oc}"])
                    else:
                        mt = mtmp[oc % 2]
                        self.tt("dve", mt[:], bank[:], gt[:, oc, :], ALU.mult, [bk, gk], [f"mtmp{oc % 2}"])
                        self.tt("pool", mT[:, oc, :], macc[:, oc, :], mt[:], ALU.add, [f"mtmp{oc % 2}", f"macc{oc}"], [f"mT{oc}"])
            mTk = [f"mT{oc}" for oc in range(8)]
            for half in range(2):
                w = wts[wn % 3]
                wk = f"wtM{wn % 3}"
                wn += 1
                self.dma("sp", w[:], self.wb_out[l][:, half * 512:(half + 1) * 512].rearrange("(k p) n -> p k n", p=128), [], [wk])
                for s in range(4):
                    bank = self.pb[3 + pbn % 2]
                    bk = f"pb{3 + pbn % 2}"
                    pbn += 1
                    for kc in range(8):
                        self.mm(bank[:], mT[:, kc, s * 128:(s + 1) * 128], w[:, kc, :], kc == 0, kc == 7, [wk, f"mT{kc}"], [bk])
                    self.tt("dve", xt[:, s, half * 512:(half + 1) * 512], xt[:, s, half * 512:(half + 1) * 512], bank[:], ALU.add,
                            [bk, xk], [xk])
            mark = self.off
            self.rmsnorm_T_keys(xt, h2T, xk, "h2T")
            self.off = mark
            for ft in range(8):
                w = wts[wn % 3]
                wk = f"wtM{wn % 3}"
                wn += 1
                self.dma("sp", w[:], self.wb_f1[l][:, ft * 512:(ft + 1) * 512].rearrange("(k p) n -> p k n", p=128), [], [wk])
                for j in range(4):
                    fc = ft * 4 + j
                    bank = self.pb[pbn % 3]
                    bk = f"pb{pbn % 3}"
                    pbn += 1
                    for kc in range(8):
                        self.mm(bank[:], w[:, kc, j * 128:(j + 1) * 128], h2T[:, kc, :], kc == 0, kc == 7, [wk, "h2T"], [bk])
                    rl = mtmp[fc % 2]
                    self.act(rl[:], bank[:], AF.Relu, [bk], [f"mtmp{fc % 2}"])
                    self.tt("dve" if fc % 2 else "pool", uT[:, fc, :], rl[:], rl[:], ALU.mult, [f"mtmp{fc % 2}"], [f"uT{fc}"])
            for half in range(2):
                banks = [(self.pb[3 + s], f"pb{3 + s}") for s in range(4)]
                for fg in range(4):
                    w = wts[wn % 3]
                    wk = f"wtM{wn % 3}"
                    wn += 1
                    self.dma("sp", w[:], self.wb_f2[l][fg * 1024:(fg + 1) * 1024, half * 512:(half + 1) * 512].rearrange("(k p) n -> p k n", p=128),
                             [], [wk])
                    for s in range(4):
                        bank, bk = banks[s]
                        for kc in range(8):
                            fc = fg * 8 + kc
                            self.mm(bank[:], uT[:, fc, s * 128:(s + 1) * 128], w[:, kc, :], fg == 0 and kc == 0, fg == 3 and kc == 7,
                                    [wk, f"uT{fc}"], [bk])
                for s in range(4):
                    bank, bk = banks[s]
                    self.tt("dve", xt[:, s, half * 512:(half + 1) * 512], xt[:, s, half * 512:(half + 1) * 512], bank[:], ALU.add,
                            [bk, xk], [xk])
            if not last:
                self.dma("pool", self.xres[t0:t0 + 512, :].rearrange("(s p) d -> p s d", p=128), xt[:], [xk], [])
            else:
                ss = self.T("ssF", [128, 8], F32)
                junk = self.T("junkF", [128, 1024], BF16)
                self.memset("dve", ss[:], 0.0, ["ssF"])
                for s in range(4):
                    self.act(junk[:], xt[:, s, :], AF.Square, [xk, "ssF"], ["junkF", "ssF"], accum=ss[:, s:s + 1])
                self.ts("dve", ss[:, 4:8], ss[:, 0:4], 1.0 / D_MODEL, ALU.mult, ["ssF"], ["ssF"], s2=EPS, op1=ALU.add)
                self.act(ss[:, 4:8], ss[:, 4:8], AF.Sqrt, ["ssF"], ["ssF"])
                self.recip(ss[:, 4:8], ss[:, 4:8], ["ssF"], ["ssF"])
                for s in range(4):
                    self.stt("dve", xt[:, s, :], xt[:, s, :], ss[:, 4 + s:5 + s], gfin[:], ALU.mult, ALU.mult,
                             [xk, "ssF", "gfin"], [xk])
                finals.append(self.dma("pool", self.out[t0:t0 + 512, :].rearrange("(s p) d -> p s d", p=128), xt[:], [xk], []))
                self.off = mark
        return finals

    def mixer_d(self, l):
        self.phase_reset()
        S = self.Sq
        NKB = S // 128
        NQT = S // 512
        NSB = 6
        LA = 5
        kT = self.T("dkT", [128, S], BF16)
        va = self.T("dva", [128, NKB, 128], BF16)
        qts = [self.T(f"dq{i}", [128, 512], BF16) for i in range(3)]
        pts = [self.T(f"dpt{i}", [128, 512], BF16) for i in range(6)]
        rec = [self.T(f"drec{i}", [128, 512], F32) for i in range(2)]
        yst = [self.T(f"dyst{i}", [64, 512], BF16) for i in range(2)]
        self.memset("dve", kT[64:128, :], 0.0, ["dkT"])
        for i in range(3):
            self.memset("pool", qts[i][64:128, :], 0.0, [f"dq{i}"])
        self.memset("pool", va[:, :, 64:128], 1.0, ["dva"])
        gq = 0
        for g in range(2):
            self.dma("sp", kT[0:64, :], self.qkT[OFF_DK + g * 64:OFF_DK + (g + 1) * 64, :], [], ["dkT"])
            self.dma("sp", va[:, :, 0:64], self.vtm[:, VC_D + g * 64:VC_D + (g + 1) * 64].rearrange("(k p) c -> p k c", p=128), [], ["dva"])
            items = [(h, qt, kb) for h in range(4 * g, 4 * g + 4) for qt in range(NQT) for kb in range(NKB)]
            n = len(items)

            def qk(i, gq=gq):
                h, qt, kb = items[i]
                qi = gq + i // NKB
                q = qts[qi % 3]
                qk_ = f"dq{qi % 3}"
                if kb == 0:
                    for i2 in ([i, i + NKB] if i == 0 else [i + NKB]):
                        if i2 >= n:
                            continue
                        h2, qt2, _ = items[i2]
                        qj = gq + i2 // NKB
                        self.dma("sp", qts[qj % 3][0:64, :], self.qkT[OFF_DQ + h2 * 64:OFF_DQ + (h2 + 1) * 64, qt2 * 512:(qt2 + 1) * 512], [], [f"dq{qj % 3}"])
                self.mm(self.pb[i % NSB][:], kT[:, kb * 128:(kb + 1) * 128], q[:], True, True, ["dkT", qk_], [f"pb{i % NSB}"])

            def ex_av(i, gq=gq):
                h, qt, kb = items[i]
                qi = gq + i // NKB
                acc = self.pb[6 + qi % 2]
                ak = f"pb{6 + qi % 2}"
                pt = pts[i % 6]
                pk = f"dpt{i % 6}"
                self.act(pt[:], self.pb[i % NSB][:], AF.Exp, [f"pb{i % NSB}"], [pk], scale=0.125)
                self.mm(acc[:], va[:, kb, :], pt[:], kb == 0, kb == NKB - 1, ["dva", pk], [ak])
                if kb == NKB - 1:
                    ys = yst[qi % 2]
                    yk = f"dyst{qi % 2}"
                    rc = rec[qi % 2]
                    rk = f"drec{qi % 2}"
                    self.recip(rc[64:128, :], acc[64:128, :], [ak], [rk])
                    self.tt("dve", ys[:], acc[0:64, :], rc[64:128, :], ALU.mult, [ak, rk], [yk])
                    self.dma("pool", self.yT[3, h * 64:(h + 1) * 64, qt * 512:(qt + 1) * 512], ys[:], [yk], [])

            for i in range(n + LA):
                if i < n:
                    qk(i)
                if i >= LA:
                    ex_av(i - LA)
            gq += n // NKB

    def mixer_a(self, l):
        self.phase_reset()
        S = self.Sq
        sm = self.small
        NKB = S // 128
        NQT = S // 512
        NSB = 4
        LA = 3
        kTs = [self.T(f"akT{c}", [128, S], BF16) for c in range(2)]
        va = self.T("ava", [128, NKB, 128], BF16)
        qts = [[[self.T(f"aq{i}{c}{v}", [128, 512], BF16) for v in range(2)] for c in range(2)] for i in range(2)]
        pts = [self.T(f"apt{i}", [128, 512], BF16) for i in range(5)]
        for c in range(2):
            self.memset("dve", kTs[c][64:128, :], 0.0, [f"akT{c}"])
            for i in range(2):
                for v in range(2):
                    self.memset("pool", qts[i][c][v][64:128, :], 0.0, [f"aq{i}{c}{v}"])
        cdt = self.T("acd", [128, 4, 512], F32)
        dgt = [self.T(f"adg{i}", [128, 512], F32) for i in range(3)]
        r0 = [self.T(f"ar0{i}", [128, 512], F32) for i in range(2)]
        t0s = [self.T(f"at0{i}", [128, 512], F32) for i in range(2)]
        t1_ = self.T("at1", [128, 512], F32)
        sq = self.T("asq", [128, 512], F32)
        rs = self.T("ars", [128, 512], F32)
        yst = [self.T(f"ayst{i}", [128, 512], BF16) for i in range(2)]
        pssum = self.pb[7]
        gq = 0
        for h in range(4):
            for c in range(2):
                r = OFF_AK + (h * 2 + c) * 64
                self.dma("sp", kTs[c][0:64, :], self.qkT[r:r + 64, :], [], [f"akT{c}"])
                self.dma("sp", kTs[c][64:69, :], self.cd["c_kaug"][h, :, :], [], [f"akT{c}"])
            self.dma("sp", va[:], self.vtm[:, VC_A + h * 128:VC_A + (h + 1) * 128].rearrange("(k p) c -> p k c", p=128), [], ["ava"])
            self.dma("sp", cdt[:], self.cd["c_cd"][:, h, :, :], [], ["acd"])
            items = [(qt, c, kb) for qt in range(NQT) for c in range(2) for kb in range(NKB)]
            n = len(items)

            def qk(i, h=h, gq=gq):
                qt, c, kb = items[i]
                qi = (gq + qt) % 2
                if c == 0 and kb == 0:
                    for qt2 in ([0, 1] if qt == 0 else [qt + 1]):
                        if qt2 >= NQT:
                            continue
                        qj = (gq + qt2) % 2
                        for c2 in range(2):
                            r = OFF_AQ + (h * 2 + c2) * 64
                            for v in range(2):
                                self.dma("sp", qts[qj][c2][v][0:64, :], self.qkT[r:r + 64, qt2 * 512:(qt2 + 1) * 512], [], [f"aq{qj}{c2}{v}"])
                                self.dma("sp", qts[qj][c2][v][64:69, :], self.cd["c_qaug"][h, v, :, qt2 * 512:(qt2 + 1) * 512], [], [f"aq{qj}{c2}{v}"])
                rel = kb - qt * 4
                v = 1 if rel > 3 else 0
                self.mm(self.pb[i % NSB][:], kTs[c][:, kb * 128:(kb + 1) * 128], qts[qi][c][v][:], True, True,
                        [f"akT{c}", f"aq{qi}{c}{v}"], [f"pb{i % NSB}"])

            def ex_av(i, h=h, gq=gq):
                qt, c, kb = items[i]
                gi = gq * 2 + i // NKB
                acc, ak = self.pb[4 + 2 * (gi % 2)], f"pb{4 + 2 * (gi % 2)}"
                den, dk_ = self.pb[5 + 2 * (gi % 2)], f"pb{5 + 2 * (gi % 2)}"
                sb, sk = self.pb[i % NSB], f"pb{i % NSB}"
                pt, pk = pts[i % 5], f"apt{i % 5}"
                rel = kb - qt * 4
                if rel < 0 or rel > 3:
                    self.act(pt[:], sb[:], AF.Exp, [sk], [pk], scale=0.125)
                else:
                    dg, dgk = dgt[i % 3], f"adg{i % 3}"
                    self.ts("dve", dg[:], sb[:], 0.125, ALU.mult, [sk], [dgk])
                    self.tt("dve", dg[:], dg[:], cdt[:, rel, :], ALU.add, [dgk, "acd"], [dgk])
                    self.act(pt[:], dg[:], AF.Exp, [dgk], [pk])
                self.mm(acc[:], va[:, kb, :], pt[:], kb == 0, kb == NKB - 1, ["ava", pk], [ak])
                self.mm(den[:], self.ones_b[:], pt[:], kb == 0, kb == NKB - 1, ["const", pk], [dk_])
                if kb != NKB - 1:
                    return
                qg = gq + qt
                t0_, t0k = t0s[qg % 2], f"at0{qg % 2}"
                rr, rrk = r0[gi % 2], f"ar0{gi % 2}"
                self.recip(rr[:], den[:], [dk_], [rrk])
                if c == 0:
                    self.tt("dve", t0_[:], acc[:], rr[:], ALU.mult, [ak, rrk], [t0k])
                    return
                self.tt("dve", t1_[:], acc[:], rr[:], ALU.mult, [ak, rrk], ["at1"])
                self.stt("dve", t0_[:], t1_[:], sm[:, 0:1], t0_[:], ALU.mult, ALU.add, [t0k, "at1", "small"], [t0k])
                self.act(sq[:], t0_[:], AF.Square, [t0k], ["asq"])
                self.mm(pssum[:], self.ones_f[:], sq[:], True, True, ["asq", "const"], ["pb7"])
                self.ts("dve", rs[:], pssum[:], 1.0 / 128, ALU.mult, ["pb7"], ["ars"], s2=EPS, op1=ALU.add)
                self.act(rs[:], rs[:], AF.Sqrt, ["ars"], ["ars"])
                self.recip(rs[:], rs[:], ["ars"], ["ars"])
                ys, yk = yst[qg % 2], f"ayst{qg % 2}"
                self.stt("dve", ys[:], t0_[:], sm[:, 1:2], rs[:], ALU.mult, ALU.mult, [t0k, "ars", "small"], [yk])
                self.dma("pool", self.yT[0, h * 128:(h + 1) * 128, qt * 512:(qt + 1) * 512], ys[:], [yk], [])

            for i in range(n + LA):
                if i < n:
                    qk(i)
                if i >= LA:
                    ex_av(i - LA)
            gq += NQT

    def local_groups(self, qT, kT, va, keys, groups, mask_of, out_fn, tagp):
        pts = self._lp
        tmp = self._lt
        G = len(groups)
        state = {}

        def pass1(gi):
            qbs, rels = groups[gi]
            nq = len(qbs)
            used = []
            for ri, (r, kbs, m) in enumerate(rels):
                js = [j for j in range(nq) if kbs[j] is not None]
                if not js:
                    continue
                j0, j1 = js[0], js[-1] + 1
                assert js == list(range(j0, j1))
                sb = self.pb[self._sn % 4]
                sk = f"pb{self._sn % 4}"
                self._sn += 1
                for j in js:
                    self.mm(sb[:, j * 128:(j + 1) * 128], kT[:, kbs[j] * 128:(kbs[j] + 1) * 128], qT[:, qbs[j] * 128:(qbs[j] + 1) * 128],
                            True, True, keys, [sk])
                t = tmp[self._pn % 4]
                tk = f"{tagp}lt{self._pn % 4}"
                self._pn += 1
                pi = (gi % 2) * 9 + ri
                pt = pts[pi]
                pk = f"{tagp}lp{pi}"
                nj = j1 - j0
                mv = m.unsqueeze(1).broadcast_to([128, nj, 128]) if nj > 1 else m
                tv = t[:, j0 * 128:j1 * 128].rearrange("p (j q) -> p j q", j=nj) if nj > 1 else t[:, j0 * 128:j1 * 128]
                sv = sb[:, j0 * 128:j1 * 128].rearrange("p (j q) -> p j q", j=nj) if nj > 1 else sb[:, j0 * 128:j1 * 128]
                self.stt("dve", tv, sv, 0.125, mv, ALU.mult, ALU.add, [sk, "lmask"], [tk])
                self.act(pt[:, j0 * 128:j1 * 128], t[:, j0 * 128:j1 * 128], AF.Exp, [tk], [pk])
                used.append((ri, kbs, pt, pk))
            state[gi] = used

        def pass2(gi):
            qbs, rels = groups[gi]
            nq = len(qbs)
            acc = self.pb[4 + gi % 2]
            ak = f"pb{4 + gi % 2}"
            used = state.pop(gi)
            for j in range(nq):
                mine = [(ri, kbs, pt, pk) for (ri, kbs, pt, pk) in used if kbs[j] is not None]
                for n_, (ri, kbs, pt, pk) in enumerate(mine):
                    self.mm(acc[:, j * 128:(j + 1) * 128], va[:, kbs[j], :], pt[:, j * 128:(j + 1) * 128],
                            n_ == 0, n_ == len(mine) - 1, keys + [pk], [ak])
            out_fn(gi, qbs, acc, ak)

        for gi in range(G + 1):
            if gi < G:
                pass1(gi)
            if gi >= 1:
                pass2(gi - 1)

    def local_init(self, tagp):
        self._lp = [self.T(f"{tagp}lp{i}", [128, 512], BF16) for i in range(18)]
        self._lt = [self.T(f"{tagp}lt{i}", [128, 512], F32) for i in range(4)]
        self._sn = 0
        self._pn = 0

    def mixer_c(self, l):
        self.phase_reset()
        S = self.Sq
        M = S // 128
        meta = self.cmeta
        NV = len(meta)
        qT = self.T("cqT", [128, S], BF16)
        kT = self.T("ckT", [128, S], BF16)
        self.memset("dve", qT[64:128, :], 0.0, ["cqT"])
        self.memset("pool", kT[64:128, :], 0.0, ["ckT"])
        va = self.T("cva", [128, M, 128], BF16)
        mcv = self.T("cmcv", [128, NV, 128], F32)
        mcm = self.T("cmcm", [128, NV, 128], F32)
        TT = self.T("cTT", [128, 15, 64], F32)
        rp = self.T("crp", [120, 128], F32)
        rec = self.T("crec", [128, 512], F32)
        yst = [self.T(f"cyst{i}", [64, 512], BF16) for i in range(2)]
        self.local_init("c")
        self.memset("pool", va[:, :, 64:128], 1.0, ["cva"])
        self.dma("sp", mcv[:], self.cd["c_mcv"][:, :, :], [], ["cmcv"])
        self.memset("dve", rp[:], 0.0, ["crp"])
        self.dma("sp", rp[:, 48:79], self.na_rpb[l, :, :], [], ["crp"])
        self.dma("sp", self.rpbr[:, :], rp[:], ["crp"], ["rpbr"])
        Hd = self.T("cHd", [64, 15, 2, 64], F32)
        j64 = self.T("cj64", [64, 64], F32)
        self.dma("sp", j64[:], self.cd["c_j64"][:, :], [], ["cj64"])
        for h in range(8):
            self.dma("sp", qT[0:64, :], self.qkT[OFF_CQ + h * 64:OFF_CQ + (h + 1) * 64, :], [], ["cqT"])
            self.dma("sp", kT[0:64, :], self.qkT[OFF_CK + h * 64:OFF_CK + (h + 1) * 64, :], [], ["ckT"])
            self.dma("sp", va[:, :, 0:64], self.vtm[:, VC_C + h * 64:VC_C + (h + 1) * 64].rearrange("(k p) c -> p k c", p=128), [], ["cva"])
            for a in range(2):
                src = bass.AP(tensor=self.rpbr.tensor, offset=h * 15 * 128, ap=[[1, 64], [128, 15], [1, 64]])
                self.dma("sp", Hd[:, :, a, :], src, ["rpbr"], ["cHd"])
            for i in range(15):
                bank = self.pb[4 + i // 8]
                bk = f"pb{4 + i // 8}"
                self.mm(bank[:, (i % 8) * 64:(i % 8 + 1) * 64], Hd[:, i, :, :].rearrange("p a k -> p (a k)"), j64[:], True, True,
                        ["cHd", "cj64"], [bk])
            self.cp("dve", TT[:, 0:8, :], self.pb[4][:].rearrange("p (i c) -> p i c", i=8), ["pb4"], ["cTT"])
            self.cp("dve", TT[:, 8:15, :], self.pb[5][:, 0:448].rearrange("p (i c) -> p i c", i=7), ["pb5"], ["cTT"])
            for vi, (cls, dr) in enumerate(meta):
                for a in range(2):
                    for b in range(2):
                        ri = 2 * dr + a - b + 7
                        o = mcm[a * 64:(a + 1) * 64, vi, b * 64:(b + 1) * 64]
                        i0 = mcv[a * 64:(a + 1) * 64, vi, b * 64:(b + 1) * 64]
                        if 0 <= ri <= 14:
                            self.tt("pool", o, i0, TT[a * 64:(a + 1) * 64, ri, :], ALU.add, ["cmcv", "cTT"], ["lmask"])
                        else:
                            self.cp("pool", o, i0, ["cmcv"], ["lmask"])
            midx = {(cls, dr): vi for vi, (cls, dr) in enumerate(meta)}
            groups = []

            def mk(qbs, cls):
                rels = []
                for dr in range(-4, 5):
                    if (cls, dr) not in midx:
                        continue
                    kbs = [qb + dr if 0 <= qb + dr < M else None for qb in qbs]
                    rels.append((dr, kbs, mcm[:, midx[(cls, dr)], :]))
                return (qbs, rels)
            groups.append(mk([0], "m0"))
            groups.append(mk([1], "m1"))
            ints = list(range(2, M - 2))
            for i in range(0, len(ints), 4):
                groups.append(mk(ints[i:i + 4], "int"))
            groups.append(mk([M - 2], "mL2"))
            groups.append(mk([M - 1], "mL1"))

            def outf(gi, qbs, acc, ak, h=h):
                n = len(qbs) * 128
                ys = yst[gi % 2]
                yk = f"cyst{gi % 2}"
                self.recip(rec[64:128, 0:n], acc[64:128, 0:n], [ak], ["crec"])
                self.tt("dve", ys[:, 0:n], acc[0:64, 0:n], rec[64:128, 0:n], ALU.mult, [ak, "crec"], [yk])
                self.dma("pool", self.yT[2, h * 64:(h + 1) * 64, qbs[0] * 128:qbs[0] * 128 + n], ys[:, 0:n], [yk], [])
            import os
            if os.environ.get("CDBG", "0") != "1":
                self.local_groups(qT, kT, va, ["cqT", "ckT", "cva"], groups, None, outf, "c")

    def mixer_b(self, l):
        self.phase_reset()
        S = self.Sq
        NB = S // 128
        qn_ = self.T("bqn", [128, S], BF16)
        kn_ = self.T("bkn", [128, S], BF16)
        qp = self.T("bqp", [128, S], BF16)
        kp = self.T("bkp", [128, S], BF16)
        for t_, k_ in ((qn_, "bqn"), (kn_, "bkn"), (qp, "bqp"), (kp, "bkp")):
            self.memset("dve", t_[64:128, :], 0.0, [k_])
        va = self.T("bva", [128, NB, 128], BF16)
        accB = self.T("baccB", [128, S], F32)
        mk_ = self.T("bmk", [128, 3, 128], F32)
        rec = self.T("brec", [128, 512], F32)
        yst = [self.T(f"byst{i}", [64, 512], BF16) for i in range(2)]
        self.local_init("b")
        self.memset("pool", va[:, :, 64:128], 1.0, ["bva"])
        for h in range(8):
            for g, (win, dil) in enumerate(B_PATTERNS):
                L = S // dil
                nbc = L // 128
                base = OFF_B + g * 1536
                self.dma("sp", qn_[0:64, :], self.qkT[base + h * 64:base + (h + 1) * 64, :], [], ["bqn"])
                self.dma("sp", kn_[0:64, :], self.qkT[base + 512 + h * 64:base + 512 + (h + 1) * 64, :], [], ["bkn"])
                self.dma("sp", mk_[:], self.cd["c_mb"][g * 8 + h, :, :].rearrange("p (r q) -> p r q", r=3), [], ["lmask"])
                vcol = VC_B + g * 512 + h * 64
                if dil == 1:
                    qq, kk = qn_, kn_
                    keys = ["bqn", "bkn", "bva"]
                    self.dma("sp", va[:, :, 0:64], self.vtm[:, vcol:vcol + 64].rearrange("(k p) c -> p k c", p=128), [], ["bva"])
                else:
                    qq, kk = qp, kp
                    keys = ["bqp", "bkp", "bva"]
                    self.cp("pool", qp[0:64, :].rearrange("p (j i) -> p j i", j=dil), qn_[0:64, :].rearrange("p (i j) -> p j i", j=dil), ["bqn"], ["bqp"])
                    self.cp("dve", kp[0:64, :].rearrange("p (j i) -> p j i", j=dil), kn_[0:64, :].rearrange("p (i j) -> p j i", j=dil), ["bkn"], ["bkp"])
                    for j in range(dil):
                        src = self.vtm[:, vcol:vcol + 64].rearrange("(k p j) c -> j p k c", p=128, j=dil)[j]
                        self.dma("sp", va[:, j * nbc:(j + 1) * nbc, 0:64], src, [], ["bva"])
                groups = []
                gs = min(4, nbc)
                for j in range(dil):
                    for b0 in range(0, nbc, gs):
                        qbs = [j * nbc + b0 + i for i in range(gs)]
                        rels = []
                        for r in (-1, 0, 1):
                            kbs = [(qb + r) if 0 <= (qb - j * nbc + r) < nbc else None for qb in qbs]
                            rels.append((r, kbs, mk_[:, r + 1, :]))
                        groups.append((qbs, rels))

                def outf(gi, qbs, acc, ak, g=g, dil=dil, nbc=nbc):
                    n = len(qbs) * 128
                    j = qbs[0] // nbc
                    i0 = (qbs[0] - j * nbc) * 128
                    if dil == 1:
                        dst = accB[:, i0:i0 + n]
                        self.cp("dve", dst, acc[:, 0:n], [ak], ["baccB"])
                    else:
                        dst = accB[:, i0 * dil + j:(i0 + n - 1) * dil + j + 1:dil]
                        self.tt("dve", dst, dst, acc[:, 0:n], ALU.add, [ak, "baccB"], ["baccB"])
                self.local_groups(qq, kk, va, keys, groups, None, outf, "b")
            for ct in range(S // 512):
                ys = yst[ct % 2]
                yk = f"byst{ct % 2}"
                self.recip(rec[64:128, :], accB[64:128, ct * 512:(ct + 1) * 512], ["baccB"], ["brec"])
                self.cp_psum_num(accB, ct)
                self.tt("dve", ys[:], self.pb[6][0:64, :], rec[64:128, :], ALU.mult, ["pb6", "brec"], [yk])
                self.dma("pool", self.yT[1, h * 64:(h + 1) * 64, ct * 512:(ct + 1) * 512], ys[:], [yk], [])

    def cp_psum_num(self, accB, ct):
        self.mm(self.pb[6][0:64, :], self.identf[0:64, 0:64], accB[0:64, ct * 512:(ct + 1) * 512], True, True, ["baccB", "const"], ["pb6"])


_CACHE = {}


def _get_builder(S):
    if S not in _CACHE:
        b = Builder(S)
        b.build()
        _CACHE[S] = b
    return _CACHE[S]


def make_in_map(b, x1, inputs):
    m = {"x": np.ascontiguousarray(x1, dtype=np.float32)}
    m["norm_mix"] = inputs["norm_mix"]
    m["w_in"] = inputs["w_in"]
    m["b_gate"] = inputs["b_gate"]
    m["diff_lambda"] = inputs["diff_lambda"].reshape(DEPTH, 256)
    m["diff_subln"] = inputs["diff_subln"]
    m["na_rpb"] = inputs["na_rpb"].reshape(DEPTH, 120, 31)
    m["qk_norm"] = inputs["qk_norm"]
    m["w_branch"] = inputs["w_branch"].reshape(DEPTH, 2048, D_MODEL)
    m["w_out"] = inputs["w_out"]
    m["norm_ffn"] = inputs["norm_ffn"]
    m["w_ff1"] = inputs["w_ff1"]
    m["w_ff2"] = inputs["w_ff2"]
    m["norm_final"] = inputs["norm_final"].reshape(1, D_MODEL)
    for k, v in b.consts.items():
        m[k] = v
    return {k: np.ascontiguousarray(v) for k, v in m.items()}


def kernel(**inputs):
    inputs = {k: np.asarray(v) for k, v in inputs.items()}
    x = inputs["x"]
    B, S, _ = x.shape
    b = _get_builder(S)
    in_maps = [make_in_map(b, x[i], inputs) for i in range(B)]
    res = run_bass_kernel_spmd(b.nc, in_maps, core_ids=list(range(B)))
    return np.stack([np.asarray(r["out"]) for r in res.results], axis=0).astype(np.float32)
```

```python
import math
import os
import numpy as np
import ml_dtypes
import concourse.bass as bass
import concourse.mybir as mybir
from concourse.bass_utils import run_bass_kernel_spmd

F32 = mybir.dt.float32
BF16 = mybir.dt.bfloat16
AF = mybir.ActivationFunctionType
ALU = mybir.AluOpType
AX = mybir.AxisListType

D_MODEL = 1024
DEPTH = 2
GRID_W = 64
IN_W = 12544
NQK = 8448
EPS = 1e-6
NEG = -1e30
B_PATTERNS = ((128, 1), (512, 4), (2048, 16))
OFF_AQ, OFF_AK, OFF_AV, OFF_B, OFF_CQ, OFF_CK, OFF_CV, OFF_DQ, OFF_DK, OFF_DV, OFF_G = (
    0, 512, 1024, 1536, 6144, 6656, 7168, 7680, 8192, 8320, 8448)
VC_A, VC_B, VC_C, VC_D, VC_N = 0, 512, 2048, 2560, 2688

ENGS = ("pe", "act", "dve", "pool", "sp")
SEM_ROLL = 8000
DMA_SLOTS = 8
SB_BASE = 16512
SB_LIMIT = 228864


class Op:
    __slots__ = ("eng", "fn", "deps", "is_dma", "sig", "need_sig", "slot")

    def __init__(self, eng, fn, is_dma):
        self.eng = eng
        self.fn = fn
        self.is_dma = is_dma
        self.deps = []
        self.sig = None
        self.need_sig = False
        self.slot = None


class Sched:
    def __init__(self, nc):
        self.nc = nc
        self.ops = {e: [] for e in ENGS}
        self.last_w = {}
        self.readers = {}
        self.pending = {e: [] for e in ENGS}

    def _dep(self, op, d):
        if d is op:
            return
        if (not d.is_dma) and (not op.is_dma) and d.eng == op.eng and op.eng == "pe":
            return
        d.need_sig = True
        op.deps.append(d)

    def add(self, eng, fn, reads=(), writes=(), is_dma=False):
        op = Op(eng, fn, is_dma)
        deps = {}
        for r in reads:
            w = self.last_w.get(r)
            if w is not None:
                deps[id(w)] = w
        for r in writes:
            w = self.last_w.get(r)
            if w is not None:
                deps[id(w)] = w
            for rd in self.readers.get(r, ()):
                deps[id(rd)] = rd
        for d in self.pending[eng]:
            deps[id(d)] = d
        self.pending[eng] = []
        for d in deps.values():
            self._dep(op, d)
        for r in writes:
            self.last_w[r] = op
            self.readers[r] = []
        for r in reads:
            if r not in writes:
                self.readers.setdefault(r, []).append(op)
        self.ops[eng].append(op)
        return op

    def barrier(self):
        deps = []
        for e in ENGS:
            ops = self.ops[e]
            for op in reversed(ops):
                if not op.is_dma:
                    deps.append(op)
                    break
            k = 0
            for op in reversed(ops):
                if op.is_dma:
                    deps.append(op)
                    k += 1
                    if k >= DMA_SLOTS:
                        break
        for e in ENGS:
            self.pending[e] = list(deps)
        self.last_w.clear()
        self.readers.clear()

    def emit(self, final_waits=()):
        nc = self.nc
        sem_ctx = []

        def new_sem(name):
            cm = nc.semaphore(name)
            s = cm.__enter__()
            sem_ctx.append(cm)
            return s

        cnt = 0
        for e in ENGS:
            cur = None
            val = 0
            slots = [None] * DMA_SLOTS
            slotv = [0] * DMA_SLOTS
            nd = 0
            for op in self.ops[e]:
                if op.is_dma:
                    s = nd % DMA_SLOTS
                    nd += 1
                    if slots[s] is None or slotv[s] + 16 > SEM_ROLL:
                        slots[s] = new_sem(f"d{e}{s}_{cnt}")
                        cnt += 1
                        slotv[s] = 0
                        prev = None
                    else:
                        prev = (slots[s], slotv[s])
                    slotv[s] += 16
                    op.sig = (slots[s], slotv[s])
                    op.slot = prev
                elif op.need_sig:
                    if cur is None or val + 1 > SEM_ROLL:
                        cur = new_sem(f"c{e}_{cnt}")
                        cnt += 1
                        val = 0
                    val += 1
                    op.sig = (cur, val)
        self.n_sems = cnt
        engmap = {"pe": "tensor", "act": "scalar", "dve": "vector", "pool": "gpsimd", "sp": "sync"}
        with nc.Block() as block:
            for e in ENGS:
                ops = self.ops[e]

                def body(eng, ops=ops, e=e):
                    waited = {}

                    def wait(sem, v):
                        k = id(sem)
                        if waited.get(k, 0) >= v:
                            return
                        waited[k] = v
                        eng.wait_ge(sem, v)

                    for op in ops:
                        if op.is_dma and op.slot is not None:
                            wait(*op.slot)
                        for d in op.deps:
                            wait(*d.sig)
                        ins = op.fn(eng)
                        if op.sig is not None:
                            ins.then_inc(op.sig[0], 16 if op.is_dma else 1)
                    if e == "sp":
                        for fw in final_waits:
                            wait(*fw.sig)

                getattr(block, engmap[e])(body)
        for cm in reversed(sem_ctx):
            cm.__exit__(None, None, None)


def alibi_slopes(n):
    return [2.0 ** (-8.0 * (i + 1) / n) for i in range(n)]


def c_mask_meta(S):
    rows = S // GRID_W
    M = rows // 2
    reps = {"int": 2, "m0": 0, "m1": 1, "mL2": M - 2, "mL1": M - 1}
    meta = []
    tiles = []
    kc = np.arange(64)
    c = np.arange(64)
    qstart = np.clip(c - 8, 0, GRID_W - 16)
    colok = (kc[:, None] >= qstart[None, :]) & (kc[:, None] < qstart[None, :] + 16)
    for cls, m in reps.items():
        for dr in range(-4, 5):
            mk = m + dr
            if mk < 0 or mk >= M:
                continue
            t = np.full((128, 128), NEG, np.float32)
            anyv = False
            for a in range(2):
                for b in range(2):
                    r = 2 * m + b
                    kr = 2 * mk + a
                    rs = min(max(r - 4, 0), rows - 8)
                    if rs <= kr < rs + 8:
                        t[a * 64:(a + 1) * 64, b * 64:(b + 1) * 64] = np.where(colok, 0.0, NEG)
                        anyv = True
            if anyv:
                meta.append((cls, dr))
                tiles.append(t)
    return meta, np.stack(tiles)


def make_consts(S):
    bf = ml_dtypes.bfloat16
    c = {}
    c["c_ident"] = np.eye(128, dtype=np.float32).astype(bf)
    c["c_identf"] = np.eye(128, dtype=np.float32)
    p = np.arange(128)
    i32 = (p % 64) % 32
    partner = np.where(i32 < 16, p + 16, p - 16)
    pm = np.zeros((128, 128), np.float32)
    pm[partner, p] = 1.0
    c["c_pm"] = pm
    c["c_bones"] = (p[:, None] // 64 == p[None, :] // 64).astype(np.float32)
    c["c_ones_f"] = np.ones((128, 128), np.float32)
    c["c_j64"] = np.ascontiguousarray(np.eye(64, dtype=np.float32)[::-1])
    c["c_ones_b"] = np.ones((128, 128), np.float32).astype(bf)
    t = np.arange(S)
    d = p % 64
    half = d // 32
    idx = (d % 32) % 16
    first = (d % 32) < 16
    inv = (10000.0 ** (-np.arange(16, dtype=np.float32) / 16)).astype(np.float32)
    pos = np.where(half[:, None] == 0, (t // GRID_W)[None, :], (t % GRID_W)[None, :]).astype(np.float32)
    ang = pos * inv[idx][:, None]
    rope = np.zeros((2, 128, S), np.float32)
    rope[0] = np.cos(ang)
    rope[1] = np.where(first[:, None], -np.sin(ang), np.sin(ang))
    c["c_rope"] = rope
    sl = alibi_slopes(4)
    kaug = np.zeros((4, 5, S), np.float32)
    qaug = np.zeros((4, 2, 5, S), np.float32)
    for h in range(4):
        s8 = 8.0 * sl[h]
        kaug[h, 0:3] = 1.0
        kaug[h, 3] = s8 * (128 * (t // 128))
        kaug[h, 4] = s8 * (t % 128)
        qaug[h, 0, 0] = -s8 * (512 * (t // 512))
        qaug[h, 0, 1] = -s8 * (256 * ((t % 512) // 256))
        qaug[h, 0, 2] = -s8 * (t % 256)
        qaug[h, 0, 3:5] = 1.0
        qaug[h, 1] = -qaug[h, 0]
    c["c_kaug"] = kaug.astype(bf)
    c["c_qaug"] = qaug.astype(bf)
    assert np.array_equal(c["c_kaug"].astype(np.float32), kaug) and np.array_equal(c["c_qaug"].astype(np.float32), qaug)
    k = np.arange(128)[:, None]
    q = np.arange(128)[None, :]
    q5 = np.arange(512)[None, :]
    cd = np.zeros((128, 4, 4, 512), np.float32)
    for h in range(4):
        for rel in range(4):
            dd = q5 - (k + 128 * rel)
            cd[:, h, rel, :] = 2.0 * sl[h] * np.minimum(dd, 0)
    c["c_cd"] = cd
    slb = alibi_slopes(8)
    mb = np.zeros((24, 128, 3, 128), np.float32)
    for g, (win, dil) in enumerate(B_PATTERNS):
        for h in range(8):
            for r in range(3):
                rel = (128 * (r - 1) + k) - q
                mb[g * 8 + h, :, r, :] = np.where(np.abs(rel) <= 64, -slb[h] * np.abs(rel) * dil, NEG)
    c["c_mb"] = mb.reshape(24, 128, 384)
    meta, mcv = c_mask_meta(S)
    c["c_mcv"] = np.ascontiguousarray(mcv.transpose(1, 0, 2))
    return c, meta


class Builder:
    def __init__(self, S, nl=DEPTH, dbg=None):
        self.Sq = S
        self.nl = nl
        self.dbg = dbg or {}
        self.nc = bass.Bass("TRN2", target_bir_lowering=False)
        self.S = Sched(self.nc)
        self.off = SB_BASE
        self.ncnt = 0
        self.consts, self.cmeta = make_consts(S)
        self.outs = []

    def T(self, name, shape, dt):
        sz = 4 if dt == F32 else 2
        n = 1
        for s in shape[1:]:
            n *= s
        nbytes = (n * sz + 63) // 64 * 64
        self.ncnt += 1
        t = self.nc.alloc_sbuf_tensor_at(f"{name}_{self.ncnt}", list(shape), dt, offset=self.off)
        self.off += nbytes
        assert self.off <= SB_LIMIT, (name, self.off)
        return t

    def dram(self, name, shape, dt, kind="Internal"):
        return self.nc.dram_tensor(name, list(shape), dt, kind=kind).ap()

    def mm(self, out, lhsT, rhs, start, stop, rd, wr):
        return self.S.add("pe", lambda e: e.matmul(out, lhsT=lhsT, rhs=rhs, start=start, stop=stop, skip_group_check=True), rd, wr)

    def tr(self, out, in_, ident, rd, wr):
        return self.S.add("pe", lambda e: e.transpose(out, in_, ident), rd, wr)

    def act(self, out, in_, func, rd, wr, bias=None, scale=1.0, accum=None):
        def f(e):
            kw = {}
            if bias is not None:
                kw["bias"] = bias
            if accum is not None:
                kw["accum_out"] = accum
            return e.activation(out=out, in_=in_, func=func, scale=scale, **kw)
        return self.S.add("act", f, rd, wr)

    def tt(self, eng, out, in0, in1, op, rd, wr):
        return self.S.add(eng, lambda e: e.tensor_tensor(out=out, in0=in0, in1=in1, op=op), rd, wr)

    def ts(self, eng, out, in0, s1, op0, rd, wr, s2=None, op1=None):
        if op1 is None:
            return self.S.add(eng, lambda e: e.tensor_scalar(out=out, in0=in0, scalar1=s1, scalar2=None, op0=op0), rd, wr)
        return self.S.add(eng, lambda e: e.tensor_scalar(out=out, in0=in0, scalar1=s1, scalar2=s2, op0=op0, op1=op1), rd, wr)

    def stt(self, eng, out, in0, scalar, in1, op0, op1, rd, wr):
        return self.S.add(eng, lambda e: e.scalar_tensor_tensor(out=out, in0=in0, scalar=scalar, in1=in1, op0=op0, op1=op1), rd, wr)

    def cp(self, eng, out, in_, rd, wr):
        if eng == "act":
            return self.S.add("act", lambda e: e.copy(out=out, in_=in_), rd, wr)
        return self.S.add(eng, lambda e: e.tensor_copy(out=out, in_=in_), rd, wr)

    def recip(self, out, in_, rd, wr):
        return self.S.add("dve", lambda e: e.reciprocal(out=out, in_=in_), rd, wr)

    def memset(self, eng, ap, v, wr):
        return self.S.add(eng, lambda e: e.memset(ap, v), [], wr)

    def dma(self, q, out, in_, rd, wr, slow=False):
        if slow:
            return self.S.add(q, lambda e: e.dma_start(out=out, in_=in_, allow_slow_non_contiguous=True), rd, wr, is_dma=True)
        return self.S.add(q, lambda e: e.dma_start(out=out, in_=in_), rd, wr, is_dma=True)

    def build(self, stages="WPABCDM"):
        self.stages = stages
        nc = self.nc
        S = self.Sq
        nl = self.nl
        dbg = self.dbg
        di = lambda n, sh, dt=F32: nc.dram_tensor(n, list(sh), dt, kind="ExternalInput").ap()
        self.x_in = di("x", [S, D_MODEL])
        self.norm_mix = di("norm_mix", [DEPTH, D_MODEL])
        self.w_in = di("w_in", [DEPTH, D_MODEL, IN_W])
        self.b_gate = di("b_gate", [DEPTH, 4096])
        self.diff_lambda = di("diff_lambda", [DEPTH, 256])
        self.diff_subln = di("diff_subln", [DEPTH, 128])
        self.na_rpb = di("na_rpb", [DEPTH, 120, 31])
        self.qk_norm = di("qk_norm", [DEPTH, 2, 64])
        self.w_branch = di("w_branch", [DEPTH, 2048, D_MODEL])
        self.w_out = di("w_out", [DEPTH, D_MODEL, D_MODEL])
        self.norm_ffn = di("norm_ffn", [DEPTH, D_MODEL])
        self.w_ff1 = di("w_ff1", [DEPTH, D_MODEL, 4096])
        self.w_ff2 = di("w_ff2", [DEPTH, 4096, D_MODEL])
        self.norm_final = di("norm_final", [1, D_MODEL])
        self.cd = {}
        for k, v in self.consts.items():
            self.cd[k] = di(k, v.shape, BF16 if v.dtype == ml_dtypes.bfloat16 else F32)
        okind = "ExternalOutput"
        self.out = nc.dram_tensor("out", [S, D_MODEL], F32, kind=okind).ap()
        dk = lambda n: okind if dbg.get(n) else "Internal"
        self.wb_in = [self.dram(f"wb_in{l}", [D_MODEL, IN_W], BF16) for l in range(nl)]
        self.wb_br = [self.dram(f"wb_br{l}", [2048, D_MODEL], BF16) for l in range(nl)]
        self.wb_out = [self.dram(f"wb_out{l}", [D_MODEL, D_MODEL], BF16) for l in range(nl)]
        self.wb_f1 = [self.dram(f"wb_f1{l}", [D_MODEL, 4096], BF16) for l in range(nl)]
        self.wb_f2 = [self.dram(f"wb_f2{l}", [4096, D_MODEL], BF16) for l in range(nl)]
        self.qkT = self.dram("qkT", [NQK, S], BF16, dk("qkT"))
        self.vtm = self.dram("vtm", [S, VC_N], BF16, dk("vtm"))
        self.gT = self.dram("gT", [4096, S], BF16, dk("gT"))
        if dbg.get("yT_in"):
            self.yT = di("yT", [4, 512, S], BF16)
        else:
            self.yT = self.dram("yT", [4, 512, S], BF16, dk("yT"))
        self.xres = self.dram("xres", [S, D_MODEL], F32, dk("xres"))
        self.rpbr = self.dram("rpbr", [120, 128], F32)
        for n in ("qkT", "vtm", "gT", "yT", "xres"):
            if dbg.get(n):
                self.outs.append(n)

        self.ident = self.T("ident", [128, 128], BF16)
        self.pm = self.T("pm", [128, 128], F32)
        self.identf = self.T("identf", [128, 128], F32)
        self.bones = self.T("bones", [128, 128], F32)
        self.ones_f = self.T("ones_f", [128, 128], F32)
        self.ones_b = self.T("ones_b", [128, 128], BF16)
        self.small = self.T("small", [128, 64], F32)
        self.gcol = self.T("gcol", [128, 32], F32)
        for l in range(nl):
            self.dma("sp", self.gcol[:, l * 16:l * 16 + 8], self.norm_mix[l, :].rearrange("(c p) -> p c", p=128), [], ["const"], slow=True)
            self.dma("sp", self.gcol[:, l * 16 + 8:l * 16 + 16], self.norm_ffn[l, :].rearrange("(c p) -> p c", p=128), [], ["const"], slow=True)
        for t, n in ((self.ident, "c_ident"), (self.pm, "c_pm"), (self.identf, "c_identf"), (self.bones, "c_bones"),
                     (self.ones_f, "c_ones_f"), (self.ones_b, "c_ones_b")):
            self.dma("sp", t[:], self.cd[n][:, :], [], ["const"])
        self.pall = nc.alloc_psum_tensor("pall", [128, 4096], F32)
        self.pbig = self.pall[:, 0:2048]
        self.pb = [self.pall[:, i * 512:(i + 1) * 512] for i in range(8)]
        self.pbt = self.pall.bitcast(BF16)[:, 7 * 1024:8 * 1024]
        self.base_off = self.off
        self.S.barrier()

        if "W" in stages:
            self.phase_w()
        final = []
        for l in range(nl):
            self.layer_smalls(l)
            xsrc = self.x_in if l == 0 else self.xres
            if "P" in stages:
                self.phase_p(l, xsrc)
            if "D" in stages:
                self.mixer_d(l)
            if "A" in stages:
                self.mixer_a(l)
            if "C" in stages:
                self.mixer_c(l)
            if "B" in stages:
                self.mixer_b(l)
            if "M" in stages:
                final = self.phase_m(l, xsrc, last=(l == nl - 1))
        self.S.barrier()
        self.S.emit(final_waits=final)
        return nc

    def phase_reset(self):
        self.S.barrier()
        self.off = self.base_off

    def conv_chunks(self, l):
        jobs = [(self.w_in[l], self.wb_in[l], D_MODEL, IN_W, l * 16),
                (self.w_branch[l], self.wb_br[l], 2048, D_MODEL, None),
                (self.w_out[l], self.wb_out[l], D_MODEL, D_MODEL, None),
                (self.w_ff1[l], self.wb_f1[l], D_MODEL, 4096, l * 16 + 8),
                (self.w_ff2[l], self.wb_f2[l], 4096, D_MODEL, None)]
        out = []
        for src, dst, R, C, gc in jobs:
            for rc in range(R // 128):
                for c0 in range(0, C, 4096):
                    out.append((src, dst, rc, c0, min(4096, C - c0), gc))
        return out

    def conv_emit(self, job, bi, bo, n):
        src, dst, rc, c0, cw, gc = job
        i = n % 2
        gcol = self.gcol
        self.dma("sp", bi[i][:, 0:cw], src[rc * 128:(rc + 1) * 128, c0:c0 + cw], [], [f"wci{i}"])
        eng = "dve" if n % 2 else "pool"
        if gc is not None:
            self.ts(eng, bo[i][:, 0:cw], bi[i][:, 0:cw], gcol[:, gc + rc:gc + rc + 1], ALU.mult,
                    [f"wci{i}", "gcol"], [f"wco{i}"])
        else:
            self.cp(eng, bo[i][:, 0:cw], bi[i][:, 0:cw], [f"wci{i}"], [f"wco{i}"])
        self.dma("pool", dst[rc * 128:(rc + 1) * 128, c0:c0 + cw], bo[i][:, 0:cw], [f"wco{i}"], [])

    def phase_w(self):
        self.phase_reset()
        bi = [self.T(f"wci{i}", [128, 4096], F32) for i in range(2)]
        bo = [self.T(f"wco{i}", [128, 4096], BF16) for i in range(2)]
        for n, job in enumerate(self.conv_chunks(0)):
            self.conv_emit(job, bi, bo, n)

    def layer_smalls(self, l):
        self.phase_reset()
        sm = self.small
        lam_init = 0.8 - 0.6 * math.exp(-0.3 * l)
        lt = self.T("lamt", [128, 256], F32)
        pr = self.T("lampr", [128, 128], F32)
        sv = self.T("lamsv", [128, 4], F32)
        self.dma("sp", lt[:], self.diff_lambda[l:l + 1, :].partition_broadcast(128), [], ["lamt"])
        self.tt("dve", pr[:, 0:64], lt[:, 0:64], lt[:, 64:128], ALU.mult, ["lamt"], ["lampr"])
        self.tt("dve", pr[:, 64:128], lt[:, 128:192], lt[:, 192:256], ALU.mult, ["lamt"], ["lampr"])
        self.S.add("dve", lambda e: e.reduce_sum(out=sv[:, 0:2], in_=pr[:].rearrange("p (a b) -> p a b", a=2), axis=AX.X),
                   ["lampr"], ["lamsv"])
        self.act(sv[:, 2:4], sv[:, 0:2], AF.Exp, ["lamsv"], ["lamsv"])
        self.tt("dve", sm[:, 0:1], sv[:, 3:4], sv[:, 2:3], ALU.subtract, ["lamsv"], ["small"])
        self.ts("dve", sm[:, 0:1], sm[:, 0:1], -lam_init, ALU.add, ["small"], ["small"])
        self.dma("sp", sm[:, 1:2], self.diff_subln[l, :].rearrange("(p o) -> p o", o=1), [], ["small"])
        self.ts("dve", sm[:, 1:2], sm[:, 1:2], 1.0 - lam_init, ALU.mult, ["small"], ["small"])
        for i in range(2):
            for hf in range(2):
                self.dma("sp", sm[hf * 64:(hf + 1) * 64, 2 + i:3 + i], self.qk_norm[l, i, :].rearrange("(p o) -> p o", o=1), [], ["small"])
        self.dma("sp", sm[:, 8:40], self.b_gate[l, :].rearrange("(c p) -> p c", p=128), [], ["small"], slow=True)

    def phase_p(self, l, xsrc):
        self.phase_reset()
        S = self.Sq
        sm = self.small
        NT = S // 512
        xts = [self.T(f"xtP{i}", [128, 4, 1024], F32) for i in range(2)]
        hTs = [self.T(f"hTP{i}", [128, 8, 512], BF16) for i in range(2)]
        wts = [self.T(f"wtP{i}", [128, 8, 512], BF16) for i in range(3)]
        stg = [self.T(f"stgP{i}", [128, 512], BF16) for i in range(4)]
        vst = [self.T(f"vstP{i}", [128, 4, 512], BF16) for i in range(2)]
        rope = [self.T(f"ropeP{i}", [128, 2, 512], F32) for i in range(2)]
        dtmp = [self.T(f"dtmpP{i}", [128, 512], F32) for i in range(5)]
        plan = []
        for c0 in range(0, OFF_G, 512):
            if c0 in (OFF_AV, OFF_B + 1024, OFF_B + 1536 + 1024, OFF_B + 3072 + 1024, OFF_CV):
                vc = {OFF_AV: VC_A, OFF_B + 1024: VC_B, OFF_B + 2560: VC_B + 512, OFF_B + 4096: VC_B + 1024, OFF_CV: VC_C}[c0]
                plan.append((c0, 512, "tm", vc))
            elif c0 == OFF_DQ:
                plan.append((c0, 512, "dq", None))
            elif c0 == OFF_DK:
                plan.append((c0, 256, "dkv", None))
            else:
                plan.append((c0, 512, "fm", None))
        for c0 in range(OFF_G, IN_W, 512):
            plan.append((c0, 512, "gate", None))
        wn = 0
        cnt = {"sn": 0, "pbn": 0, "vn": 0}
        NPAIR = 2 if NT % 2 == 0 else 1

        STQ = os.environ.get("STQ", "pool")

        def emit(tt_, w, wk, c0, ncol, kind, vc):
            t0 = tt_ * 512
            ti = tt_ % 2
            hT = hTs[ti]
            hk = f"hTP{ti}"
            rp = rope[ti]
            rk = f"ropeP{ti}"
            if kind in ("fm", "gate", "dq", "dkv"):
                nfm = {"fm": 4, "gate": 4, "dq": 4, "dkv": 1}[kind]
                for j in range(nfm):
                    bank = self.pb[cnt["pbn"] % 4]
                    bk = f"pb{cnt['pbn'] % 4}"
                    cnt["pbn"] += 1
                    for kc in range(8):
                        self.mm(bank[:], w[:, kc, j * 128:(j + 1) * 128], hT[:, kc, :], kc == 0, kc == 7, [wk, hk], [bk])
                    st = stg[cnt["sn"] % 4]
                    sk = f"stgP{cnt['sn'] % 4}"
                    cnt["sn"] += 1
                    col = c0 + j * 128
                    if kind == "fm":
                        self.cp("act" if cnt["sn"] % 2 else "dve", st[:], bank[:], [bk], [sk])
                        self.dma(STQ, self.qkT[col:col + 128, t0:t0 + 512], st[:], [sk], [])
                    elif kind == "gate":
                        gc = (col - OFF_G) // 128
                        self.act(st[:], bank[:], AF.Sigmoid, [bk, "small"], [sk], bias=sm[:, 8 + gc:9 + gc])
                        self.dma(STQ, self.gT[col - OFF_G:col - OFF_G + 128, t0:t0 + 512], st[:], [sk], [])
                    else:
                        gi = 2 if kind == "dq" else 3
                        sq, rst, xg, t1, t2 = dtmp
                        self.act(sq[:], bank[:], AF.Square, [bk], ["dsq"])
                        self.mm(self.pb[4][:], self.bones[:], sq[:], True, True, ["dsq", "const"], ["pb4"])
                        self.ts("dve", rst[:], self.pb[4][:], 1.0 / 64, ALU.mult, ["pb4"], ["drst"], s2=EPS, op1=ALU.add)
                        self.act(rst[:], rst[:], AF.Sqrt, ["drst"], ["drst"])
                        self.recip(rst[:], rst[:], ["drst"], ["drst"])
                        self.ts("dve", xg[:], bank[:], sm[:, gi:gi + 1], ALU.mult, [bk, "small"], ["dxg"])
                        self.mm(self.pb[5][:], self.pm[:], xg[:], True, True, ["dxg", "const"], ["pb5"])
                        self.tt("pool", t1[:], xg[:], rp[:, 0, :], ALU.mult, ["dxg", rk], ["dt1"])
                        self.tt("dve", t2[:], self.pb[5][:], rp[:, 1, :], ALU.mult, ["pb5", rk], ["dt2"])
                        self.tt("pool", t1[:], t1[:], t2[:], ALU.add, ["dt1", "dt2"], ["dt1"])
                        self.tt("dve", st[:], t1[:], rst[:], ALU.mult, ["dt1", "drst"], [sk])
                        self.dma(STQ, self.qkT[col:col + 128, t0:t0 + 512], st[:], [sk], [])
            if kind in ("tm", "dkv"):
                if kind == "tm":
                    wc0, nvc, vcol = 0, 512, vc
                else:
                    wc0, nvc, vcol = 128, 128, VC_D
                vs = vst[cnt["vn"] % 2]
                vk = f"vstP{cnt['vn'] % 2}"
                cnt["vn"] += 1
                for s_ in range(4):
                    bank = self.pb[cnt["pbn"] % 4]
                    bk = f"pb{cnt['pbn'] % 4}"
                    cnt["pbn"] += 1
                    for kc in range(8):
                        self.mm(bank[:, 0:nvc], hT[:, kc, s_ * 128:(s_ + 1) * 128], w[:, kc, wc0:wc0 + nvc], kc == 0, kc == 7, [wk, hk], [bk])
                    self.cp("act" if s_ % 2 else "dve", vs[:, s_, 0:nvc], bank[:, 0:nvc], [bk], [vk])
                self.dma(STQ, self.vtm[t0:t0 + 512, vcol:vcol + nvc].rearrange("(s p) c -> p s c", p=128),
                         vs[:, :, 0:nvc], [vk], [])

        for p0 in range(0, NT, NPAIR):
            tiles = list(range(p0, p0 + NPAIR))
            for tt_ in tiles:
                t0 = tt_ * 512
                ti = tt_ % 2
                self.dma("sp", xts[ti][:], xsrc[t0:t0 + 512, :].rearrange("(s p) d -> p s d", p=128), [], [f"xtP{ti}"])
                self.dma("sp", rope[ti][:], self.cd["c_rope"][:, :, t0:t0 + 512].rearrange("a p t -> p a t"), [], [f"ropeP{ti}"])
            for tt_ in tiles:
                ti = tt_ % 2
                mark = self.off
                self.rmsnorm_T_keys(xts[ti], hTs[ti], f"xtP{ti}", f"hTP{ti}")
                self.off = mark
            for (c0, ncol, kind, vc) in plan:
                w = wts[wn % 3]
                wk = f"wtP{wn % 3}"
                wn += 1
                self.dma("sp", w[:, :, 0:ncol], self.wb_in[l][:, c0:c0 + ncol].rearrange("(k p) n -> p k n", p=128), [], [wk])
                for tt_ in tiles:
                    emit(tt_, w, wk, c0, ncol, kind, vc)

    def rmsnorm_T_keys(self, xt, hT, xk, hk, inv_d=1.0 / D_MODEL):
        ss = self.T("ss", [128, 8], F32)
        junk = self.T("junk", [128, 1024], BF16)
        hn = self.T("hn", [128, 4, 1024], BF16)
        self.memset("dve", ss[:], 0.0, ["ss"])
        for s in range(4):
            self.act(junk[:], xt[:, s, :], AF.Square, [xk, "ss"], ["junk", "ss"], accum=ss[:, s:s + 1])
        self.ts("dve", ss[:, 4:8], ss[:, 0:4], inv_d, ALU.mult, ["ss"], ["ss"], s2=EPS, op1=ALU.add)
        self.act(ss[:, 4:8], ss[:, 4:8], AF.Sqrt, ["ss"], ["ss"])
        self.recip(ss[:, 4:8], ss[:, 4:8], ["ss"], ["ss"])
        for s in range(4):
            self.ts("dve" if s % 2 else "pool", hn[:, s, :], xt[:, s, :], ss[:, 4 + s:5 + s], ALU.mult,
                    [xk, "ss"], [f"hn{s}"])
        for c in range(8):
            for s in range(4):
                self.tr(self.pbt[:, s * 128:(s + 1) * 128], hn[:, s, c * 128:(c + 1) * 128], self.ident[:],
                        [f"hn{s}", "const"], ["pb7"])
            self.cp("act" if c % 2 else "dve", hT[:, c, :], self.pbt[:, 0:512], ["pb7"], [hk])

    def phase_m(self, l, xsrc, last):
        self.phase_reset()
        S = self.Sq
        NT = S // 512
        xts = [self.T(f"xtM{i}", [128, 4, 1024], F32) for i in range(2)]
        yts = [self.T(f"ytM{i}", [128, 4, 512], BF16) for i in range(2)]
        gts = [self.T(f"gtM{i}", [128, 8, 512], BF16) for i in range(2)]
        wts = [self.T(f"wtM{i}", [128, 8, 512], BF16) for i in range(3)]
        macc = self.T("macc", [128, 8, 512], F32)
        mtmp = [self.T(f"mtmp{i}", [128, 512], F32) for i in range(2)]
        mT = self.T("mT", [128, 8, 512], BF16)
        h2T = self.T("h2T", [128, 8, 512], BF16)
        uT = self.T("uT", [128, 32, 512], BF16)
        gfin = None
        if last:
            gfin = self.T("gfin", [128, 1024], F32)
            self.dma("sp", gfin[:], self.norm_final[0:1, :].partition_broadcast(128), [], ["gfin"])
        wn = 0
        pbn = 0
        yn = 0
        finals = []
        for tt_ in range(NT):
            t0 = tt_ * 512
            xt = xts[tt_ % 2]
            xk = f"xtM{tt_ % 2}"
            self.dma("sp", xt[:], xsrc[t0:t0 + 512, :].rearrange("(s p) d -> p s d", p=128), [], [xk])
            for n in range(4):
                yt = yts[yn % 2]
                gt = gts[yn % 2]
                yk, gk = f"ytM{yn % 2}", f"gtM{yn % 2}"
                yn += 1
                w = wts[wn % 3]
                wk = f"wtM{wn % 3}"
                wn += 1
                self.dma("sp", yt[:], self.yT[n, :, t0:t0 + 512].rearrange("(k p) t -> p k t", p=128), [], [yk])
                self.dma("sp", gt[:], self.gT[n * 1024:(n + 1) * 1024, t0:t0 + 512].rearrange("(k p) t -> p k t", p=128), [], [gk])
                wv = w[:].rearrange("p k n -> p (k n)").rearrange("p (k n) -> p k n", k=4)
                self.dma("sp", wv, self.wb_br[l][n * 512:(n + 1) * 512, :].rearrange("(k p) n -> p k n", p=128), [], [wk])
                for oc in range(8):
                    bank = self.pb[pbn % 3]
                    bk = f"pb{pbn % 3}"
                    pbn += 1
                    for kc in range(4):
                        self.mm(bank[:], wv[:, kc, oc * 128:(oc + 1) * 128], yt[:, kc, :], kc == 0, kc == 3, [wk, yk], [bk])
                    eng = "dve" if oc % 2 else "pool"
                    if n == 0:
                        self.tt("dve", macc[:, oc, :], bank[:], gt[:, oc, :], ALU.mult, [bk, gk], [f"macc{oc}"])
                    elif n < 3:
                        mt = mtmp[oc % 2]
                        self.tt("dve", mt[:], bank[:], gt[:, oc, :], ALU.mult, [bk, gk], [f"mtmp{oc % 2}"])
                        self.tt("pool", macc[:, oc, :], macc[:, oc, :], mt[:], ALU.add, [f"mtmp{oc % 2}", f"macc{oc}"], [f"macc{oc}"])
                    else:
                        mt = mtmp[oc % 2]
                        self.tt("dve", mt[:], bank[:], gt[:, oc, :], ALU.mult, [bk, gk], [f"mtmp{oc % 2}"])
                        self.tt("pool", mT[:, oc, :], macc[:, oc, :], mt[:], ALU.add, [f"mtmp{oc % 2}", f"macc{oc}"], [f"mT{oc}"])
            mTk = [f"mT{oc}" for oc in range(8)]
            for half in range(2):
                w = wts[wn % 3]
                wk = f"wtM{wn % 3}"
                wn += 1
                self.dma("sp", w[:], self.wb_out[l][:, half * 512:(half + 1) * 512].rearrange("(k p) n -> p k n", p=128), [], [wk])
                for s in range(4):
                    bank = self.pb[3 + pbn % 2]
                    bk = f"pb{3 + pbn % 2}"
                    pbn += 1
                    for kc in range(8):
                        self.mm(bank[:], mT[:, kc, s * 128:(s + 1) * 128], w[:, kc, :], kc == 0, kc == 7, [wk, f"mT{kc}"], [bk])
                    self.tt("dve", xt[:, s, half * 512:(half + 1) * 512], xt[:, s, half * 512:(half + 1) * 512], bank[:], ALU.add,
                            [bk, xk], [xk])
            mark = self.off
            self.rmsnorm_T_keys(xt, h2T, xk, "h2T")
            self.off = mark
            for ft in range(8):
                w = wts[wn % 3]
                wk = f"wtM{wn % 3}"
                wn += 1
                self.dma("sp", w[:], self.wb_f1[l][:, ft * 512:(ft + 1) * 512].rearrange("(k p) n -> p k n", p=128), [], [wk])
                for j in range(4):
                    fc = ft * 4 + j
                    bank = self.pb[pbn % 3]
                    bk = f"pb{pbn % 3}"
                    pbn += 1
                    for kc in range(8):
                        self.mm(bank[:], w[:, kc, j * 128:(j + 1) * 128], h2T[:, kc, :], kc == 0, kc == 7, [wk, "h2T"], [bk])
                    rl = mtmp[fc % 2]
                    self.act(rl[:], bank[:], AF.Relu, [bk], [f"mtmp{fc % 2}"])
                    self.tt("dve" if fc % 2 else "pool", uT[:, fc, :], rl[:], rl[:], ALU.mult, [f"mtmp{fc % 2}"], [f"uT{fc}"])
            for half in range(2):
                banks = [(self.pb[3 + s], f"pb{3 + s}") for s in range(4)]
                for fg in range(4):
                    w = wts[wn % 3]
                    wk = f"wtM{wn % 3}"
                    wn += 1
                    self.dma("sp", w[:], self.wb_f2[l][fg * 1024:(fg + 1) * 1024, half * 512:(half + 1) * 512].rearrange("(k p) n -> p k n", p=128),
                             [], [wk])
                    for s in range(4):
                        bank, bk = banks[s]
                        for kc in range(8):
                            fc = fg * 8 + kc
                            self.mm(bank[:], uT[:, fc, s * 128:(s + 1) * 128], w[:, kc, :], fg == 0 and kc == 0, fg == 3 and kc == 7,
                                    [wk, f"uT{fc}"], [bk])
                for s in range(4):
                    bank, bk = banks[s]
                    self.tt("dve", xt[:, s, half * 512:(half + 1) * 512], xt[:, s, half * 512:(half + 1) * 512], bank[:], ALU.add,
                            [bk, xk], [xk])
            if not last:
                self.dma("pool", self.xres[t0:t0 + 512, :].rearrange("(s p) d -> p s d", p=128), xt[:], [xk], [])
            else:
                ss = self.T("ssF", [128, 8], F32)
                junk = self.T("junkF", [128, 1024], BF16)
                self.memset("dve", ss[:], 0.0, ["ssF"])
                for s in range(4):
                    self.act(junk[:], xt[:, s, :], AF.Square, [xk, "ssF"], ["junkF", "ssF"], accum=ss[:, s:s + 1])
                self.ts("dve", ss[:, 4:8], ss[:, 0:4], 1.0 / D_MODEL, ALU.mult, ["ssF"], ["ssF"], s2=EPS, op1=ALU.add)
                self.act(ss[:, 4:8], ss[:, 4:8], AF.Sqrt, ["ssF"], ["ssF"])
                self.recip(ss[:, 4:8], ss[:, 4:8], ["ssF"], ["ssF"])
                for s in range(4):
                    self.stt("dve", xt[:, s, :], xt[:, s, :], ss[:, 4 + s:5 + s], gfin[:], ALU.mult, ALU.mult,
                             [xk, "ssF", "gfin"], [xk])
                finals.append(self.dma("pool", self.out[t0:t0 + 512, :].rearrange("(s p) d -> p s d", p=128), xt[:], [xk], []))
                self.off = mark
        return finals

    def mixer_d(self, l):
        self.phase_reset()
        S = self.Sq
        NKB = S // 128
        NQT = S // 512
        NSB = 6
        LA = 5
        kT = self.T("dkT", [128, S], BF16)
        va = self.T("dva", [128, NKB, 128], BF16)
        qts = [self.T(f"dq{i}", [128, 512], BF16) for i in range(3)]
        pts = [self.T(f"dpt{i}", [128, 512], BF16) for i in range(6)]
        rec = [self.T(f"drec{i}", [128, 512], F32) for i in range(2)]
        yst = [self.T(f"dyst{i}", [64, 512], BF16) for i in range(2)]
        self.memset("dve", kT[64:128, :], 0.0, ["dkT"])
        for i in range(3):
            self.memset("pool", qts[i][64:128, :], 0.0, [f"dq{i}"])
        self.memset("pool", va[:, :, 64:128], 1.0, ["dva"])
        bg = self.conv_chunks(l + 1) if (l + 1 < self.nl and "W" in self.stages) else []
        if bg:
            bi = [self.T(f"wci{i}", [128, 4096], F32) for i in range(2)]
            bo = [self.T(f"wco{i}", [128, 4096], BF16) for i in range(2)]
        bgn = 0
        step = max(1, (2 * 4 * NQT * NKB) // (len(bg) + 1)) if bg else 0
        tot = 0
        gq = 0
        for g in range(2):
            self.dma("sp", kT[0:64, :], self.qkT[OFF_DK + g * 64:OFF_DK + (g + 1) * 64, :], [], ["dkT"])
            self.dma("sp", va[:, :, 0:64], self.vtm[:, VC_D + g * 64:VC_D + (g + 1) * 64].rearrange("(k p) c -> p k c", p=128), [], ["dva"])
            items = [(h, qt, kb) for h in range(4 * g, 4 * g + 4) for qt in range(NQT) for kb in range(NKB)]
            n = len(items)

            def qk(i, gq=gq):
                h, qt, kb = items[i]
                qi = gq + i // NKB
                q = qts[qi % 3]
                qk_ = f"dq{qi % 3}"
                if kb == 0:
                    for i2 in ([i, i + NKB] if i == 0 else [i + NKB]):
                        if i2 >= n:
                            continue
                        h2, qt2, _ = items[i2]
                        qj = gq + i2 // NKB
                        self.dma("sp", qts[qj % 3][0:64, :], self.qkT[OFF_DQ + h2 * 64:OFF_DQ + (h2 + 1) * 64, qt2 * 512:(qt2 + 1) * 512], [], [f"dq{qj % 3}"])
                self.mm(self.pb[i % NSB][:], kT[:, kb * 128:(kb + 1) * 128], q[:], True, True, ["dkT", qk_], [f"pb{i % NSB}"])

            def ex_av(i, gq=gq):
                h, qt, kb = items[i]
                qi = gq + i // NKB
                acc = self.pb[6 + qi % 2]
                ak = f"pb{6 + qi % 2}"
                pt = pts[i % 6]
                pk = f"dpt{i % 6}"
                self.act(pt[:], self.pb[i % NSB][:], AF.Exp, [f"pb{i % NSB}"], [pk], scale=0.125)
                self.mm(acc[:], va[:, kb, :], pt[:], kb == 0, kb == NKB - 1, ["dva", pk], [ak])
                if kb == NKB - 1:
                    ys = yst[qi % 2]
                    yk = f"dyst{qi % 2}"
                    rc = rec[qi % 2]
                    rk = f"drec{qi % 2}"
                    self.recip(rc[64:128, :], acc[64:128, :], [ak], [rk])
                    self.tt("dve", ys[:], acc[0:64, :], rc[64:128, :], ALU.mult, [ak, rk], [yk])
                    self.dma("pool", self.yT[3, h * 64:(h + 1) * 64, qt * 512:(qt + 1) * 512], ys[:], [yk], [])

            for i in range(n + LA):
                if i < n:
                    qk(i)
                if i >= LA:
                    ex_av(i - LA)
                tot += 1
                if bg and tot % step == 0 and bgn < len(bg):
                    self.conv_emit(bg[bgn], bi, bo, bgn)
                    bgn += 1
            gq += n // NKB
        while bgn < len(bg):
            self.conv_emit(bg[bgn], bi, bo, bgn)
            bgn += 1

    def mixer_a(self, l):
        self.phase_reset()
        S = self.Sq
        sm = self.small
        NKB = S // 128
        NQT = S // 512
        NSB = 4
        LA = 3
        kTs = [self.T(f"akT{c}", [128, S], BF16) for c in range(2)]
        va = self.T("ava", [128, NKB, 128], BF16)
        qts = [[[self.T(f"aq{i}{c}{v}", [128, 512], BF16) for v in range(2)] for c in range(2)] for i in range(2)]
        pts = [self.T(f"apt{i}", [128, 512], BF16) for i in range(5)]
        for c in range(2):
            self.memset("dve", kTs[c][64:128, :], 0.0, [f"akT{c}"])
            for i in range(2):
                for v in range(2):
                    self.memset("pool", qts[i][c][v][64:128, :], 0.0, [f"aq{i}{c}{v}"])
        cdt = self.T("acd", [128, 4, 512], F32)
        dgt = [self.T(f"adg{i}", [128, 512], F32) for i in range(3)]
        r0 = [self.T(f"ar0{i}", [128, 512], F32) for i in range(2)]
        t0s = [self.T(f"at0{i}", [128, 512], F32) for i in range(2)]
        t1_ = self.T("at1", [128, 512], F32)
        sq = self.T("asq", [128, 512], F32)
        rs = self.T("ars", [128, 512], F32)
        yst = [self.T(f"ayst{i}", [128, 512], BF16) for i in range(2)]
        pssum = self.pb[7]
        gq = 0
        for h in range(4):
            for c in range(2):
                r = OFF_AK + (h * 2 + c) * 64
                self.dma("sp", kTs[c][0:64, :], self.qkT[r:r + 64, :], [], [f"akT{c}"])
                self.dma("sp", kTs[c][64:69, :], self.cd["c_kaug"][h, :, :], [], [f"akT{c}"])
            self.dma("sp", va[:], self.vtm[:, VC_A + h * 128:VC_A + (h + 1) * 128].rearrange("(k p) c -> p k c", p=128), [], ["ava"])
            self.dma("sp", cdt[:], self.cd["c_cd"][:, h, :, :], [], ["acd"])
            items = [(qt, c, kb) for qt in range(NQT) for c in range(2) for kb in range(NKB)]
            n = len(items)

            def qk(i, h=h, gq=gq):
                qt, c, kb = items[i]
                qi = (gq + qt) % 2
                if c == 0 and kb == 0:
                    for qt2 in ([0, 1] if qt == 0 else [qt + 1]):
                        if qt2 >= NQT:
                            continue
                        qj = (gq + qt2) % 2
                        for c2 in range(2):
                            r = OFF_AQ + (h * 2 + c2) * 64
                            for v in range(2):
                                self.dma("sp", qts[qj][c2][v][0:64, :], self.qkT[r:r + 64, qt2 * 512:(qt2 + 1) * 512], [], [f"aq{qj}{c2}{v}"])
                                self.dma("sp", qts[qj][c2][v][64:69, :], self.cd["c_qaug"][h, v, :, qt2 * 512:(qt2 + 1) * 512], [], [f"aq{qj}{c2}{v}"])
                rel = kb - qt * 4
                v = 1 if rel > 3 else 0
                self.mm(self.pb[i % NSB][:], kTs[c][:, kb * 128:(kb + 1) * 128], qts[qi][c][v][:], True, True,
                        [f"akT{c}", f"aq{qi}{c}{v}"], [f"pb{i % NSB}"])

            def ex_av(i, h=h, gq=gq):
                qt, c, kb = items[i]
                gi = gq * 2 + i // NKB
                acc, ak = self.pb[4 + 2 * (gi % 2)], f"pb{4 + 2 * (gi % 2)}"
                den, dk_ = self.pb[5 + 2 * (gi % 2)], f"pb{5 + 2 * (gi % 2)}"
                sb, sk = self.pb[i % NSB], f"pb{i % NSB}"
                pt, pk = pts[i % 5], f"apt{i % 5}"
                rel = kb - qt * 4
                if rel < 0 or rel > 3:
                    self.act(pt[:], sb[:], AF.Exp, [sk], [pk], scale=0.125)
                else:
                    dg, dgk = dgt[i % 3], f"adg{i % 3}"
                    self.ts("dve", dg[:], sb[:], 0.125, ALU.mult, [sk], [dgk])
                    self.tt("dve", dg[:], dg[:], cdt[:, rel, :], ALU.add, [dgk, "acd"], [dgk])
                    self.act(pt[:], dg[:], AF.Exp, [dgk], [pk])
                self.mm(acc[:], va[:, kb, :], pt[:], kb == 0, kb == NKB - 1, ["ava", pk], [ak])
                self.mm(den[:], self.ones_b[:], pt[:], kb == 0, kb == NKB - 1, ["const", pk], [dk_])
                if kb != NKB - 1:
                    return
                qg = gq + qt
                t0_, t0k = t0s[qg % 2], f"at0{qg % 2}"
                rr, rrk = r0[gi % 2], f"ar0{gi % 2}"
                self.recip(rr[:], den[:], [dk_], [rrk])
                if c == 0:
                    self.tt("dve", t0_[:], acc[:], rr[:], ALU.mult, [ak, rrk], [t0k])
                    return
                self.tt("dve", t1_[:], acc[:], rr[:], ALU.mult, [ak, rrk], ["at1"])
                self.stt("dve", t0_[:], t1_[:], sm[:, 0:1], t0_[:], ALU.mult, ALU.add, [t0k, "at1", "small"], [t0k])
                self.act(sq[:], t0_[:], AF.Square, [t0k], ["asq"])
                self.mm(pssum[:], self.ones_f[:], sq[:], True, True, ["asq", "const"], ["pb7"])
                self.ts("dve", rs[:], pssum[:], 1.0 / 128, ALU.mult, ["pb7"], ["ars"], s2=EPS, op1=ALU.add)
                self.act(rs[:], rs[:], AF.Sqrt, ["ars"], ["ars"])
                self.recip(rs[:], rs[:], ["ars"], ["ars"])
                ys, yk = yst[qg % 2], f"ayst{qg % 2}"
                self.stt("dve", ys[:], t0_[:], sm[:, 1:2], rs[:], ALU.mult, ALU.mult, [t0k, "ars", "small"], [yk])
                self.dma("pool", self.yT[0, h * 128:(h + 1) * 128, qt * 512:(qt + 1) * 512], ys[:], [yk], [])

            for i in range(n + LA):
                if i < n:
                    qk(i)
                if i >= LA:
                    ex_av(i - LA)
            gq += NQT

    def local_groups(self, qT, kT, va, keys, groups, mask_of, out_fn, tagp):
        pts = self._lp
        tmp = self._lt
        G = len(groups)
        state = {}

        def pass1(gi):
            qbs, rels = groups[gi]
            nq = len(qbs)
            used = []
            for ri, (r, kbs, m) in enumerate(rels):
                js = [j for j in range(nq) if kbs[j] is not None]
                if not js:
                    continue
                j0, j1 = js[0], js[-1] + 1
                assert js == list(range(j0, j1))
                sb = self.pb[self._sn % 4]
                sk = f"pb{self._sn % 4}"
                self._sn += 1
                for j in js:
                    self.mm(sb[:, j * 128:(j + 1) * 128], kT[:, kbs[j] * 128:(kbs[j] + 1) * 128], qT[:, qbs[j] * 128:(qbs[j] + 1) * 128],
                            True, True, keys, [sk])
                t = tmp[self._pn % 4]
                tk = f"{tagp}lt{self._pn % 4}"
                self._pn += 1
                pi = (gi % 2) * 9 + ri
                pt = pts[pi]
                pk = f"{tagp}lp{pi}"
                nj = j1 - j0
                mv = m.unsqueeze(1).broadcast_to([128, nj, 128]) if nj > 1 else m
                tv = t[:, j0 * 128:j1 * 128].rearrange("p (j q) -> p j q", j=nj) if nj > 1 else t[:, j0 * 128:j1 * 128]
                sv = sb[:, j0 * 128:j1 * 128].rearrange("p (j q) -> p j q", j=nj) if nj > 1 else sb[:, j0 * 128:j1 * 128]
                self.stt("dve", tv, sv, 0.125, mv, ALU.mult, ALU.add, [sk, "lmask"], [tk])
                self.act(pt[:, j0 * 128:j1 * 128], t[:, j0 * 128:j1 * 128], AF.Exp, [tk], [pk])
                used.append((ri, kbs, pt, pk))
            state[gi] = used

        def pass2(gi):
            qbs, rels = groups[gi]
            nq = len(qbs)
            acc = self.pb[4 + gi % 2]
            ak = f"pb{4 + gi % 2}"
            used = state.pop(gi)
            for j in range(nq):
                mine = [(ri, kbs, pt, pk) for (ri, kbs, pt, pk) in used if kbs[j] is not None]
                for n_, (ri, kbs, pt, pk) in enumerate(mine):
                    self.mm(acc[:, j * 128:(j + 1) * 128], va[:, kbs[j], :], pt[:, j * 128:(j + 1) * 128],
                            n_ == 0, n_ == len(mine) - 1, keys + [pk], [ak])
            out_fn(gi, qbs, acc, ak)

        for gi in range(G + 1):
            if gi < G:
                pass1(gi)
            if gi >= 1:
                pass2(gi - 1)

    def local_init(self, tagp):
        self._lp = [self.T(f"{tagp}lp{i}", [128, 512], BF16) for i in range(18)]
        self._lt = [self.T(f"{tagp}lt{i}", [128, 512], F32) for i in range(4)]
        self._sn = 0
        self._pn = 0

    def mixer_c(self, l):
        self.phase_reset()
        S = self.Sq
        M = S // 128
        meta = self.cmeta
        NV = len(meta)
        qT = self.T("cqT", [128, S], BF16)
        kT = self.T("ckT", [128, S], BF16)
        self.memset("dve", qT[64:128, :], 0.0, ["cqT"])
        self.memset("pool", kT[64:128, :], 0.0, ["ckT"])
        va = self.T("cva", [128, M, 128], BF16)
        mcv = self.T("cmcv", [128, NV, 128], F32)
        mcm = self.T("cmcm", [128, NV, 128], F32)
        TT = self.T("cTT", [128, 15, 64], F32)
        rp = self.T("crp", [120, 128], F32)
        rec = self.T("crec", [128, 512], F32)
        yst = [self.T(f"cyst{i}", [64, 512], BF16) for i in range(2)]
        self.local_init("c")
        self.memset("pool", va[:, :, 64:128], 1.0, ["cva"])
        self.dma("sp", mcv[:], self.cd["c_mcv"][:, :, :], [], ["cmcv"])
        self.memset("dve", rp[:], 0.0, ["crp"])
        self.dma("sp", rp[:, 48:79], self.na_rpb[l, :, :], [], ["crp"])
        self.dma("sp", self.rpbr[:, :], rp[:], ["crp"], ["rpbr"])
        Hd = self.T("cHd", [64, 15, 2, 64], F32)
        j64 = self.T("cj64", [64, 64], F32)
        self.dma("sp", j64[:], self.cd["c_j64"][:, :], [], ["cj64"])
        for h in range(8):
            self.dma("sp", qT[0:64, :], self.qkT[OFF_CQ + h * 64:OFF_CQ + (h + 1) * 64, :], [], ["cqT"])
            self.dma("sp", kT[0:64, :], self.qkT[OFF_CK + h * 64:OFF_CK + (h + 1) * 64, :], [], ["ckT"])
            self.dma("sp", va[:, :, 0:64], self.vtm[:, VC_C + h * 64:VC_C + (h + 1) * 64].rearrange("(k p) c -> p k c", p=128), [], ["cva"])
            for a in range(2):
                src = bass.AP(tensor=self.rpbr.tensor, offset=h * 15 * 128, ap=[[1, 64], [128, 15], [1, 64]])
                self.dma("sp", Hd[:, :, a, :], src, ["rpbr"], ["cHd"])
            for i in range(15):
                bank = self.pb[4 + i // 8]
                bk = f"pb{4 + i // 8}"
                self.mm(bank[:, (i % 8) * 64:(i % 8 + 1) * 64], Hd[:, i, :, :].rearrange("p a k -> p (a k)"), j64[:], True, True,
                        ["cHd", "cj64"], [bk])
            self.cp("dve", TT[:, 0:8, :], self.pb[4][:].rearrange("p (i c) -> p i c", i=8), ["pb4"], ["cTT"])
            self.cp("dve", TT[:, 8:15, :], self.pb[5][:, 0:448].rearrange("p (i c) -> p i c", i=7), ["pb5"], ["cTT"])
            for vi, (cls, dr) in enumerate(meta):
                for a in range(2):
                    for b in range(2):
                        ri = 2 * dr + a - b + 7
                        o = mcm[a * 64:(a + 1) * 64, vi, b * 64:(b + 1) * 64]
                        i0 = mcv[a * 64:(a + 1) * 64, vi, b * 64:(b + 1) * 64]
                        if 0 <= ri <= 14:
                            self.tt("pool", o, i0, TT[a * 64:(a + 1) * 64, ri, :], ALU.add, ["cmcv", "cTT"], ["lmask"])
                        else:
                            self.cp("pool", o, i0, ["cmcv"], ["lmask"])
            midx = {(cls, dr): vi for vi, (cls, dr) in enumerate(meta)}
            groups = []

            def mk(qbs, cls):
                rels = []
                for dr in range(-4, 5):
                    if (cls, dr) not in midx:
                        continue
                    kbs = [qb + dr if 0 <= qb + dr < M else None for qb in qbs]
                    rels.append((dr, kbs, mcm[:, midx[(cls, dr)], :]))
                return (qbs, rels)
            groups.append(mk([0], "m0"))
            groups.append(mk([1], "m1"))
            ints = list(range(2, M - 2))
            for i in range(0, len(ints), 4):
                groups.append(mk(ints[i:i + 4], "int"))
            groups.append(mk([M - 2], "mL2"))
            groups.append(mk([M - 1], "mL1"))

            def outf(gi, qbs, acc, ak, h=h):
                n = len(qbs) * 128
                ys = yst[gi % 2]
                yk = f"cyst{gi % 2}"
                self.recip(rec[64:128, 0:n], acc[64:128, 0:n], [ak], ["crec"])
                self.tt("dve", ys[:, 0:n], acc[0:64, 0:n], rec[64:128, 0:n], ALU.mult, [ak, "crec"], [yk])
                self.dma("pool", self.yT[2, h * 64:(h + 1) * 64, qbs[0] * 128:qbs[0] * 128 + n], ys[:, 0:n], [yk], [])
            import os
            if os.environ.get("CDBG", "0") != "1":
                self.local_groups(qT, kT, va, ["cqT", "ckT", "cva"], groups, None, outf, "c")

    def mixer_b(self, l):
        self.phase_reset()
        S = self.Sq
        NB = S // 128
        qn_ = self.T("bqn", [128, S], BF16)
        kn_ = self.T("bkn", [128, S], BF16)
        qp = self.T("bqp", [128, S], BF16)
        kp = self.T("bkp", [128, S], BF16)
        for t_, k_ in ((qn_, "bqn"), (kn_, "bkn"), (qp, "bqp"), (kp, "bkp")):
            self.memset("dve", t_[64:128, :], 0.0, [k_])
        va = self.T("bva", [128, NB, 128], BF16)
        accB = self.T("baccB", [128, S], F32)
        mk_ = self.T("bmk", [128, 3, 128], F32)
        rec = self.T("brec", [128, 512], F32)
        yst = [self.T(f"byst{i}", [64, 512], BF16) for i in range(2)]
        self.local_init("b")
        self.memset("pool", va[:, :, 64:128], 1.0, ["bva"])
        for h in range(8):
            for g, (win, dil) in enumerate(B_PATTERNS):
                L = S // dil
                nbc = L // 128
                base = OFF_B + g * 1536
                self.dma("sp", qn_[0:64, :], self.qkT[base + h * 64:base + (h + 1) * 64, :], [], ["bqn"])
                self.dma("sp", kn_[0:64, :], self.qkT[base + 512 + h * 64:base + 512 + (h + 1) * 64, :], [], ["bkn"])
                self.dma("sp", mk_[:], self.cd["c_mb"][g * 8 + h, :, :].rearrange("p (r q) -> p r q", r=3), [], ["lmask"])
                vcol = VC_B + g * 512 + h * 64
                if dil == 1:
                    qq, kk = qn_, kn_
                    keys = ["bqn", "bkn", "bva"]
                    self.dma("sp", va[:, :, 0:64], self.vtm[:, vcol:vcol + 64].rearrange("(k p) c -> p k c", p=128), [], ["bva"])
                else:
                    qq, kk = qp, kp
                    keys = ["bqp", "bkp", "bva"]
                    self.cp("pool", qp[0:64, :].rearrange("p (j i) -> p j i", j=dil), qn_[0:64, :].rearrange("p (i j) -> p j i", j=dil), ["bqn"], ["bqp"])
                    self.cp("dve", kp[0:64, :].rearrange("p (j i) -> p j i", j=dil), kn_[0:64, :].rearrange("p (i j) -> p j i", j=dil), ["bkn"], ["bkp"])
                    for j in range(dil):
                        src = self.vtm[:, vcol:vcol + 64].rearrange("(k p j) c -> j p k c", p=128, j=dil)[j]
                        self.dma("sp", va[:, j * nbc:(j + 1) * nbc, 0:64], src, [], ["bva"])
                groups = []
                gs = min(4, nbc)
                for j in range(dil):
                    for b0 in range(0, nbc, gs):
                        qbs = [j * nbc + b0 + i for i in range(gs)]
                        rels = []
                        for r in (-1, 0, 1):
                            kbs = [(qb + r) if 0 <= (qb - j * nbc + r) < nbc else None for qb in qbs]
                            rels.append((r, kbs, mk_[:, r + 1, :]))
                        groups.append((qbs, rels))

                def outf(gi, qbs, acc, ak, g=g, dil=dil, nbc=nbc):
                    n = len(qbs) * 128
                    j = qbs[0] // nbc
                    i0 = (qbs[0] - j * nbc) * 128
                    if dil == 1:
                        dst = accB[:, i0:i0 + n]
                        self.cp("dve", dst, acc[:, 0:n], [ak], ["baccB"])
                    else:
                        dst = accB[:, i0 * dil + j:(i0 + n - 1) * dil + j + 1:dil]
                        self.tt("dve", dst, dst, acc[:, 0:n], ALU.add, [ak, "baccB"], ["baccB"])
                self.local_groups(qq, kk, va, keys, groups, None, outf, "b")
            for ct in range(S // 512):
                ys = yst[ct % 2]
                yk = f"byst{ct % 2}"
                self.recip(rec[64:128, :], accB[64:128, ct * 512:(ct + 1) * 512], ["baccB"], ["brec"])
                self.cp_psum_num(accB, ct)
                self.tt("dve", ys[:], self.pb[6][0:64, :], rec[64:128, :], ALU.mult, ["pb6", "brec"], [yk])
                self.dma("pool", self.yT[1, h * 64:(h + 1) * 64, ct * 512:(ct + 1) * 512], ys[:], [yk], [])

    def cp_psum_num(self, accB, ct):
        self.mm(self.pb[6][0:64, :], self.identf[0:64, 0:64], accB[0:64, ct * 512:(ct + 1) * 512], True, True, ["baccB", "const"], ["pb6"])


_CACHE = {}


def _get_builder(S):
    if S not in _CACHE:
        b = Builder(S)
        b.build()
        _CACHE[S] = b
    return _CACHE[S]


def make_in_map(b, x1, inputs):
    m = {"x": np.ascontiguousarray(x1, dtype=np.float32)}
    m["norm_mix"] = inputs["norm_mix"]
    m["w_in"] = inputs["w_in"]
    m["b_gate"] = inputs["b_gate"]
    m["diff_lambda"] = inputs["diff_lambda"].reshape(DEPTH, 256)
    m["diff_subln"] = inputs["diff_subln"]
    m["na_rpb"] = inputs["na_rpb"].reshape(DEPTH, 120, 31)
    m["qk_norm"] = inputs["qk_norm"]
    m["w_branch"] = inputs["w_branch"].reshape(DEPTH, 2048, D_MODEL)
    m["w_out"] = inputs["w_out"]
    m["norm_ffn"] = inputs["norm_ffn"]
    m["w_ff1"] = inputs["w_ff1"]
    m["w_ff2"] = inputs["w_ff2"]
    m["norm_final"] = inputs["norm_final"].reshape(1, D_MODEL)
    for k, v in b.consts.items():
        m[k] = v
    return {k: np.ascontiguousarray(v) for k, v in m.items()}


def kernel(**inputs):
    inputs = {k: np.asarray(v) for k, v in inputs.items()}
    x = inputs["x"]
    B, S, _ = x.shape
    b = _get_builder(S)
    in_maps = [make_in_map(b, x[i], inputs) for i in range(B)]
    res = run_bass_kernel_spmd(b.nc, in_maps, core_ids=list(range(B)))
    return np.stack([np.asarray(r["out"]) for r in res.results], axis=0).astype(np.float32)
```

```python
import math
import os
import numpy as np
import ml_dtypes
import concourse.bass as bass
import concourse.mybir as mybir
from concourse.bass_utils import run_bass_kernel_spmd

F32 = mybir.dt.float32
BF16 = mybir.dt.bfloat16
AF = mybir.ActivationFunctionType
ALU = mybir.AluOpType
AX = mybir.AxisListType

D_MODEL = 1024
DEPTH = 2
GRID_W = 64
IN_W = 12544
NQK = 8448
EPS = 1e-6
NEG = -1e30
B_PATTERNS = ((128, 1), (512, 4), (2048, 16))
OFF_AQ, OFF_AK, OFF_AV, OFF_B, OFF_CQ, OFF_CK, OFF_CV, OFF_DQ, OFF_DK, OFF_DV, OFF_G = (
    0, 512, 1024, 1536, 6144, 6656, 7168, 7680, 8192, 8320, 8448)
VC_A, VC_B, VC_C, VC_D, VC_N = 0, 512, 2048, 2560, 2688

ENGS = ("pe", "act", "dve", "pool", "sp")
SEM_ROLL = 8000
DMA_SLOTS = 8
SB_BASE = 16512
SB_LIMIT = 228864


class Op:
    __slots__ = ("eng", "fn", "deps", "is_dma", "sig", "need_sig", "slot")

    def __init__(self, eng, fn, is_dma):
        self.eng = eng
        self.fn = fn
        self.is_dma = is_dma
        self.deps = []
        self.sig = None
        self.need_sig = False
        self.slot = None


class Sched:
    def __init__(self, nc):
        self.nc = nc
        self.ops = {e: [] for e in ENGS}
        self.last_w = {}
        self.readers = {}
        self.pending = {e: [] for e in ENGS}

    def _dep(self, op, d):
        if d is op:
            return
        if (not d.is_dma) and (not op.is_dma) and d.eng == op.eng and op.eng == "pe":
            return
        d.need_sig = True
        op.deps.append(d)

    def add(self, eng, fn, reads=(), writes=(), is_dma=False):
        op = Op(eng, fn, is_dma)
        deps = {}
        for r in reads:
            w = self.last_w.get(r)
            if w is not None:
                deps[id(w)] = w
        for r in writes:
            w = self.last_w.get(r)
            if w is not None:
                deps[id(w)] = w
            for rd in self.readers.get(r, ()):
                deps[id(rd)] = rd
        for d in self.pending[eng]:
            deps[id(d)] = d
        self.pending[eng] = []
        for d in deps.values():
            self._dep(op, d)
        for r in writes:
            self.last_w[r] = op
            self.readers[r] = []
        for r in reads:
            if r not in writes:
                self.readers.setdefault(r, []).append(op)
        self.ops[eng].append(op)
        return op

    def barrier(self):
        deps = []
        for e in ENGS:
            ops = self.ops[e]
            for op in reversed(ops):
                if not op.is_dma:
                    deps.append(op)
                    break
            k = 0
            for op in reversed(ops):
                if op.is_dma:
                    deps.append(op)
                    k += 1
                    if k >= DMA_SLOTS:
                        break
        for e in ENGS:
            self.pending[e] = list(deps)
        self.last_w.clear()
        self.readers.clear()

    def emit(self, final_waits=()):
        nc = self.nc
        sem_ctx = []

        def new_sem(name):
            cm = nc.semaphore(name)
            s = cm.__enter__()
            sem_ctx.append(cm)
            return s

        cnt = 0
        for e in ENGS:
            cur = None
            val = 0
            slots = [None] * DMA_SLOTS
            slotv = [0] * DMA_SLOTS
            nd = 0
            for op in self.ops[e]:
                if op.is_dma:
                    s = nd % DMA_SLOTS
                    nd += 1
                    if slots[s] is None or slotv[s] + 16 > SEM_ROLL:
                        slots[s] = new_sem(f"d{e}{s}_{cnt}")
                        cnt += 1
                        slotv[s] = 0
                        prev = None
                    else:
                        prev = (slots[s], slotv[s])
                    slotv[s] += 16
                    op.sig = (slots[s], slotv[s])
                    op.slot = prev
                elif op.need_sig:
                    if cur is None or val + 1 > SEM_ROLL:
                        cur = new_sem(f"c{e}_{cnt}")
                        cnt += 1
                        val = 0
                    val += 1
                    op.sig = (cur, val)
        self.n_sems = cnt
        engmap = {"pe": "tensor", "act": "scalar", "dve": "vector", "pool": "gpsimd", "sp": "sync"}
        with nc.Block() as block:
            for e in ENGS:
                ops = self.ops[e]

                def body(eng, ops=ops, e=e):
                    waited = {}

                    def wait(sem, v):
                        k = id(sem)
                        if waited.get(k, 0) >= v:
                            return
                        waited[k] = v
                        eng.wait_ge(sem, v)

                    for op in ops:
                        if op.is_dma and op.slot is not None:
                            wait(*op.slot)
                        for d in op.deps:
                            wait(*d.sig)
                        ins = op.fn(eng)
                        if op.sig is not None:
                            ins.then_inc(op.sig[0], 16 if op.is_dma else 1)
                    if e == "sp":
                        for fw in final_waits:
                            wait(*fw.sig)

                getattr(block, engmap[e])(body)
        for cm in reversed(sem_ctx):
            cm.__exit__(None, None, None)


def alibi_slopes(n):
    return [2.0 ** (-8.0 * (i + 1) / n) for i in range(n)]


def c_mask_meta(S):
    rows = S // GRID_W
    M = rows // 2
    reps = {"int": 2, "m0": 0, "m1": 1, "mL2": M - 2, "mL1": M - 1}
    meta = []
    tiles = []
    kc = np.arange(64)
    c = np.arange(64)
    qstart = np.clip(c - 8, 0, GRID_W - 16)
    colok = (kc[:, None] >= qstart[None, :]) & (kc[:, None] < qstart[None, :] + 16)
    for cls, m in reps.items():
        for dr in range(-4, 5):
            mk = m + dr
            if mk < 0 or mk >= M:
                continue
            t = np.full((128, 128), NEG, np.float32)
            anyv = False
            for a in range(2):
                for b in range(2):
                    r = 2 * m + b
                    kr = 2 * mk + a
                    rs = min(max(r - 4, 0), rows - 8)
                    if rs <= kr < rs + 8:
                        t[a * 64:(a + 1) * 64, b * 64:(b + 1) * 64] = np.where(colok, 0.0, NEG)
                        anyv = True
            if anyv:
                meta.append((cls, dr))
                tiles.append(t)
    return meta, np.stack(tiles)


def make_consts(S):
    bf = ml_dtypes.bfloat16
    c = {}
    c["c_ident"] = np.eye(128, dtype=np.float32).astype(bf)
    c["c_identf"] = np.eye(128, dtype=np.float32)
    p = np.arange(128)
    i32 = (p % 64) % 32
    partner = np.where(i32 < 16, p + 16, p - 16)
    pm = np.zeros((128, 128), np.float32)
    pm[partner, p] = 1.0
    c["c_pm"] = pm
    c["c_bones"] = (p[:, None] // 64 == p[None, :] // 64).astype(np.float32)
    c["c_ones_f"] = np.ones((128, 128), np.float32)
    c["c_j64"] = np.ascontiguousarray(np.eye(64, dtype=np.float32)[::-1])
    c["c_ones_b"] = np.ones((128, 128), np.float32).astype(bf)
    t = np.arange(S)
    d = p % 64
    half = d // 32
    idx = (d % 32) % 16
    first = (d % 32) < 16
    inv = (10000.0 ** (-np.arange(16, dtype=np.float32) / 16)).astype(np.float32)
    pos = np.where(half[:, None] == 0, (t // GRID_W)[None, :], (t % GRID_W)[None, :]).astype(np.float32)
    ang = pos * inv[idx][:, None]
    rope = np.zeros((2, 128, S), np.float32)
    rope[0] = np.cos(ang)
    rope[1] = np.where(first[:, None], -np.sin(ang), np.sin(ang))
    c["c_rope"] = rope
    sl = alibi_slopes(4)
    kaug = np.zeros((4, 5, S), np.float32)
    qaug = np.zeros((4, 2, 5, S), np.float32)
    for h in range(4):
        s8 = 8.0 * sl[h]
        kaug[h, 0:3] = 1.0
        kaug[h, 3] = s8 * (128 * (t // 128))
        kaug[h, 4] = s8 * (t % 128)
        qaug[h, 0, 0] = -s8 * (512 * (t // 512))
        qaug[h, 0, 1] = -s8 * (256 * ((t % 512) // 256))
        qaug[h, 0, 2] = -s8 * (t % 256)
        qaug[h, 0, 3:5] = 1.0
        qaug[h, 1] = -qaug[h, 0]
    c["c_kaug"] = kaug.astype(bf)
    c["c_qaug"] = qaug.astype(bf)
    assert np.array_equal(c["c_kaug"].astype(np.float32), kaug) and np.array_equal(c["c_qaug"].astype(np.float32), qaug)
    k = np.arange(128)[:, None]
    q = np.arange(128)[None, :]
    q5 = np.arange(512)[None, :]
    cd = np.zeros((128, 4, 4, 512), np.float32)
    for h in range(4):
        for rel in range(4):
            dd = q5 - (k + 128 * rel)
            cd[:, h, rel, :] = 8.0 * 2.0 * sl[h] * np.minimum(dd, 0)
    c["c_cd"] = cd
    slb = alibi_slopes(8)
    mb = np.zeros((24, 128, 3, 128), np.float32)
    for g, (win, dil) in enumerate(B_PATTERNS):
        for h in range(8):
            for r in range(3):
                rel = (128 * (r - 1) + k) - q
                mb[g * 8 + h, :, r, :] = np.where(np.abs(rel) <= 64, -slb[h] * np.abs(rel) * dil, NEG)
    c["c_mb"] = mb.reshape(24, 128, 384)
    meta, mcv = c_mask_meta(S)
    c["c_mcv"] = np.ascontiguousarray(mcv.transpose(1, 0, 2))
    return c, meta


class Builder:
    def __init__(self, S, nl=DEPTH, dbg=None):
        self.Sq = S
        self.nl = nl
        self.dbg = dbg or {}
        self.nc = bass.Bass("TRN2", target_bir_lowering=False)
        self.S = Sched(self.nc)
        self.off = SB_BASE
        self.ncnt = 0
        self.consts, self.cmeta = make_consts(S)
        self.outs = []

    def T(self, name, shape, dt):
        sz = 4 if dt == F32 else 2
        n = 1
        for s in shape[1:]:
            n *= s
        nbytes = (n * sz + 63) // 64 * 64
        self.ncnt += 1
        t = self.nc.alloc_sbuf_tensor_at(f"{name}_{self.ncnt}", list(shape), dt, offset=self.off)
        self.off += nbytes
        assert self.off <= SB_LIMIT, (name, self.off)
        return t

    def dram(self, name, shape, dt, kind="Internal"):
        return self.nc.dram_tensor(name, list(shape), dt, kind=kind).ap()

    def mm(self, out, lhsT, rhs, start, stop, rd, wr):
        return self.S.add("pe", lambda e: e.matmul(out, lhsT=lhsT, rhs=rhs, start=start, stop=stop, skip_group_check=True), rd, wr)

    def tr(self, out, in_, ident, rd, wr):
        return self.S.add("pe", lambda e: e.transpose(out, in_, ident), rd, wr)

    def act(self, out, in_, func, rd, wr, bias=None, scale=1.0, accum=None):
        def f(e):
            kw = {}
            if bias is not None:
                kw["bias"] = bias
            if accum is not None:
                kw["accum_out"] = accum
            return e.activation(out=out, in_=in_, func=func, scale=scale, **kw)
        return self.S.add("act", f, rd, wr)

    def tt(self, eng, out, in0, in1, op, rd, wr):
        return self.S.add(eng, lambda e: e.tensor_tensor(out=out, in0=in0, in1=in1, op=op), rd, wr)

    def ts(self, eng, out, in0, s1, op0, rd, wr, s2=None, op1=None):
        if op1 is None:
            return self.S.add(eng, lambda e: e.tensor_scalar(out=out, in0=in0, scalar1=s1, scalar2=None, op0=op0), rd, wr)
        return self.S.add(eng, lambda e: e.tensor_scalar(out=out, in0=in0, scalar1=s1, scalar2=s2, op0=op0, op1=op1), rd, wr)

    def stt(self, eng, out, in0, scalar, in1, op0, op1, rd, wr):
        return self.S.add(eng, lambda e: e.scalar_tensor_tensor(out=out, in0=in0, scalar=scalar, in1=in1, op0=op0, op1=op1), rd, wr)

    def cp(self, eng, out, in_, rd, wr):
        if eng == "act":
            return self.S.add("act", lambda e: e.copy(out=out, in_=in_), rd, wr)
        return self.S.add(eng, lambda e: e.tensor_copy(out=out, in_=in_), rd, wr)

    def recip(self, out, in_, rd, wr):
        return self.S.add("dve", lambda e: e.reciprocal(out=out, in_=in_), rd, wr)

    def memset(self, eng, ap, v, wr):
        return self.S.add(eng, lambda e: e.memset(ap, v), [], wr)

    def dma(self, q, out, in_, rd, wr, slow=False):
        if slow:
            return self.S.add(q, lambda e: e.dma_start(out=out, in_=in_, allow_slow_non_contiguous=True), rd, wr, is_dma=True)
        return self.S.add(q, lambda e: e.dma_start(out=out, in_=in_), rd, wr, is_dma=True)

    def build(self, stages="WPABCDM"):
        self.stages = stages
        nc = self.nc
        S = self.Sq
        nl = self.nl
        dbg = self.dbg
        di = lambda n, sh, dt=F32: nc.dram_tensor(n, list(sh), dt, kind="ExternalInput").ap()
        self.x_in = di("x", [S, D_MODEL])
        self.norm_mix = di("norm_mix", [DEPTH, D_MODEL])
        self.w_in = di("w_in", [DEPTH, D_MODEL, IN_W])
        self.b_gate = di("b_gate", [DEPTH, 4096])
        self.diff_lambda = di("diff_lambda", [DEPTH, 256])
        self.diff_subln = di("diff_subln", [DEPTH, 128])
        self.na_rpb = di("na_rpb", [DEPTH, 120, 31])
        self.qk_norm = di("qk_norm", [DEPTH, 2, 64])
        self.w_branch = di("w_branch", [DEPTH, 2048, D_MODEL])
        self.w_out = di("w_out", [DEPTH, D_MODEL, D_MODEL])
        self.norm_ffn = di("norm_ffn", [DEPTH, D_MODEL])
        self.w_ff1 = di("w_ff1", [DEPTH, D_MODEL, 4096])
        self.w_ff2 = di("w_ff2", [DEPTH, 4096, D_MODEL])
        self.norm_final = di("norm_final", [1, D_MODEL])
        self.cd = {}
        for k, v in self.consts.items():
            self.cd[k] = di(k, v.shape, BF16 if v.dtype == ml_dtypes.bfloat16 else F32)
        okind = "ExternalOutput"
        self.out = nc.dram_tensor("out", [S, D_MODEL], F32, kind=okind).ap()
        dk = lambda n: okind if dbg.get(n) else "Internal"
        self.wb_in = [self.dram(f"wb_in{l}", [D_MODEL, IN_W], BF16) for l in range(nl)]
        self.wb_br = [self.dram(f"wb_br{l}", [2048, D_MODEL], BF16) for l in range(nl)]
        self.wb_out = [self.dram(f"wb_out{l}", [D_MODEL, D_MODEL], BF16) for l in range(nl)]
        self.wb_f1 = [self.dram(f"wb_f1{l}", [D_MODEL, 4096], BF16) for l in range(nl)]
        self.wb_f2 = [self.dram(f"wb_f2{l}", [4096, D_MODEL], BF16) for l in range(nl)]
        self.qkT = self.dram("qkT", [NQK, S], BF16, dk("qkT"))
        self.vtm = self.dram("vtm", [S, VC_N], BF16, dk("vtm"))
        self.gT = self.dram("gT", [4096, S], BF16, dk("gT"))
        if dbg.get("yT_in"):
            self.yT = di("yT", [4, 512, S], BF16)
        else:
            self.yT = self.dram("yT", [4, 512, S], BF16, dk("yT"))
        self.xres = self.dram("xres", [S, D_MODEL], F32, dk("xres"))
        self.rpbr = self.dram("rpbr", [120, 128], F32)
        for n in ("qkT", "vtm", "gT", "yT", "xres"):
            if dbg.get(n):
                self.outs.append(n)

        self.ident = self.T("ident", [128, 128], BF16)
        self.pm = self.T("pm", [128, 128], F32)
        self.identf = self.T("identf", [128, 128], F32)
        self.bones = self.T("bones", [128, 128], F32)
        self.ones_f = self.T("ones_f", [128, 128], F32)
        self.ones_b = self.T("ones_b", [128, 128], BF16)
        self.small = self.T("small", [128, 64], F32)
        self.gcol = self.T("gcol", [128, 32], F32)
        for l in range(nl):
            self.dma("sp", self.gcol[:, l * 16:l * 16 + 8], self.norm_mix[l, :].rearrange("(c p) -> p c", p=128), [], ["const"], slow=True)
            self.dma("sp", self.gcol[:, l * 16 + 8:l * 16 + 16], self.norm_ffn[l, :].rearrange("(c p) -> p c", p=128), [], ["const"], slow=True)
        for t, n in ((self.ident, "c_ident"), (self.pm, "c_pm"), (self.identf, "c_identf"), (self.bones, "c_bones"),
                     (self.ones_f, "c_ones_f"), (self.ones_b, "c_ones_b")):
            self.dma("sp", t[:], self.cd[n][:, :], [], ["const"])
        self.pall = nc.alloc_psum_tensor("pall", [128, 4096], F32)
        self.pbig = self.pall[:, 0:2048]
        self.pb = [self.pall[:, i * 512:(i + 1) * 512] for i in range(8)]
        self.pbt = self.pall.bitcast(BF16)[:, 7 * 1024:8 * 1024]
        self.base_off = self.off
        self.S.barrier()

        if "W" in stages:
            self.phase_w()
        final = []
        for l in range(nl):
            self.layer_smalls(l)
            xsrc = self.x_in if l == 0 else self.xres
            if "P" in stages:
                self.phase_p(l, xsrc)
            if "D" in stages:
                self.mixer_d(l)
            if "A" in stages:
                self.mixer_a(l)
            if "C" in stages:
                self.mixer_c(l)
            if "B" in stages:
                self.mixer_b(l)
            if "M" in stages:
                final = self.phase_m(l, xsrc, last=(l == nl - 1))
        self.S.barrier()
        self.S.emit(final_waits=final)
        return nc

    def phase_reset(self):
        self.S.barrier()
        self.off = self.base_off

    def conv_chunks(self, l):
        jobs = [(self.w_in[l], self.wb_in[l], D_MODEL, IN_W, l * 16),
                (self.w_branch[l], self.wb_br[l], 2048, D_MODEL, None),
                (self.w_out[l], self.wb_out[l], D_MODEL, D_MODEL, None),
                (self.w_ff1[l], self.wb_f1[l], D_MODEL, 4096, l * 16 + 8),
                (self.w_ff2[l], self.wb_f2[l], 4096, D_MODEL, None)]
        out = []
        for src, dst, R, C, gc in jobs:
            for rc in range(R // 128):
                for c0 in range(0, C, 4096):
                    out.append((src, dst, rc, c0, min(4096, C - c0), gc))
        return out

    def conv_emit(self, job, bi, bo, n):
        src, dst, rc, c0, cw, gc = job
        i = n % 2
        gcol = self.gcol
        self.dma("sp", bi[i][:, 0:cw], src[rc * 128:(rc + 1) * 128, c0:c0 + cw], [], [f"wci{i}"])
        eng = "dve" if n % 2 else "pool"
        if gc is not None:
            self.ts(eng, bo[i][:, 0:cw], bi[i][:, 0:cw], gcol[:, gc + rc:gc + rc + 1], ALU.mult,
                    [f"wci{i}", "gcol"], [f"wco{i}"])
        else:
            self.cp(eng, bo[i][:, 0:cw], bi[i][:, 0:cw], [f"wci{i}"], [f"wco{i}"])
        self.dma("pool", dst[rc * 128:(rc + 1) * 128, c0:c0 + cw], bo[i][:, 0:cw], [f"wco{i}"], [])

    def phase_w(self):
        self.phase_reset()
        bi = [self.T(f"wci{i}", [128, 4096], F32) for i in range(2)]
        bo = [self.T(f"wco{i}", [128, 4096], BF16) for i in range(2)]
        for n, job in enumerate(self.conv_chunks(0)):
            self.conv_emit(job, bi, bo, n)

    def layer_smalls(self, l):
        self.phase_reset()
        sm = self.small
        lam_init = 0.8 - 0.6 * math.exp(-0.3 * l)
        lt = self.T("lamt", [128, 256], F32)
        pr = self.T("lampr", [128, 128], F32)
        sv = self.T("lamsv", [128, 4], F32)
        self.dma("sp", lt[:], self.diff_lambda[l:l + 1, :].partition_broadcast(128), [], ["lamt"])
        self.tt("dve", pr[:, 0:64], lt[:, 0:64], lt[:, 64:128], ALU.mult, ["lamt"], ["lampr"])
        self.tt("dve", pr[:, 64:128], lt[:, 128:192], lt[:, 192:256], ALU.mult, ["lamt"], ["lampr"])
        self.S.add("dve", lambda e: e.reduce_sum(out=sv[:, 0:2], in_=pr[:].rearrange("p (a b) -> p a b", a=2), axis=AX.X),
                   ["lampr"], ["lamsv"])
        self.act(sv[:, 2:4], sv[:, 0:2], AF.Exp, ["lamsv"], ["lamsv"])
        self.tt("dve", sm[:, 0:1], sv[:, 3:4], sv[:, 2:3], ALU.subtract, ["lamsv"], ["small"])
        self.ts("dve", sm[:, 0:1], sm[:, 0:1], -lam_init, ALU.add, ["small"], ["small"])
        self.dma("sp", sm[:, 1:2], self.diff_subln[l, :].rearrange("(p o) -> p o", o=1), [], ["small"])
        self.ts("dve", sm[:, 1:2], sm[:, 1:2], 1.0 - lam_init, ALU.mult, ["small"], ["small"])
        for i in range(2):
            for hf in range(2):
                self.dma("sp", sm[hf * 64:(hf + 1) * 64, 2 + i:3 + i], self.qk_norm[l, i, :].rearrange("(p o) -> p o", o=1), [], ["small"])
        self.dma("sp", sm[:, 8:40], self.b_gate[l, :].rearrange("(c p) -> p c", p=128), [], ["small"], slow=True)

    def phase_p(self, l, xsrc):
        self.phase_reset()
        S = self.Sq
        sm = self.small
        NT = S // 512
        xts = [self.T(f"xtP{i}", [128, 4, 1024], F32) for i in range(2)]
        hTs = [self.T(f"hTP{i}", [128, 8, 512], BF16) for i in range(2)]
        wts = [self.T(f"wtP{i}", [128, 8, 512], BF16) for i in range(3)]
        stg = [self.T(f"stgP{i}", [128, 512], BF16) for i in range(4)]
        vst = [self.T(f"vstP{i}", [128, 4, 512], BF16) for i in range(2)]
        rope = [self.T(f"ropeP{i}", [128, 2, 512], F32) for i in range(2)]
        dtmp = [self.T(f"dtmpP{i}", [128, 512], F32) for i in range(5)]
        plan = []
        for c0 in range(0, OFF_G, 512):
            if c0 in (OFF_AV, OFF_B + 1024, OFF_B + 1536 + 1024, OFF_B + 3072 + 1024, OFF_CV):
                vc = {OFF_AV: VC_A, OFF_B + 1024: VC_B, OFF_B + 2560: VC_B + 512, OFF_B + 4096: VC_B + 1024, OFF_CV: VC_C}[c0]
                plan.append((c0, 512, "tm", vc))
            elif c0 == OFF_DQ:
                plan.append((c0, 512, "dq", None))
            elif c0 == OFF_DK:
                plan.append((c0, 256, "dkv", None))
            else:
                plan.append((c0, 512, "fm", None))
        for c0 in range(OFF_G, IN_W, 512):
            plan.append((c0, 512, "gate", None))
        wn = 0
        cnt = {"sn": 0, "pbn": 0, "vn": 0}
        NPAIR = 2 if NT % 2 == 0 else 1

        STQ = os.environ.get("STQ", "pool")

        def emit(tt_, w, wk, c0, ncol, kind, vc):
            t0 = tt_ * 512
            ti = tt_ % 2
            hT = hTs[ti]
            hk = f"hTP{ti}"
            rp = rope[ti]
            rk = f"ropeP{ti}"
            if kind in ("fm", "gate", "dq", "dkv"):
                nfm = {"fm": 4, "gate": 4, "dq": 4, "dkv": 1}[kind]
                for j in range(nfm):
                    bank = self.pb[cnt["pbn"] % 4]
                    bk = f"pb{cnt['pbn'] % 4}"
                    cnt["pbn"] += 1
                    for kc in range(8):
                        self.mm(bank[:], w[:, kc, j * 128:(j + 1) * 128], hT[:, kc, :], kc == 0, kc == 7, [wk, hk], [bk])
                    st = stg[cnt["sn"] % 4]
                    sk = f"stgP{cnt['sn'] % 4}"
                    cnt["sn"] += 1
                    col = c0 + j * 128
                    if kind == "fm":
                        self.cp("act" if cnt["sn"] % 2 else "dve", st[:], bank[:], [bk], [sk])
                        self.dma(STQ, self.qkT[col:col + 128, t0:t0 + 512], st[:], [sk], [])
                    elif kind == "gate":
                        gc = (col - OFF_G) // 128
                        self.act(st[:], bank[:], AF.Sigmoid, [bk, "small"], [sk], bias=sm[:, 8 + gc:9 + gc])
                        self.dma(STQ, self.gT[col - OFF_G:col - OFF_G + 128, t0:t0 + 512], st[:], [sk], [])
                    else:
                        gi = 2 if kind == "dq" else 3
                        sq, rst, xg, t1, t2 = dtmp
                        self.act(sq[:], bank[:], AF.Square, [bk], ["dsq"])
                        self.mm(self.pb[4][:], self.bones[:], sq[:], True, True, ["dsq", "const"], ["pb4"])
                        self.ts("dve", rst[:], self.pb[4][:], 1.0 / 64, ALU.mult, ["pb4"], ["drst"], s2=EPS, op1=ALU.add)
                        self.act(rst[:], rst[:], AF.Sqrt, ["drst"], ["drst"])
                        self.recip(rst[:], rst[:], ["drst"], ["drst"])
                        self.ts("dve", xg[:], bank[:], sm[:, gi:gi + 1], ALU.mult, [bk, "small"], ["dxg"])
                        self.mm(self.pb[5][:], self.pm[:], xg[:], True, True, ["dxg", "const"], ["pb5"])
                        self.tt("pool", t1[:], xg[:], rp[:, 0, :], ALU.mult, ["dxg", rk], ["dt1"])
                        self.tt("dve", t2[:], self.pb[5][:], rp[:, 1, :], ALU.mult, ["pb5", rk], ["dt2"])
                        self.tt("pool", t1[:], t1[:], t2[:], ALU.add, ["dt1", "dt2"], ["dt1"])
                        self.tt("dve", st[:], t1[:], rst[:], ALU.mult, ["dt1", "drst"], [sk])
                        self.dma(STQ, self.qkT[col:col + 128, t0:t0 + 512], st[:], [sk], [])
            if kind in ("tm", "dkv"):
                if kind == "tm":
                    wc0, nvc, vcol = 0, 512, vc
                else:
                    wc0, nvc, vcol = 128, 128, VC_D
                vs = vst[cnt["vn"] % 2]
                vk = f"vstP{cnt['vn'] % 2}"
                cnt["vn"] += 1
                for s_ in range(4):
                    bank = self.pb[cnt["pbn"] % 4]
                    bk = f"pb{cnt['pbn'] % 4}"
                    cnt["pbn"] += 1
                    for kc in range(8):
                        self.mm(bank[:, 0:nvc], hT[:, kc, s_ * 128:(s_ + 1) * 128], w[:, kc, wc0:wc0 + nvc], kc == 0, kc == 7, [wk, hk], [bk])
                    self.cp("act" if s_ % 2 else "dve", vs[:, s_, 0:nvc], bank[:, 0:nvc], [bk], [vk])
                self.dma(STQ, self.vtm[t0:t0 + 512, vcol:vcol + nvc].rearrange("(s p) c -> p s c", p=128),
                         vs[:, :, 0:nvc], [vk], [])

        for p0 in range(0, NT, NPAIR):
            tiles = list(range(p0, p0 + NPAIR))
            for tt_ in tiles:
                t0 = tt_ * 512
                ti = tt_ % 2
                self.dma("sp", xts[ti][:], xsrc[t0:t0 + 512, :].rearrange("(s p) d -> p s d", p=128), [], [f"xtP{ti}"])
                self.dma("sp", rope[ti][:], self.cd["c_rope"][:, :, t0:t0 + 512].rearrange("a p t -> p a t"), [], [f"ropeP{ti}"])
            for tt_ in tiles:
                ti = tt_ % 2
                mark = self.off
                self.rmsnorm_T_keys(xts[ti], hTs[ti], f"xtP{ti}", f"hTP{ti}")
                self.off = mark
            for (c0, ncol, kind, vc) in plan:
                w = wts[wn % 3]
                wk = f"wtP{wn % 3}"
                wn += 1
                self.dma("sp", w[:, :, 0:ncol], self.wb_in[l][:, c0:c0 + ncol].rearrange("(k p) n -> p k n", p=128), [], [wk])
                for tt_ in tiles:
                    emit(tt_, w, wk, c0, ncol, kind, vc)

    def rmsnorm_T_keys(self, xt, hT, xk, hk, inv_d=1.0 / D_MODEL):
        ss = self.T("ss", [128, 8], F32)
        junk = self.T("junk", [128, 1024], BF16)
        hn = self.T("hn", [128, 4, 1024], BF16)
        self.memset("dve", ss[:], 0.0, ["ss"])
        for s in range(4):
            self.act(junk[:], xt[:, s, :], AF.Square, [xk, "ss"], ["junk", "ss"], accum=ss[:, s:s + 1])
        self.ts("dve", ss[:, 4:8], ss[:, 0:4], inv_d, ALU.mult, ["ss"], ["ss"], s2=EPS, op1=ALU.add)
        self.act(ss[:, 4:8], ss[:, 4:8], AF.Sqrt, ["ss"], ["ss"])
        self.recip(ss[:, 4:8], ss[:, 4:8], ["ss"], ["ss"])
        for s in range(4):
            self.ts("dve" if s % 2 else "pool", hn[:, s, :], xt[:, s, :], ss[:, 4 + s:5 + s], ALU.mult,
                    [xk, "ss"], [f"hn{s}"])
        for c in range(8):
            for s in range(4):
                self.tr(self.pbt[:, s * 128:(s + 1) * 128], hn[:, s, c * 128:(c + 1) * 128], self.ident[:],
                        [f"hn{s}", "const"], ["pb7"])
            self.cp("act" if c % 2 else "dve", hT[:, c, :], self.pbt[:, 0:512], ["pb7"], [hk])

    def phase_m(self, l, xsrc, last):
        self.phase_reset()
        S = self.Sq
        NT = S // 512
        xts = [self.T(f"xtM{i}", [128, 4, 1024], F32) for i in range(2)]
        yts = [self.T(f"ytM{i}", [128, 4, 512], BF16) for i in range(2)]
        gts = [self.T(f"gtM{i}", [128, 8, 512], BF16) for i in range(2)]
        wts = [self.T(f"wtM{i}", [128, 8, 512], BF16) for i in range(3)]
        macc = self.T("macc", [128, 8, 512], F32)
        mtmp = [self.T(f"mtmp{i}", [128, 512], F32) for i in range(2)]
        mT = self.T("mT", [128, 8, 512], BF16)
        h2T = self.T("h2T", [128, 8, 512], BF16)
        uT = self.T("uT", [128, 32, 512], BF16)
        gfin = None
        if last:
            gfin = self.T("gfin", [128, 1024], F32)
            self.dma("sp", gfin[:], self.norm_final[0:1, :].partition_broadcast(128), [], ["gfin"])
        wn = 0
        pbn = 0
        yn = 0
        finals = []
        for tt_ in range(NT):
            t0 = tt_ * 512
            xt = xts[tt_ % 2]
            xk = f"xtM{tt_ % 2}"
            self.dma("sp", xt[:], xsrc[t0:t0 + 512, :].rearrange("(s p) d -> p s d", p=128), [], [xk])
            for n in range(4):
                yt = yts[yn % 2]
                gt = gts[yn % 2]
                yk, gk = f"ytM{yn % 2}", f"gtM{yn % 2}"
                yn += 1
                w = wts[wn % 3]
                wk = f"wtM{wn % 3}"
                wn += 1
                self.dma("sp", yt[:], self.yT[n, :, t0:t0 + 512].rearrange("(k p) t -> p k t", p=128), [], [yk])
                self.dma("sp", gt[:], self.gT[n * 1024:(n + 1) * 1024, t0:t0 + 512].rearrange("(k p) t -> p k t", p=128), [], [gk])
                wv = w[:].rearrange("p k n -> p (k n)").rearrange("p (k n) -> p k n", k=4)
                self.dma("sp", wv, self.wb_br[l][n * 512:(n + 1) * 512, :].rearrange("(k p) n -> p k n", p=128), [], [wk])
                for oc in range(8):
                    bank = self.pb[pbn % 3]
                    bk = f"pb{pbn % 3}"
                    pbn += 1
                    for kc in range(4):
                        self.mm(bank[:], wv[:, kc, oc * 128:(oc + 1) * 128], yt[:, kc, :], kc == 0, kc == 3, [wk, yk], [bk])
                    eng = "dve" if oc % 2 else "pool"
                    if n == 0:
                        self.tt("dve", macc[:, oc, :], bank[:], gt[:, oc, :], ALU.mult, [bk, gk], [f"macc{oc}"])
                    elif n < 3:
                        mt = mtmp[oc % 2]
                        self.tt("dve", mt[:], bank[:], gt[:, oc, :], ALU.mult, [bk, gk], [f"mtmp{oc % 2}"])
                        self.tt("pool", macc[:, oc, :], macc[:, oc, :], mt[:], ALU.add, [f"mtmp{oc % 2}", f"macc{oc}"], [f"macc{oc}"])
                    else:
                        mt = mtmp[oc % 2]
                        self.tt("dve", mt[:], bank[:], gt[:, oc, :], ALU.mult, [bk, gk], [f"mtmp{oc % 2}"])
                        self.tt("pool", mT[:, oc, :], macc[:, oc, :], mt[:], ALU.add, [f"mtmp{oc % 2}", f"macc{oc}"], [f"mT{oc}"])
            mTk = [f"mT{oc}" for oc in range(8)]
            for half in range(2):
                w = wts[wn % 3]
                wk = f"wtM{wn % 3}"
                wn += 1
                self.dma("sp", w[:], self.wb_out[l][:, half * 512:(half + 1) * 512].rearrange("(k p) n -> p k n", p=128), [], [wk])
                for s in range(4):
                    bank = self.pb[3 + pbn % 2]
                    bk = f"pb{3 + pbn % 2}"
                    pbn += 1
                    for kc in range(8):
                        self.mm(bank[:], mT[:, kc, s * 128:(s + 1) * 128], w[:, kc, :], kc == 0, kc == 7, [wk, f"mT{kc}"], [bk])
                    self.tt("dve", xt[:, s, half * 512:(half + 1) * 512], xt[:, s, half * 512:(half + 1) * 512], bank[:], ALU.add,
                            [bk, xk], [xk])
            mark = self.off
            self.rmsnorm_T_keys(xt, h2T, xk, "h2T")
            self.off = mark
            for ft in range(8):
                w = wts[wn % 3]
                wk = f"wtM{wn % 3}"
                wn += 1
                self.dma("sp", w[:], self.wb_f1[l][:, ft * 512:(ft + 1) * 512].rearrange("(k p) n -> p k n", p=128), [], [wk])
                for j in range(4):
                    fc = ft * 4 + j
                    bank = self.pb[pbn % 3]
                    bk = f"pb{pbn % 3}"
                    pbn += 1
                    for kc in range(8):
                        self.mm(bank[:], w[:, kc, j * 128:(j + 1) * 128], h2T[:, kc, :], kc == 0, kc == 7, [wk, "h2T"], [bk])
                    rl = mtmp[fc % 2]
                    self.act(rl[:], bank[:], AF.Relu, [bk], [f"mtmp{fc % 2}"])
                    self.tt("dve" if fc % 2 else "pool", uT[:, fc, :], rl[:], rl[:], ALU.mult, [f"mtmp{fc % 2}"], [f"uT{fc}"])
            for half in range(2):
                banks = [(self.pb[3 + s], f"pb{3 + s}") for s in range(4)]
                for fg in range(4):
                    w = wts[wn % 3]
                    wk = f"wtM{wn % 3}"
                    wn += 1
                    self.dma("sp", w[:], self.wb_f2[l][fg * 1024:(fg + 1) * 1024, half * 512:(half + 1) * 512].rearrange("(k p) n -> p k n", p=128),
                             [], [wk])
                    for s in range(4):
                        bank, bk = banks[s]
                        for kc in range(8):
                            fc = fg * 8 + kc
                            self.mm(bank[:], uT[:, fc, s * 128:(s + 1) * 128], w[:, kc, :], fg == 0 and kc == 0, fg == 3 and kc == 7,
                                    [wk, f"uT{fc}"], [bk])
                for s in range(4):
                    bank, bk = banks[s]
                    self.tt("dve", xt[:, s, half * 512:(half + 1) * 512], xt[:, s, half * 512:(half + 1) * 512], bank[:], ALU.add,
                            [bk, xk], [xk])
            if not last:
                self.dma("pool", self.xres[t0:t0 + 512, :].rearrange("(s p) d -> p s d", p=128), xt[:], [xk], [])
            else:
                ss = self.T("ssF", [128, 8], F32)
                junk = self.T("junkF", [128, 1024], BF16)
                self.memset("dve", ss[:], 0.0, ["ssF"])
                for s in range(4):
                    self.act(junk[:], xt[:, s, :], AF.Square, [xk, "ssF"], ["junkF", "ssF"], accum=ss[:, s:s + 1])
                self.ts("dve", ss[:, 4:8], ss[:, 0:4], 1.0 / D_MODEL, ALU.mult, ["ssF"], ["ssF"], s2=EPS, op1=ALU.add)
                self.act(ss[:, 4:8], ss[:, 4:8], AF.Sqrt, ["ssF"], ["ssF"])
                self.recip(ss[:, 4:8], ss[:, 4:8], ["ssF"], ["ssF"])
                for s in range(4):
                    self.stt("dve", xt[:, s, :], xt[:, s, :], ss[:, 4 + s:5 + s], gfin[:], ALU.mult, ALU.mult,
                             [xk, "ssF", "gfin"], [xk])
                finals.append(self.dma("pool", self.out[t0:t0 + 512, :].rearrange("(s p) d -> p s d", p=128), xt[:], [xk], []))
                self.off = mark
        return finals

    def mixer_d(self, l):
        self.phase_reset()
        S = self.Sq
        NKB = S // 128
        NQT = S // 512
        NSB = 6
        LA = 5
        kT = self.T("dkT", [128, S], BF16)
        va = self.T("dva", [128, NKB, 128], BF16)
        qts = [self.T(f"dq{i}", [128, 512], BF16) for i in range(3)]
        pts = [self.T(f"dpt{i}", [128, 512], BF16) for i in range(6)]
        rec = [self.T(f"drec{i}", [128, 512], F32) for i in range(2)]
        yst = [self.T(f"dyst{i}", [64, 512], BF16) for i in range(2)]
        self.memset("dve", kT[64:128, :], 0.0, ["dkT"])
        for i in range(3):
            self.memset("pool", qts[i][64:128, :], 0.0, [f"dq{i}"])
        self.memset("pool", va[:, :, 64:128], 1.0, ["dva"])
        bg = self.conv_chunks(l + 1) if (l + 1 < self.nl and "W" in self.stages) else []
        if bg:
            bi = [self.T(f"wci{i}", [128, 4096], F32) for i in range(2)]
            bo = [self.T(f"wco{i}", [128, 4096], BF16) for i in range(2)]
        bgn = 0
        step = max(1, (2 * 4 * NQT * NKB) // (len(bg) + 1)) if bg else 0
        tot = 0
        gq = 0
        for g in range(2):
            self.dma("sp", kT[0:64, :], self.qkT[OFF_DK + g * 64:OFF_DK + (g + 1) * 64, :], [], ["dkT"])
            self.dma("sp", va[:, :, 0:64], self.vtm[:, VC_D + g * 64:VC_D + (g + 1) * 64].rearrange("(k p) c -> p k c", p=128), [], ["dva"])
            items = [(h, qt, kb) for h in range(4 * g, 4 * g + 4) for qt in range(NQT) for kb in range(NKB)]
            n = len(items)

            def qk(i, gq=gq):
                h, qt, kb = items[i]
                qi = gq + i // NKB
                q = qts[qi % 3]
                qk_ = f"dq{qi % 3}"
                if kb == 0:
                    for i2 in ([i, i + NKB] if i == 0 else [i + NKB]):
                        if i2 >= n:
                            continue
                        h2, qt2, _ = items[i2]
                        qj = gq + i2 // NKB
                        self.dma("sp", qts[qj % 3][0:64, :], self.qkT[OFF_DQ + h2 * 64:OFF_DQ + (h2 + 1) * 64, qt2 * 512:(qt2 + 1) * 512], [], [f"dq{qj % 3}"])
                self.mm(self.pb[i % NSB][:], kT[:, kb * 128:(kb + 1) * 128], q[:], True, True, ["dkT", qk_], [f"pb{i % NSB}"])

            def ex_av(i, gq=gq):
                h, qt, kb = items[i]
                qi = gq + i // NKB
                acc = self.pb[6 + qi % 2]
                ak = f"pb{6 + qi % 2}"
                pt = pts[i % 6]
                pk = f"dpt{i % 6}"
                self.act(pt[:], self.pb[i % NSB][:], AF.Exp, [f"pb{i % NSB}"], [pk], scale=0.125)
                self.mm(acc[:], va[:, kb, :], pt[:], kb == 0, kb == NKB - 1, ["dva", pk], [ak])
                if kb == NKB - 1:
                    ys = yst[qi % 2]
                    yk = f"dyst{qi % 2}"
                    rc = rec[qi % 2]
                    rk = f"drec{qi % 2}"
                    self.recip(rc[64:128, :], acc[64:128, :], [ak], [rk])
                    self.tt("dve", ys[:], acc[0:64, :], rc[64:128, :], ALU.mult, [ak, rk], [yk])
                    self.dma("pool", self.yT[3, h * 64:(h + 1) * 64, qt * 512:(qt + 1) * 512], ys[:], [yk], [])

            for i in range(n + LA):
                if i < n:
                    qk(i)
                if i >= LA:
                    ex_av(i - LA)
                tot += 1
                if bg and tot % step == 0 and bgn < len(bg):
                    self.conv_emit(bg[bgn], bi, bo, bgn)
                    bgn += 1
            gq += n // NKB
        while bgn < len(bg):
            self.conv_emit(bg[bgn], bi, bo, bgn)
            bgn += 1

    def mixer_a(self, l):
        self.phase_reset()
        S = self.Sq
        sm = self.small
        NKB = S // 128
        NQT = S // 512
        NSB = 4
        LA = 3
        kTs = [self.T(f"akT{c}", [128, S], BF16) for c in range(2)]
        va = self.T("ava", [128, NKB, 128], BF16)
        qts = [[[self.T(f"aq{i}{c}{v}", [128, 512], BF16) for v in range(2)] for c in range(2)] for i in range(2)]
        pts = [self.T(f"apt{i}", [128, 512], BF16) for i in range(5)]
        for c in range(2):
            self.memset("dve", kTs[c][64:128, :], 0.0, [f"akT{c}"])
            for i in range(2):
                for v in range(2):
                    self.memset("pool", qts[i][c][v][64:128, :], 0.0, [f"aq{i}{c}{v}"])
        cdt = self.T("acd", [128, 4, 512], F32)
        dgt = [self.T(f"adg{i}", [128, 512], F32) for i in range(3)]
        r0 = [self.T(f"ar0{i}", [128, 512], F32) for i in range(2)]
        t0s = [self.T(f"at0{i}", [128, 512], F32) for i in range(2)]
        t1_ = self.T("at1", [128, 512], F32)
        sq = self.T("asq", [128, 512], F32)
        rs = self.T("ars", [128, 512], F32)
        yst = [self.T(f"ayst{i}", [128, 512], BF16) for i in range(2)]
        pssum = self.pb[7]
        gq = 0
        for h in range(4):
            for c in range(2):
                r = OFF_AK + (h * 2 + c) * 64
                self.dma("sp", kTs[c][0:64, :], self.qkT[r:r + 64, :], [], [f"akT{c}"])
                self.dma("sp", kTs[c][64:69, :], self.cd["c_kaug"][h, :, :], [], [f"akT{c}"])
            self.dma("sp", va[:], self.vtm[:, VC_A + h * 128:VC_A + (h + 1) * 128].rearrange("(k p) c -> p k c", p=128), [], ["ava"])
            self.dma("sp", cdt[:], self.cd["c_cd"][:, h, :, :], [], ["acd"])
            def kb_order(qt):
                diag = [4 * qt + j for j in range(4)]
                rest = [kb for kb in range(NKB) if kb not in diag]
                slots = {8 + (NKB // 4) * j: diag[j] for j in range(4)} if NKB >= 32 else {2 + 4 * j: diag[j] for j in range(4)}
                out, ri = [], 0
                for pos in range(NKB):
                    if pos in slots:
                        out.append(slots[pos])
                    else:
                        out.append(rest[ri])
                        ri += 1
                return out
            items = [(qt, c, kb, pos) for qt in range(NQT) for c in range(2) for pos, kb in enumerate(kb_order(qt))]
            n = len(items)

            def qk(i, h=h, gq=gq):
                qt, c, kb, pos = items[i]
                qi = (gq + qt) % 2
                if c == 0 and pos == 0:
                    for qt2 in ([0, 1] if qt == 0 else [qt + 1]):
                        if qt2 >= NQT:
                            continue
                        qj = (gq + qt2) % 2
                        for c2 in range(2):
                            r = OFF_AQ + (h * 2 + c2) * 64
                            for v in range(2):
                                self.dma("sp", qts[qj][c2][v][0:64, :], self.qkT[r:r + 64, qt2 * 512:(qt2 + 1) * 512], [], [f"aq{qj}{c2}{v}"])
                                self.dma("sp", qts[qj][c2][v][64:69, :], self.cd["c_qaug"][h, v, :, qt2 * 512:(qt2 + 1) * 512], [], [f"aq{qj}{c2}{v}"])
                rel = kb - qt * 4
                v = 1 if rel > 3 else 0
                self.mm(self.pb[i % NSB][:], kTs[c][:, kb * 128:(kb + 1) * 128], qts[qi][c][v][:], True, True,
                        [f"akT{c}", f"aq{qi}{c}{v}"], [f"pb{i % NSB}"])

            def ex_av(i, h=h, gq=gq):
                qt, c, kb, pos = items[i]
                gi = gq * 2 + i // NKB
                acc, ak = self.pb[4 + 2 * (gi % 2)], f"pb{4 + 2 * (gi % 2)}"
                den, dk_ = self.pb[5 + 2 * (gi % 2)], f"pb{5 + 2 * (gi % 2)}"
                sb, sk = self.pb[i % NSB], f"pb{i % NSB}"
                pt, pk = pts[i % 5], f"apt{i % 5}"
                rel = kb - qt * 4
                if rel < 0 or rel > 3:
                    self.act(pt[:], sb[:], AF.Exp, [sk], [pk], scale=0.125)
                else:
                    dg, dgk = dgt[i % 3], f"adg{i % 3}"
                    self.tt("dve", dg[:], sb[:], cdt[:, rel, :], ALU.add, [sk, "acd"], [dgk])
                    self.act(pt[:], dg[:], AF.Exp, [dgk], [pk], scale=0.125)
                self.mm(acc[:], va[:, kb, :], pt[:], pos == 0, pos == NKB - 1, ["ava", pk], [ak])
                self.mm(den[:], self.ones_b[:], pt[:], pos == 0, pos == NKB - 1, ["const", pk], [dk_])
                if pos != NKB - 1:
                    return
                qg = gq + qt
                t0_, t0k = t0s[qg % 2], f"at0{qg % 2}"
                rr, rrk = r0[gi % 2], f"ar0{gi % 2}"
                self.recip(rr[:], den[:], [dk_], [rrk])
                if c == 0:
                    self.tt("dve", t0_[:], acc[:], rr[:], ALU.mult, [ak, rrk], [t0k])
                    return
                self.tt("dve", t1_[:], acc[:], rr[:], ALU.mult, [ak, rrk], ["at1"])
                self.stt("dve", t0_[:], t1_[:], sm[:, 0:1], t0_[:], ALU.mult, ALU.add, [t0k, "at1", "small"], [t0k])
                self.act(sq[:], t0_[:], AF.Square, [t0k], ["asq"])
                self.mm(pssum[:], self.ones_f[:], sq[:], True, True, ["asq", "const"], ["pb7"])
                self.ts("dve", rs[:], pssum[:], 1.0 / 128, ALU.mult, ["pb7"], ["ars"], s2=EPS, op1=ALU.add)
                self.act(rs[:], rs[:], AF.Sqrt, ["ars"], ["ars"])
                self.recip(rs[:], rs[:], ["ars"], ["ars"])
                ys, yk = yst[qg % 2], f"ayst{qg % 2}"
                self.stt("dve", ys[:], t0_[:], sm[:, 1:2], rs[:], ALU.mult, ALU.mult, [t0k, "ars", "small"], [yk])
                self.dma("pool", self.yT[0, h * 128:(h + 1) * 128, qt * 512:(qt + 1) * 512], ys[:], [yk], [])

            for i in range(n + LA):
                if i < n:
                    qk(i)
                if i >= LA:
                    ex_av(i - LA)
            gq += NQT

    def local_groups(self, qT, kT, va, keys, groups, mask_of, out_fn, tagp):
        pts = self._lp
        tmp = self._lt
        G = len(groups)
        state = {}

        def pass1(gi):
            qbs, rels = groups[gi]
            nq = len(qbs)
            used = []
            for ri, (r, kbs, m) in enumerate(rels):
                js = [j for j in range(nq) if kbs[j] is not None]
                if not js:
                    continue
                j0, j1 = js[0], js[-1] + 1
                assert js == list(range(j0, j1))
                sb = self.pb[self._sn % 4]
                sk = f"pb{self._sn % 4}"
                self._sn += 1
                for j in js:
                    self.mm(sb[:, j * 128:(j + 1) * 128], kT[:, kbs[j] * 128:(kbs[j] + 1) * 128], qT[:, qbs[j] * 128:(qbs[j] + 1) * 128],
                            True, True, keys, [sk])
                t = tmp[self._pn % 4]
                tk = f"{tagp}lt{self._pn % 4}"
                self._pn += 1
                pi = (gi % 2) * 9 + ri
                pt = pts[pi]
                pk = f"{tagp}lp{pi}"
                nj = j1 - j0
                mv = m.unsqueeze(1).broadcast_to([128, nj, 128]) if nj > 1 else m
                tv = t[:, j0 * 128:j1 * 128].rearrange("p (j q) -> p j q", j=nj) if nj > 1 else t[:, j0 * 128:j1 * 128]
                sv = sb[:, j0 * 128:j1 * 128].rearrange("p (j q) -> p j q", j=nj) if nj > 1 else sb[:, j0 * 128:j1 * 128]
                self.stt("dve", tv, sv, 0.125, mv, ALU.mult, ALU.add, [sk, "lmask"], [tk])
                self.act(pt[:, j0 * 128:j1 * 128], t[:, j0 * 128:j1 * 128], AF.Exp, [tk], [pk])
                used.append((ri, kbs, pt, pk))
            state[gi] = used

        def pass2(gi):
            qbs, rels = groups[gi]
            nq = len(qbs)
            acc = self.pb[4 + gi % 2]
            ak = f"pb{4 + gi % 2}"
            used = state.pop(gi)
            for j in range(nq):
                mine = [(ri, kbs, pt, pk) for (ri, kbs, pt, pk) in used if kbs[j] is not None]
                for n_, (ri, kbs, pt, pk) in enumerate(mine):
                    self.mm(acc[:, j * 128:(j + 1) * 128], va[:, kbs[j], :], pt[:, j * 128:(j + 1) * 128],
                            n_ == 0, n_ == len(mine) - 1, keys + [pk], [ak])
            out_fn(gi, qbs, acc, ak)

        for gi in range(G + 1):
            if gi < G:
                pass1(gi)
            if gi >= 1:
                pass2(gi - 1)

    def local_init(self, tagp):
        self._lp = [self.T(f"{tagp}lp{i}", [128, 512], BF16) for i in range(18)]
        self._lt = [self.T(f"{tagp}lt{i}", [128, 512], F32) for i in range(4)]
        self._sn = 0
        self._pn = 0

    def mixer_c(self, l):
        self.phase_reset()
        S = self.Sq
        M = S // 128
        meta = self.cmeta
        NV = len(meta)
        qT = self.T("cqT", [128, S], BF16)
        kT = self.T("ckT", [128, S], BF16)
        self.memset("dve", qT[64:128, :], 0.0, ["cqT"])
        self.memset("pool", kT[64:128, :], 0.0, ["ckT"])
        va = self.T("cva", [128, M, 128], BF16)
        mcv = self.T("cmcv", [128, NV, 128], F32)
        mcm = self.T("cmcm", [128, NV, 128], F32)
        TT = self.T("cTT", [128, 15, 64], F32)
        rp = self.T("crp", [120, 128], F32)
        rec = self.T("crec", [128, 512], F32)
        yst = [self.T(f"cyst{i}", [64, 512], BF16) for i in range(2)]
        self.local_init("c")
        self.memset("pool", va[:, :, 64:128], 1.0, ["cva"])
        self.dma("sp", mcv[:], self.cd["c_mcv"][:, :, :], [], ["cmcv"])
        self.memset("dve", rp[:], 0.0, ["crp"])
        self.dma("sp", rp[:, 48:79], self.na_rpb[l, :, :], [], ["crp"])
        self.dma("sp", self.rpbr[:, :], rp[:], ["crp"], ["rpbr"])
        Hd = self.T("cHd", [64, 15, 2, 64], F32)
        j64 = self.T("cj64", [64, 64], F32)
        self.dma("sp", j64[:], self.cd["c_j64"][:, :], [], ["cj64"])
        for h in range(8):
            self.dma("sp", qT[0:64, :], self.qkT[OFF_CQ + h * 64:OFF_CQ + (h + 1) * 64, :], [], ["cqT"])
            self.dma("sp", kT[0:64, :], self.qkT[OFF_CK + h * 64:OFF_CK + (h + 1) * 64, :], [], ["ckT"])
            self.dma("sp", va[:, :, 0:64], self.vtm[:, VC_C + h * 64:VC_C + (h + 1) * 64].rearrange("(k p) c -> p k c", p=128), [], ["cva"])
            for a in range(2):
                src = bass.AP(tensor=self.rpbr.tensor, offset=h * 15 * 128, ap=[[1, 64], [128, 15], [1, 64]])
                self.dma("sp", Hd[:, :, a, :], src, ["rpbr"], ["cHd"])
            for i in range(15):
                bank = self.pb[4 + i // 8]
                bk = f"pb{4 + i // 8}"
                self.mm(bank[:, (i % 8) * 64:(i % 8 + 1) * 64], Hd[:, i, :, :].rearrange("p a k -> p (a k)"), j64[:], True, True,
                        ["cHd", "cj64"], [bk])
            self.cp("dve", TT[:, 0:8, :], self.pb[4][:].rearrange("p (i c) -> p i c", i=8), ["pb4"], ["cTT"])
            self.cp("dve", TT[:, 8:15, :], self.pb[5][:, 0:448].rearrange("p (i c) -> p i c", i=7), ["pb5"], ["cTT"])
            for vi, (cls, dr) in enumerate(meta):
                for a in range(2):
                    for b in range(2):
                        ri = 2 * dr + a - b + 7
                        o = mcm[a * 64:(a + 1) * 64, vi, b * 64:(b + 1) * 64]
                        i0 = mcv[a * 64:(a + 1) * 64, vi, b * 64:(b + 1) * 64]
                        if 0 <= ri <= 14:
                            self.tt("pool", o, i0, TT[a * 64:(a + 1) * 64, ri, :], ALU.add, ["cmcv", "cTT"], ["lmask"])
                        else:
                            self.cp("pool", o, i0, ["cmcv"], ["lmask"])
            midx = {(cls, dr): vi for vi, (cls, dr) in enumerate(meta)}
            groups = []

            def mk(qbs, cls):
                rels = []
                for dr in range(-4, 5):
                    if (cls, dr) not in midx:
                        continue
                    kbs = [qb + dr if 0 <= qb + dr < M else None for qb in qbs]
                    rels.append((dr, kbs, mcm[:, midx[(cls, dr)], :]))
                return (qbs, rels)
            groups.append(mk([0], "m0"))
            groups.append(mk([1], "m1"))
            ints = list(range(2, M - 2))
            for i in range(0, len(ints), 4):
                groups.append(mk(ints[i:i + 4], "int"))
            groups.append(mk([M - 2], "mL2"))
            groups.append(mk([M - 1], "mL1"))

            def outf(gi, qbs, acc, ak, h=h):
                n = len(qbs) * 128
                ys = yst[gi % 2]
                yk = f"cyst{gi % 2}"
                self.recip(rec[64:128, 0:n], acc[64:128, 0:n], [ak], ["crec"])
                self.tt("dve", ys[:, 0:n], acc[0:64, 0:n], rec[64:128, 0:n], ALU.mult, [ak, "crec"], [yk])
                self.dma("pool", self.yT[2, h * 64:(h + 1) * 64, qbs[0] * 128:qbs[0] * 128 + n], ys[:, 0:n], [yk], [])
            import os
            if os.environ.get("CDBG", "0") != "1":
                self.local_groups(qT, kT, va, ["cqT", "ckT", "cva"], groups, None, outf, "c")

    def mixer_b(self, l):
        self.phase_reset()
        S = self.Sq
        NB = S // 128
        qn_ = self.T("bqn", [128, S], BF16)
        kn_ = self.T("bkn", [128, S], BF16)
        qp = self.T("bqp", [128, S], BF16)
        kp = self.T("bkp", [128, S], BF16)
        for t_, k_ in ((qn_, "bqn"), (kn_, "bkn"), (qp, "bqp"), (kp, "bkp")):
            self.memset("dve", t_[64:128, :], 0.0, [k_])
        va = self.T("bva", [128, NB, 128], BF16)
        accB = self.T("baccB", [128, S], F32)
        mk_ = self.T("bmk", [128, 3, 128], F32)
        rec = self.T("brec", [128, 512], F32)
        yst = [self.T(f"byst{i}", [64, 512], BF16) for i in range(2)]
        self.local_init("b")
        self.memset("pool", va[:, :, 64:128], 1.0, ["bva"])
        for h in range(8):
            for g, (win, dil) in enumerate(B_PATTERNS):
                L = S // dil
                nbc = L // 128
                base = OFF_B + g * 1536
                self.dma("sp", qn_[0:64, :], self.qkT[base + h * 64:base + (h + 1) * 64, :], [], ["bqn"])
                self.dma("sp", kn_[0:64, :], self.qkT[base + 512 + h * 64:base + 512 + (h + 1) * 64, :], [], ["bkn"])
                self.dma("sp", mk_[:], self.cd["c_mb"][g * 8 + h, :, :].rearrange("p (r q) -> p r q", r=3), [], ["lmask"])
                vcol = VC_B + g * 512 + h * 64
                if dil == 1:
                    qq, kk = qn_, kn_
                    keys = ["bqn", "bkn", "bva"]
                    self.dma("sp", va[:, :, 0:64], self.vtm[:, vcol:vcol + 64].rearrange("(k p) c -> p k c", p=128), [], ["bva"])
                else:
                    qq, kk = qp, kp
                    keys = ["bqp", "bkp", "bva"]
                    self.cp("pool", qp[0:64, :].rearrange("p (j i) -> p j i", j=dil), qn_[0:64, :].rearrange("p (i j) -> p j i", j=dil), ["bqn"], ["bqp"])
                    self.cp("dve", kp[0:64, :].rearrange("p (j i) -> p j i", j=dil), kn_[0:64, :].rearrange("p (i j) -> p j i", j=dil), ["bkn"], ["bkp"])
                    for j in range(dil):
                        src = self.vtm[:, vcol:vcol + 64].rearrange("(k p j) c -> j p k c", p=128, j=dil)[j]
                        self.dma("sp", va[:, j * nbc:(j + 1) * nbc, 0:64], src, [], ["bva"])
                groups = []
                gs = min(4, nbc)
                for j in range(dil):
                    for b0 in range(0, nbc, gs):
                        qbs = [j * nbc + b0 + i for i in range(gs)]
                        rels = []
                        for r in (-1, 0, 1):
                            kbs = [(qb + r) if 0 <= (qb - j * nbc + r) < nbc else None for qb in qbs]
                            rels.append((r, kbs, mk_[:, r + 1, :]))
                        groups.append((qbs, rels))

                def outf(gi, qbs, acc, ak, g=g, dil=dil, nbc=nbc):
                    n = len(qbs) * 128
                    j = qbs[0] // nbc
                    i0 = (qbs[0] - j * nbc) * 128
                    if dil == 1:
                        dst = accB[:, i0:i0 + n]
                        self.cp("dve", dst, acc[:, 0:n], [ak], ["baccB"])
                    else:
                        dst = accB[:, i0 * dil + j:(i0 + n - 1) * dil + j + 1:dil]
                        self.tt("dve", dst, dst, acc[:, 0:n], ALU.add, [ak, "baccB"], ["baccB"])
                self.local_groups(qq, kk, va, keys, groups, None, outf, "b")
            for ct in range(S // 512):
                ys = yst[ct % 2]
                yk = f"byst{ct % 2}"
                self.recip(rec[64:128, :], accB[64:128, ct * 512:(ct + 1) * 512], ["baccB"], ["brec"])
                self.cp_psum_num(accB, ct)
                self.tt("dve", ys[:], self.pb[6][0:64, :], rec[64:128, :], ALU.mult, ["pb6", "brec"], [yk])
                self.dma("pool", self.yT[1, h * 64:(h + 1) * 64, ct * 512:(ct + 1) * 512], ys[:], [yk], [])

    def cp_psum_num(self, accB, ct):
        self.mm(self.pb[6][0:64, :], self.identf[0:64, 0:64], accB[0:64, ct * 512:(ct + 1) * 512], True, True, ["baccB", "const"], ["pb6"])


_CACHE = {}


def _get_builder(S):
    if S not in _CACHE:
        b = Builder(S)
        b.build()
        _CACHE[S] = b
    return _CACHE[S]


def make_in_map(b, x1, inputs):
    m = {"x": np.ascontiguousarray(x1, dtype=np.float32)}
    m["norm_mix"] = inputs["norm_mix"]
    m["w_in"] = inputs["w_in"]
    m["b_gate"] = inputs["b_gate"]
    m["diff_lambda"] = inputs["diff_lambda"].reshape(DEPTH, 256)
    m["diff_subln"] = inputs["diff_subln"]
    m["na_rpb"] = inputs["na_rpb"].reshape(DEPTH, 120, 31)
    m["qk_norm"] = inputs["qk_norm"]
    m["w_branch"] = inputs["w_branch"].reshape(DEPTH, 2048, D_MODEL)
    m["w_out"] = inputs["w_out"]
    m["norm_ffn"] = inputs["norm_ffn"]
    m["w_ff1"] = inputs["w_ff1"]
    m["w_ff2"] = inputs["w_ff2"]
    m["norm_final"] = inputs["norm_final"].reshape(1, D_MODEL)
    for k, v in b.consts.items():
        m[k] = v
    return {k: np.ascontiguousarray(v) for k, v in m.items()}


def kernel(**inputs):
    inputs = {k: np.asarray(v) for k, v in inputs.items()}
    x = inputs["x"]
    B, S, _ = x.shape
    b = _get_builder(S)
    in_maps = [make_in_map(b, x[i], inputs) for i in range(B)]
    res = run_bass_kernel_spmd(b.nc, in_maps, core_ids=list(range(B)))
    return np.stack([np.asarray(r["out"]) for r in res.results], axis=0).astype(np.float32)
```
